# Optimizing a Trainium2 kernel written in Bass

```python
import math
import jax, jax.numpy as jnp
from jax import lax
import numpy as np

D_MODEL = 1024
BATCH = 4
SEQ = 8192
DEPTH = 4

N_MIXERS = 3
MEM_LEN = 256
EPS = 1e-6
DA_HEADS = 8
DA_QK_DIM = 64
DA_V_DIM = 2 * DA_QK_DIM
Q_BLOCK = 128
REL_BUCKETS = 32
REL_MAX_DIST = 128
HG_EXPAND = 128
HG_HEADS = D_MODEL // HG_EXPAND
HG_V_DIM = D_MODEL // HG_HEADS
HG_CHUNK = 64
SG_CHUNK = 128
SG_GROUPS = 8
SG_WIDTH = D_MODEL
SG_GROUP_DIM = SG_WIDTH // SG_GROUPS
CA_HEADS = 4
CA_HEAD_DIM = D_MODEL // CA_HEADS
D_FF = int(math.ceil(8 * D_MODEL / 3 / 256)) * 256
N_A = (DEPTH + 2) // 3
N_B = (DEPTH + 1) // 3
N_C = DEPTH // 3

kernel_name = "hybrid_diffattn_hgrn2_gmlp_trunk"


def rmsnorm(x, g):
    xf = x.astype(jnp.float32)
    y = xf * lax.rsqrt(jnp.mean(xf * xf, axis=-1, keepdims=True) + EPS)
    return (y * g.astype(jnp.float32)).astype(x.dtype)


def rel_bucket(dist):
    n = jnp.maximum(dist, 0)
    exact = REL_BUCKETS // 2
    nf = jnp.maximum(n, exact).astype(jnp.float32)
    large = exact + (jnp.log(nf / exact) / math.log(REL_MAX_DIST / exact)
                     * (REL_BUCKETS - exact)).astype(jnp.int32)
    large = jnp.minimum(large, REL_BUCKETS - 1)
    return jnp.where(n < exact, n, large)


def diff_attention(h, w_in, w_out, lq1, lk1, lq2, lk2, subln_g, rel_bias, layer_idx):
    B, S, _ = h.shape
    q, k, v = jnp.split(h @ w_in, 3, axis=-1)
    q = q.reshape(B, S, DA_HEADS, 2, DA_QK_DIM)
    k = k.reshape(B, S, DA_HEADS, 2, DA_QK_DIM)
    v = v.reshape(B, S, DA_HEADS, DA_V_DIM)
    lam_init = 0.8 - 0.6 * math.exp(-0.3 * layer_idx)
    lam = (jnp.exp(jnp.sum(lq1.astype(jnp.float32) * lk1.astype(jnp.float32)))
           - jnp.exp(jnp.sum(lq2.astype(jnp.float32) * lk2.astype(jnp.float32))) + lam_init)
    n_blk = S // Q_BLOCK
    qb = q.reshape(B, n_blk, Q_BLOCK, DA_HEADS, 2, DA_QK_DIM).transpose(1, 0, 2, 3, 4, 5)
    starts = jnp.arange(n_blk, dtype=jnp.int32) * Q_BLOCK
    k_pos = jnp.arange(S, dtype=jnp.int32)
    scale = DA_QK_DIM ** -0.5

    def block(args):
        q_blk, start = args
        dist = (start + jnp.arange(Q_BLOCK, dtype=jnp.int32))[:, None] - k_pos[None, :]
        bias = rel_bias.astype(jnp.float32)[:, rel_bucket(dist)]
        logits = jnp.einsum('bqhcd,bkhcd->bhcqk', q_blk, k).astype(jnp.float32) * scale
        logits = logits + bias[None, :, None]
        logits = jnp.where(dist >= 0, logits, -jnp.inf)
        p = jax.nn.softmax(logits, axis=-1)
        attn = p[:, :, 0] - lam * p[:, :, 1]
        return jnp.einsum('bhqk,bkhd->bqhd', attn.astype(v.dtype), v)

    o = lax.map(block, (qb, starts))
    o = o.transpose(1, 0, 2, 3, 4).reshape(B, S, DA_HEADS, DA_V_DIM)
    o = rmsnorm(o, subln_g) * (1.0 - lam_init)
    return o.reshape(B, S, D_MODEL).astype(h.dtype) @ w_out


def hgrn2(h, w_in, w_out, lower_bound, onorm_g):
    B, S, _ = h.shape
    q, f, i, g = jnp.split(h @ w_in, 4, axis=-1)
    lb = lower_bound.astype(jnp.float32)
    ff = f.astype(jnp.float32)
    log_f = jnp.logaddexp(jnp.log(lb), jnp.log1p(-lb) + jax.nn.log_sigmoid(ff))
    key = 1.0 - jnp.exp(log_f)
    qf = jax.nn.silu(q.astype(jnp.float32))
    vf = i.astype(jnp.float32)
    n_ch = S // HG_CHUNK

    def to_chunks(t, d):
        return t.reshape(B, n_ch, HG_CHUNK, HG_HEADS, d).transpose(1, 0, 3, 2, 4)

    qc, kc, gc = (to_chunks(t, HG_EXPAND) for t in (qf, key, log_f))
    vc = to_chunks(vf, HG_V_DIM)
    causal = jnp.tril(jnp.ones((HG_CHUNK, HG_CHUNK), dtype=bool))

    def step(state, xs):
        qt, kt, vt, gt = xs
        G = jnp.cumsum(gt, axis=2)
        o_inter = jnp.einsum('bhtk,bhkv->bhtv', qt * jnp.exp(G), state)
        diff = G[:, :, :, None, :] - G[:, :, None, :, :]
        decay = jnp.exp(jnp.where(causal[:, :, None], diff, -jnp.inf))
        A = jnp.einsum('bhtk,bhsk,bhtsk->bhts', qt, kt, decay)
        o_intra = jnp.einsum('bhts,bhsv->bhtv', A, vt)
        G_last = G[:, :, -1]
        k_dec = kt * jnp.exp(G_last[:, :, None] - G)
        new_state = jnp.exp(G_last)[..., None] * state + jnp.einsum('bhsk,bhsv->bhkv', k_dec, vt)
        return new_state, o_inter + o_intra

    s0 = jnp.zeros((B, HG_HEADS, HG_EXPAND, HG_V_DIM), jnp.float32)
    _, o = lax.scan(step, s0, (qc, kc, vc, gc))
    o = o.transpose(1, 0, 3, 2, 4).reshape(B, S, HG_HEADS, HG_V_DIM).astype(h.dtype)
    o = rmsnorm(o, onorm_g).reshape(B, S, D_MODEL) * jax.nn.silu(g)
    return o @ w_out


def chunked_sgu(h, w_in, w_out, vnorm_g, w_s, b_s):
    B, S, _ = h.shape
    u, v = jnp.split(jax.nn.gelu(h @ w_in, approximate=False), 2, axis=-1)
    v = rmsnorm(v, vnorm_g)
    n = S // SG_CHUNK
    v = v.reshape(B, n, SG_CHUNK, SG_GROUPS, SG_GROUP_DIM)
    w = w_s * jnp.tril(jnp.ones((SG_CHUNK, SG_CHUNK), w_s.dtype))
    mixed = jnp.einsum('gts,bnsgc->bntgc', w, v) + b_s.T[None, None, :, :, None]
    return (u * mixed.reshape(B, S, SG_WIDTH)) @ w_out


def mem_cross_attention(h, mem_n, w_q, w_kv, w_o):
    B, S, _ = h.shape
    q = (h @ w_q).reshape(B, S, CA_HEADS, CA_HEAD_DIM)
    k, v = jnp.split(mem_n @ w_kv, 2, axis=-1)
    k = k.reshape(B, -1, CA_HEADS, CA_HEAD_DIM)
    v = v.reshape(B, -1, CA_HEADS, CA_HEAD_DIM)
    logits = jnp.einsum('bshd,bmhd->bhsm', q, k).astype(jnp.float32) * (CA_HEAD_DIM ** -0.5)
    p = jax.nn.softmax(logits, axis=-1)
    o = jnp.einsum('bhsm,bmhd->bshd', p.astype(v.dtype), v).reshape(B, S, D_MODEL)
    return o @ w_o


def swiglu(h, w_gu, w_down):
    gate, up = jnp.split(h @ w_gu, 2, axis=-1)
    return (jax.nn.silu(gate) * up) @ w_down


def setup_inputs(seed: int = 0) -> dict:
    key = jax.random.key(seed)
    k = jax.random.split(key, 32)
    f32 = jnp.float32

    def nrm(kk, shape, scale):
        return jax.random.normal(kk, shape, f32) * scale

    def gain(kk, shape):
        return 1.0 + nrm(kk, shape, 0.02)

    D = D_MODEL
    return {
        "x": nrm(k[0], (BATCH, SEQ, D), 1.0),
        "mem": nrm(k[1], (BATCH, MEM_LEN, D), 1.0),
        "rel_bias": nrm(k[2], (DA_HEADS, REL_BUCKETS), 0.5),
        "norm_mix": gain(k[3], (DEPTH, D)),
        "norm_cross": gain(k[4], (DEPTH, D)),
        "norm_ffn": gain(k[5], (DEPTH, D)),
        "norm_mem": gain(k[6], (D,)),
        "norm_final": gain(k[7], (D,)),
        "da_w_in": nrm(k[8], (N_A, D, 3 * D), D ** -0.5),
        "da_w_out": nrm(k[9], (N_A, D, D), D ** -0.5),
        "da_lq1": nrm(k[10], (N_A, DA_QK_DIM), 0.1),
        "da_lk1": nrm(k[11], (N_A, DA_QK_DIM), 0.1),
        "da_lq2": nrm(k[12], (N_A, DA_QK_DIM), 0.1),
        "da_lk2": nrm(k[13], (N_A, DA_QK_DIM), 0.1),
        "da_subln": gain(k[14], (N_A, DA_V_DIM)),
        "hg_w_in": nrm(k[15], (N_B, D, 4 * D), D ** -0.5),
        "hg_w_out": nrm(k[16], (N_B, D, D), D ** -0.5),
        "hg_lower_bounds": nrm(k[17], (DEPTH, HG_HEADS * HG_EXPAND), 0.5),
        "hg_onorm": gain(k[18], (N_B, HG_V_DIM)),
        "sg_w_in": nrm(k[19], (N_C, D, 2 * SG_WIDTH), D ** -0.5),
        "sg_w_out": nrm(k[20], (N_C, SG_WIDTH, D), SG_WIDTH ** -0.5),
        "sg_vnorm": gain(k[21], (N_C, SG_WIDTH)),
        "sg_w_s": nrm(k[22], (N_C, SG_GROUPS, SG_CHUNK, SG_CHUNK), SG_CHUNK ** -0.5),
        "sg_b_s": 1.0 + nrm(k[23], (N_C, SG_GROUPS, SG_CHUNK), 0.1),
        "ca_w_q": nrm(k[24], (DEPTH, D, D), D ** -0.5),
        "ca_w_kv": nrm(k[25], (DEPTH, D, 2 * D), D ** -0.5),
        "ca_w_o": nrm(k[26], (DEPTH, D, D), D ** -0.5),
        "ffn_w_gu": nrm(k[27], (DEPTH, D, 2 * D_FF), D ** -0.5),
        "ffn_w_down": nrm(k[28], (DEPTH, D_FF, D), D_FF ** -0.5),
    }


def reference(x, mem, rel_bias, norm_mix, norm_cross, norm_ffn, norm_mem, norm_final,
              da_w_in, da_w_out, da_lq1, da_lk1, da_lq2, da_lk2, da_subln,
              hg_w_in, hg_w_out, hg_lower_bounds, hg_onorm,
              sg_w_in, sg_w_out, sg_vnorm, sg_w_s, sg_b_s,
              ca_w_q, ca_w_kv, ca_w_o, ffn_w_gu, ffn_w_down):
    mem_n = rmsnorm(mem, norm_mem)
    lb = jax.nn.softmax(hg_lower_bounds.astype(jnp.float32), axis=0)
    lb = jnp.cumsum(lb, axis=0) - lb[0]
    for i in range(DEPTH):
        kind = i % N_MIXERS
        j = i // N_MIXERS
        hn = rmsnorm(x, norm_mix[i])
        if kind == 0:
            mix = diff_attention(hn, da_w_in[j], da_w_out[j], da_lq1[j], da_lk1[j],
                                 da_lq2[j], da_lk2[j], da_subln[j], rel_bias, i)
        elif kind == 1:
            mix = hgrn2(hn, hg_w_in[j], hg_w_out[j], lb[i], hg_onorm[j])
        else:
            mix = chunked_sgu(hn, sg_w_in[j], sg_w_out[j], sg_vnorm[j], sg_w_s[j], sg_b_s[j])
        x = x + mix
        x = x + mem_cross_attention(rmsnorm(x, norm_cross[i]), mem_n, ca_w_q[i], ca_w_kv[i], ca_w_o[i])
        x = x + swiglu(rmsnorm(x, norm_ffn[i]), ffn_w_gu[i], ffn_w_down[i])
    return rmsnorm(x, norm_final)
```

```python
import math
import numpy as np
import ml_dtypes
from contextlib import ExitStack
import concourse.bass as bass
import concourse.mybir as mybir
from concourse.bass_utils import run_bass_kernel_spmd

F32 = mybir.dt.float32
BF16 = mybir.dt.bfloat16
AF = mybir.ActivationFunctionType
ALU = mybir.AluOpType
AX = mybir.AxisListType
NPBF = ml_dtypes.bfloat16

D = 1024
KC = 8
SEQ = 8192
NT = 4096
NH = 4
TT = 512
EPS = 1e-6
DFF = 2816
FFH = DFF // 2
MEM = 256


class Buf:
    __slots__ = ("name", "w", "r", "dsem", "dcnt")

    def __init__(self, name=""):
        self.name = name
        self.w = None
        self.r = {}
        self.dsem = None


class Sched:
    ENGS = ("pe", "act", "dve", "pool", "sp")

    def __init__(self, nc, stack):
        self.nc = nc
        self.stack = stack
        self.esem, self.cnt, self.waited = {}, {}, {}
        self.semown = {}
        for e in self.ENGS:
            self.esem[e] = stack.enter_context(nc.semaphore("es_" + e))
            self.cnt[e] = 0
            self.waited[e] = {}
            self.semown[id(self.esem[e])] = e
        self.em = {"pe": nc.tensor, "act": nc.scalar, "dve": nc.vector, "pool": nc.gpsimd, "sp": nc.sync}
        self.ninst = 0
        self.nsem = 0
        self.free_dsems = []
        self.local_bufs = []

    def buf(self, name=""):
        b = Buf(name)
        self.local_bufs.append(b)
        return b

    def _deps(self, eng, reads, writes):
        deps = {}

        def add(tok, raw):
            sem, val = tok
            own = self.semown.get(id(sem))
            if own == eng and (eng == "pe" or not raw):
                return
            k = id(sem)
            if k not in deps or deps[k][1] < val:
                deps[k] = (sem, val)

        for b in reads:
            if b.w is not None:
                add(b.w, True)
        for b in writes:
            if b.w is not None:
                add(b.w, False)
            for k, tok in b.r.items():
                add(tok, False)
        waits = []
        wd = self.waited[eng]
        for k, (sem, val) in deps.items():
            if wd.get(k, 0) < val:
                wd[k] = val
                waits.append((sem, val))
        return waits

    def _mark(self, tok, reads, writes):
        sem, val = tok
        k = id(sem)
        for b in reads:
            if k not in b.r or b.r[k][1] < val:
                b.r[k] = (sem, val)
        for b in writes:
            b.w = tok
            b.r = {}

    def _emit(self, eng, waits, fn, sem, inc):
        e = self.em[eng]
        for (s, v) in waits:
            e.wait_ge(s, v)
        if fn is None:
            return
        ins = fn(e)
        if sem is not None:
            ins.then_inc(sem, inc)
        self.ninst += 1

    def op(self, eng, fn, reads=(), writes=(), inc=True):
        waits = self._deps(eng, reads, writes)
        if inc:
            self.cnt[eng] += 1
            tok = (self.esem[eng], self.cnt[eng])
            self._emit(eng, waits, fn, self.esem[eng], 1)
        else:
            tok = (self.esem[eng], self.cnt[eng] + 1)
            self._emit(eng, waits, fn, None, 0)
        self._mark(tok, reads, writes)

    def dma(self, q, fn, sb, reads=(), writes=()):
        if sb.dsem is None:
            if self.free_dsems:
                sb.dsem = self.free_dsems.pop()
            else:
                sem = self.stack.enter_context(self.nc.semaphore("ds_%d" % self.nsem))
                self.nsem += 1
                sb.dsem = [sem, 0]
        waits = self._deps(q, reads, writes)
        sb.dsem[1] += 16
        tok = (sb.dsem[0], sb.dsem[1])
        self._emit(q, waits, fn, sb.dsem[0], 16)
        self._mark(tok, reads, writes)

    def wait_all(self, eng, bufs):
        waits = self._deps(eng, bufs, bufs)
        self._emit(eng, waits, None, None, 0)

    def barrier(self, extra=()):
        bufs = list(self.local_bufs) + list(extra)
        for e in self.ENGS:
            waits = self._deps(e, bufs, bufs)
            wd = self.waited[e]
            for f in self.ENGS:
                if f == e:
                    continue
                k = id(self.esem[f])
                if wd.get(k, 0) < self.cnt[f]:
                    wd[k] = self.cnt[f]
                    waits.append((self.esem[f], self.cnt[f]))
            self._emit(e, waits, None, None, 0)
        for b in self.local_bufs:
            if b.dsem is not None:
                self.free_dsems.append(b.dsem)
        self.local_bufs = []


class Ctx:
    def __init__(self, nc, st, consts_ap):
        self.nc = nc
        self.S = Sched(nc, st)
        self.uid = 0
        S = self.S
        self.psum = []
        for i in range(8):
            t = st.enter_context(nc.psum_tensor("psb%d" % i, [128, 512], F32))
            self.psum.append((t, Buf("ps%d" % i)))
        self.ps_rr = 0
        self.Bc = Buf("const")
        self.onesD = self.sb(st, [128, 128], BF16)
        self.ones128 = self.sb(st, [128, 128], BF16)
        self.ones1 = self.sb(st, [128, 128], BF16)
        self.epsT = self.sb(st, [128, 1], F32)
        self.cf = self.sb(st, [128, 2 * 128 + 512], F32)
        self.maskle = self.sb(st, [128, 128], BF16)
        self.ident = self.sb(st, [128, 128], BF16)
        Bl = Buf("cload")
        S.dma("sp", lambda e: e.dma_start(out=self.cf[:], in_=consts_ap), Bl, writes=[Bl])
        S.op("dve", lambda e: e.memset(self.onesD[:], 1.0 / D), writes=[self.Bc])
        S.op("dve", lambda e: e.memset(self.ones128[:], 1.0 / 128), writes=[self.Bc])
        S.op("dve", lambda e: e.memset(self.ones1[:], 1.0), writes=[self.Bc])
        S.op("dve", lambda e: e.memset(self.epsT[:], EPS), writes=[self.Bc])
        self.oneT = self.sb(st, [128, 1], F32)
        S.op("dve", lambda e: e.memset(self.oneT[:], 1.0), writes=[self.Bc])
        self.onesrow = self.sb(st, [1, 128], BF16)
        S.op("dve", lambda e: e.memset(self.onesrow[:], 1.0), writes=[self.Bc])
        S.op("dve", lambda e: e.tensor_copy(out=self.maskle[:], in_=self.cf[:, 0:128]), reads=[Bl], writes=[self.Bc])
        S.op("dve", lambda e: e.tensor_copy(out=self.ident[:], in_=self.cf[:, 128:256]), reads=[Bl], writes=[self.Bc])
        self.scanmask = self.cf[:, 256:768]
        self.Bcf = Bl

    def sb(self, st, shape, dt):
        self.uid += 1
        return st.enter_context(self.nc.sbuf_tensor("t%d" % self.uid, list(shape), dt))

    def ps(self):
        t, b = self.psum[self.ps_rr % 8]
        self.ps_rr += 1
        return t, b

    def ps_i(self, i):
        return self.psum[i]


def make_consts():
    c = np.zeros((128, 2 * 128 + 512), np.float32)
    c[:, 0:128] = np.triu(np.ones((128, 128), np.float32))
    c[:, 128:256] = np.eye(128, dtype=np.float32)
    m = np.ones(512, np.float32)
    m[::64] = 0.0
    c[:, 256:768] = m[None, :]
    return c


def load_w(K, dst, Bdst, w_ap, q="pool"):
    kc = w_ap.shape[0] // 128
    src = w_ap.rearrange("(c p) n -> p c n", p=128)
    for c in range(kc):
        K.S.dma(q, lambda e, c=c: e.dma_start(out=dst[:, c, :], in_=src[:, c, :]), Bdst, writes=[Bdst])


def load_vec(K, dst, Bdst, v_ap, q="sp"):
    K.S.dma(q, lambda e: e.dma_start(out=dst[:], in_=v_ap), Bdst, writes=[Bdst])


def rmsnorm_fm(K, x, Bx, g, Bg, sq, Bsq, rstd, Brstd, xn, Bxn, n, kc=KC, ones=None, xn_eng="dve"):
    S = K.S
    ones = K.onesD if ones is None else ones
    S.op("act", lambda e: e.activation(out=sq[:, :, :n], in_=x[:, :, :n], func=AF.Square), reads=[Bx], writes=[Bsq])
    ps, Bps = K.ps()
    for c in range(kc):
        S.op("pe", lambda e, c=c: e.matmul(ps[:, :n], ones[:], sq[:, c, :n], start=(c == 0), stop=(c == kc - 1)),
             reads=[Bsq, K.Bc], writes=[Bps], inc=(c == kc - 1))
    S.op("act", lambda e: e.activation(out=rstd[:, :n], in_=ps[:, :n], func=AF.Sqrt, bias=K.epsT[:], scale=1.0),
         reads=[Bps, K.Bc], writes=[Brstd])
    S.op("dve", lambda e: e.reciprocal(out=rstd[:, :n], in_=rstd[:, :n]), reads=[Brstd], writes=[Brstd])
    for c in range(kc):
        S.op(xn_eng, lambda e, c=c: e.scalar_tensor_tensor(out=xn[:, c, :n], in0=x[:, c, :n], scalar=g[:, c:c + 1],
                                                           in1=rstd[:, :n], op0=ALU.mult, op1=ALU.mult),
             reads=[Bx, Bg, Brstd], writes=[Bxn])


def linear_fm(K, w, Bw, xin, Bxin, n, jlist, evac, kc=KC, wcol0=0):
    S = K.S
    for j in jlist:
        ps, Bps = K.ps()
        for k in range(kc):
            S.op("pe", lambda e, j=j, k=k, ps=ps: e.matmul(ps[:, :n], w[:, k, wcol0 + j * 128: wcol0 + (j + 1) * 128],
                                                           xin[:, k, :n], start=(k == 0), stop=(k == kc - 1)),
                 reads=[Bw, Bxin], writes=[Bps], inc=(k == kc - 1))
        evac(j, ps, Bps)


def xt_tile(ap3, t0, n):
    return ap3[:, :, t0:t0 + n].rearrange("c p t -> p c t")


def stage_norm(K, XT, gvec, OUT, nt, out_f32=False):
    S = K.S
    with ExitStack() as st:
        g = K.sb(st, [128, KC], F32); Bg = S.buf()
        load_vec(K, g, Bg, gvec)
        xs = [(K.sb(st, [128, KC, TT], F32), S.buf()) for _ in range(2)]
        sqs = [(K.sb(st, [128, KC, TT], BF16), S.buf()) for _ in range(2)]
        rs = [(K.sb(st, [128, TT], F32), S.buf()) for _ in range(2)]
        odt = F32 if out_f32 else BF16
        os_ = [(K.sb(st, [128, KC, TT], odt), S.buf()) for _ in range(2)]
        ntile = nt // TT
        for i in range(ntile):
            x, Bx = xs[i % 2]; sq, Bsq = sqs[i % 2]; r, Br = rs[i % 2]; o, Bo = os_[i % 2]
            S.dma("sp", lambda e, x=x, i=i: e.dma_start(out=x[:], in_=xt_tile(XT, i * TT, TT)), Bx, writes=[Bx])
            rmsnorm_fm(K, x, Bx, g, Bg, sq, Bsq, r, Br, o, Bo, TT)
            S.dma("sp", lambda e, o=o, i=i: e.dma_start(out=xt_tile(OUT, i * TT, TT), in_=o[:]), Bo, reads=[Bo])
        S.barrier()


def stage_cross(K, XT, XO, memT, gmem, gx, wq, wkv, wo, nt):
    S = K.S
    HC = 4
    with ExitStack() as st:
        wq_s = K.sb(st, [128, KC, D], BF16); Bwq = S.buf()
        wkv_s = K.sb(st, [128, KC, 2 * D], BF16); Bwkv = S.buf()
        wo_s = K.sb(st, [128, KC, D], BF16); Bwo = S.buf()
        gm = K.sb(st, [128, KC], F32); Bgm = S.buf()
        g = K.sb(st, [128, KC], F32); Bg = S.buf()
        load_vec(K, gm, Bgm, gmem)
        load_vec(K, g, Bg, gx)
        load_w(K, wkv_s, Bwkv, wkv)
        load_w(K, wq_s, Bwq, wq)
        load_w(K, wo_s, Bwo, wo)
        mx = K.sb(st, [128, KC, MEM], F32); Bmx = S.buf()
        msq = K.sb(st, [128, KC, MEM], BF16); Bmsq = S.buf()
        mr = K.sb(st, [128, MEM], F32); Bmr = S.buf()
        mn = K.sb(st, [128, KC, MEM], BF16); Bmn = S.buf()
        kT = K.sb(st, [128, KC, MEM], BF16); BkT = S.buf()
        vtok = K.sb(st, [128, 2, D], BF16); Bv = S.buf()
        S.dma("sp", lambda e: e.dma_start(out=mx[:], in_=memT.rearrange("c p t -> p c t")), Bmx, writes=[Bmx])
        rmsnorm_fm(K, mx, Bmx, gm, Bgm, msq, Bmsq, mr, Bmr, mn, Bmn, MEM)

        def ev_k(j, ps, Bps):
            S.op("act", lambda e: e.activation(out=kT[:, j, :], in_=ps[:, :MEM], func=AF.Copy), reads=[Bps], writes=[BkT])
        linear_fm(K, wkv_s, Bwkv, mn, Bmn, MEM, range(KC), ev_k)
        for mc in range(2):
            for half in range(2):
                ps, Bps = K.ps()
                for k in range(KC):
                    S.op("pe", lambda e, k=k, mc=mc, half=half, ps=ps: e.matmul(
                        ps[:, :512], mn[:, k, mc * 128:(mc + 1) * 128], wkv_s[:, k, D + half * 512: D + (half + 1) * 512],
                        start=(k == 0), stop=(k == KC - 1)), reads=[Bmn, Bwkv], writes=[Bps], inc=(k == KC - 1))
                S.op("dve", lambda e, mc=mc, half=half, ps=ps: e.tensor_copy(out=vtok[:, mc, half * 512:(half + 1) * 512], in_=ps[:, :512]),
                     reads=[Bps], writes=[Bv])
        xs = [(K.sb(st, [128, KC, TT], F32), S.buf()) for _ in range(2)]
        sq = K.sb(st, [128, KC, TT], BF16); Bsq = S.buf()
        r = K.sb(st, [128, TT], F32); Br = S.buf()
        xn = K.sb(st, [128, KC, TT], BF16); Bxn = S.buf()
        qT = K.sb(st, [128, KC, TT], BF16); BqT = S.buf()
        pT = [(K.sb(st, [128, 2, TT], BF16), S.buf()) for _ in range(2)]
        rd = [(K.sb(st, [128, TT], F32), S.buf()) for _ in range(2)]
        oT = K.sb(st, [128, KC, TT], BF16); BoT = S.buf()
        ntile = nt // TT
        scale = 256 ** -0.5

        def load(i):
            x, Bx = xs[i % 2]
            S.dma("sp", lambda e: e.dma_start(out=x[:], in_=xt_tile(XT, i * TT, TT)), Bx, writes=[Bx])
        load(0)
        for i in range(ntile):
            x, Bx = xs[i % 2]
            if i + 1 < ntile:
                load(i + 1)
            rmsnorm_fm(K, x, Bx, g, Bg, sq, Bsq, r, Br, xn, Bxn, TT)

            def ev_q(j, ps, Bps):
                S.op("act", lambda e: e.activation(out=qT[:, j, :], in_=ps[:, :TT], func=AF.Copy), reads=[Bps], writes=[BqT])
            linear_fm(K, wq_s, Bwq, xn, Bxn, TT, range(KC), ev_q)
            for h in range(HC):
                p, Bp = pT[h % 2]
                rdt, Brd = rd[h % 2]
                for mc in range(2):
                    ps, Bps = K.ps()
                    for dc in range(2):
                        S.op("pe", lambda e, h=h, mc=mc, dc=dc, ps=ps: e.matmul(
                            ps[:, :TT], kT[:, 2 * h + dc, mc * 128:(mc + 1) * 128], qT[:, 2 * h + dc, :],
                            start=(dc == 0), stop=(dc == 1)), reads=[BkT, BqT], writes=[Bps], inc=(dc == 1))
                    S.op("act", lambda e, mc=mc, ps=ps, p=p: e.activation(out=p[:, mc, :], in_=ps[:, :TT], func=AF.Exp, scale=scale),
                         reads=[Bps], writes=[Bp])
                psd, Bpsd = K.ps()
                for mc in range(2):
                    S.op("pe", lambda e, mc=mc, psd=psd, p=p: e.matmul(psd[:, :TT], K.ones1[:], p[:, mc, :], start=(mc == 0), stop=(mc == 1)),
                         reads=[Bp, K.Bc], writes=[Bpsd], inc=(mc == 1))
                S.op("dve", lambda e, psd=psd, rdt=rdt: e.reciprocal(out=rdt[:], in_=psd[:, :TT]), reads=[Bpsd], writes=[Brd])
                for dv in range(2):
                    pso, Bpso = K.ps()
                    for mc in range(2):
                        S.op("pe", lambda e, h=h, mc=mc, dv=dv, pso=pso, p=p: e.matmul(
                            pso[:, :TT], vtok[:, mc, (2 * h + dv) * 128:(2 * h + dv + 1) * 128], p[:, mc, :],
                            start=(mc == 0), stop=(mc == 1)), reads=[Bv, Bp], writes=[Bpso], inc=(mc == 1))
                    S.op("dve", lambda e, h=h, dv=dv, pso=pso, rdt=rdt: e.tensor_tensor(out=oT[:, 2 * h + dv, :], in0=pso[:, :TT], in1=rdt[:], op=ALU.mult),
                         reads=[Bpso, Brd], writes=[BoT])

            def ev_o(j, ps, Bps):
                S.op("dve", lambda e: e.tensor_tensor(out=x[:, j, :], in0=ps[:, :TT], in1=x[:, j, :], op=ALU.add), reads=[Bps, Bx], writes=[Bx])
            linear_fm(K, wo_s, Bwo, oT, BoT, TT, range(KC), ev_o)
            S.dma("sp", lambda e, x=x, i=i: e.dma_start(out=xt_tile(XO, i * TT, TT), in_=x[:]), Bx, reads=[Bx])
        S.barrier()


def stage_ffn(K, XT, XO, XN, gx, wgu, wdn, half, nt):
    S = K.S
    NJ = FFH // 128
    with ExitStack() as st:
        wg_s = K.sb(st, [128, KC, FFH], BF16); Bwg = S.buf()
        wu_s = K.sb(st, [128, KC, FFH], BF16); Bwu = S.buf()
        wd_s = K.sb(st, [128, NJ, D], BF16); Bwd = S.buf()
        g = K.sb(st, [128, KC], F32); Bg = S.buf()
        load_vec(K, g, Bg, gx)
        load_w(K, wg_s, Bwg, wgu[:, half * FFH: (half + 1) * FFH])
        load_w(K, wu_s, Bwu, wgu[:, DFF + half * FFH: DFF + (half + 1) * FFH])
        load_w(K, wd_s, Bwd, wdn[half * FFH:(half + 1) * FFH, :])
        xs = [(K.sb(st, [128, KC, TT], F32), S.buf()) for _ in range(2)]
        xns = [(K.sb(st, [128, KC, TT], BF16), S.buf()) for _ in range(2)]
        sq = K.sb(st, [128, KC, TT], BF16); Bsq = S.buf()
        r = K.sb(st, [128, TT], F32); Br = S.buf()
        sg = [(K.sb(st, [128, TT], F32), S.buf()) for _ in range(2)]
        hT = K.sb(st, [128, NJ, TT], BF16); BhT = S.buf()
        ntile = nt // TT

        def load(i):
            x, Bx = xs[i % 2]
            S.dma("sp", lambda e: e.dma_start(out=x[:], in_=xt_tile(XT, i * TT, TT)), Bx, writes=[Bx])
            if half == 1:
                xn, Bxn = xns[i % 2]
                S.dma("sp", lambda e: e.dma_start(out=xn[:], in_=xt_tile(XN, i * TT, TT)), Bxn, writes=[Bxn])
        load(0)
        for i in range(ntile):
            x, Bx = xs[i % 2]
            xn, Bxn = xns[i % 2]
            if i + 1 < ntile:
                load(i + 1)
            if half == 0:
                rmsnorm_fm(K, x, Bx, g, Bg, sq, Bsq, r, Br, xn, Bxn, TT)
                S.dma("sp", lambda e, xn=xn, i=i: e.dma_start(out=xt_tile(XN, i * TT, TT), in_=xn[:]), Bxn, reads=[Bxn])
            for j in range(NJ):
                psg, Bpsg = K.ps()
                for k in range(KC):
                    S.op("pe", lambda e, j=j, k=k, psg=psg, xn=xn: e.matmul(psg[:, :TT], wg_s[:, k, j * 128:(j + 1) * 128], xn[:, k, :],
                                                                           start=(k == 0), stop=(k == KC - 1)),
                         reads=[Bwg, Bxn], writes=[Bpsg], inc=(k == KC - 1))
                psu, Bpsu = K.ps()
                for k in range(KC):
                    S.op("pe", lambda e, j=j, k=k, psu=psu, xn=xn: e.matmul(psu[:, :TT], wu_s[:, k, j * 128:(j + 1) * 128], xn[:, k, :],
                                                                           start=(k == 0), stop=(k == KC - 1)),
                         reads=[Bwu, Bxn], writes=[Bpsu], inc=(k == KC - 1))
                s_, Bs_ = sg[j % 2]
                S.op("act", lambda e, psg=psg, s_=s_: e.activation(out=s_[:], in_=psg[:, :TT], func=AF.Silu), reads=[Bpsg], writes=[Bs_])
                S.op("dve", lambda e, j=j, psu=psu, s_=s_: e.tensor_tensor(out=hT[:, j, :], in0=psu[:, :TT], in1=s_[:], op=ALU.mult),
                     reads=[Bpsu, Bs_], writes=[BhT])

            def ev_d(j, ps, Bps):
                S.op("dve", lambda e: e.tensor_tensor(out=x[:, j, :], in0=ps[:, :TT], in1=x[:, j, :], op=ALU.add), reads=[Bps, Bx], writes=[Bx])
            linear_fm(K, wd_s, Bwd, hT, BhT, TT, range(KC), ev_d, kc=NJ)
            S.dma("sp", lambda e, x=x, i=i: e.dma_start(out=xt_tile(XO, i * TT, TT), in_=x[:]), Bx, reads=[Bx])
        S.barrier()


def stage_da_in(K, XNF, wq, wk, wv, QT, KT, V, seq, nt_half, qscale=0.125):
    S = K.S
    HW = NH * 128
    with ExitStack() as st:
        wq_s = K.sb(st, [128, KC, HW], BF16); Bwq = S.buf()
        wk_s = K.sb(st, [128, KC, HW], BF16); Bwk = S.buf()
        wv_s = K.sb(st, [128, KC, HW], BF16); Bwv = S.buf()
        load_w(K, wq_s, Bwq, wq)
        load_w(K, wk_s, Bwk, wk)
        load_w(K, wv_s, Bwv, wv)
        xns = [(K.sb(st, [128, KC, TT], BF16), S.buf()) for _ in range(2)]
        qst = [(K.sb(st, [128, NH, TT], BF16), S.buf()) for _ in range(2)]
        kst = [(K.sb(st, [128, NH, TT], BF16), S.buf()) for _ in range(2)]
        vst = [(K.sb(st, [128, NH, TT // 128, 128], BF16), S.buf()) for _ in range(2)]
        ntile = seq // TT
        per = nt_half // TT

        def load(i):
            xn, Bxn = xns[i % 2]
            S.dma("sp", lambda e: e.dma_start(out=xn[:], in_=xt_tile(XNF[i // per], (i % per) * TT, TT)), Bxn, writes=[Bxn])
        load(0)
        for i in range(ntile):
            xn, Bxn = xns[i % 2]
            q_, Bq_ = qst[i % 2]; k_, Bk_ = kst[i % 2]; v_, Bv_ = vst[i % 2]
            if i + 1 < ntile:
                load(i + 1)

            def ev_q(j, ps, Bps):
                S.op("act", lambda e: e.mul(out=q_[:, j, :], in_=ps[:, :TT], mul=qscale), reads=[Bps], writes=[Bq_])
            linear_fm(K, wq_s, Bwq, xn, Bxn, TT, range(NH), ev_q)

            def ev_k(j, ps, Bps):
                S.op("dve", lambda e: e.tensor_copy(out=k_[:, j, :], in_=ps[:, :TT]), reads=[Bps], writes=[Bk_])
            linear_fm(K, wk_s, Bwk, xn, Bxn, TT, range(NH), ev_k)
            for sub in range(TT // 128):
                ps, Bps = K.ps()
                for k in range(KC):
                    S.op("pe", lambda e, k=k: e.matmul(ps[:, :HW], xn[:, k, sub * 128:(sub + 1) * 128], wv_s[:, k, :],
                                                       start=(k == 0), stop=(k == KC - 1)),
                         reads=[Bxn, Bwv], writes=[Bps], inc=(k == KC - 1))
                psv = ps[:, :HW].rearrange("p (h d) -> p h d", h=NH)
                if sub % 2 == 0:
                    S.op("act", lambda e: e.copy(out=v_[:, :, sub, :], in_=psv), reads=[Bps], writes=[Bv_])
                else:
                    S.op("dve", lambda e: e.tensor_copy(out=v_[:, :, sub, :], in_=psv), reads=[Bps], writes=[Bv_])
            s0 = i * TT
            S.dma("sp", lambda e: e.dma_start(out=QT[:, :, s0:s0 + TT].rearrange("h p t -> p h t"), in_=q_[:]), Bq_, reads=[Bq_])
            S.dma("sp", lambda e: e.dma_start(out=KT[:, :, s0:s0 + TT].rearrange("h p t -> p h t"), in_=k_[:]), Bk_, reads=[Bk_])
            j0 = s0 // 128
            S.dma("sp", lambda e: e.dma_start(out=V[:, :, j0:j0 + TT // 128, :].rearrange("h p j d -> p h j d"), in_=v_[:]), Bv_, reads=[Bv_])
        S.barrier()


def stage_da_core(K, QT, KT, V, GB, CM, misc, OT2, seq, lam_init, half=None):
    S = K.S
    NCH = seq // 128
    with ExitStack() as st:
        ms = K.sb(st, [128, 4 * 64 + NH + 1], F32); Bms = S.buf()
        cm = K.sb(st, [128, 1024], F32); Bcm = S.buf()
        load_vec(K, ms, Bms, misc)
        load_vec(K, cm, Bcm, CM)
        sc = K.sb(st, [128, 8], F32); Bsc = S.buf()
        tmp64 = K.sb(st, [128, 64], F32); Bt64 = S.buf()
        for c in range(2):
            S.op("dve", lambda e: e.tensor_tensor(out=tmp64[:], in0=ms[:, (2 * c) * 64:(2 * c + 1) * 64],
                                                  in1=ms[:, (2 * c + 1) * 64:(2 * c + 2) * 64], op=ALU.mult), reads=[Bms], writes=[Bt64])
            S.op("dve", lambda e: e.reduce_sum(out=sc[:, c:c + 1], in_=tmp64[:], axis=AX.X), reads=[Bt64], writes=[Bsc])
        S.op("act", lambda e: e.activation(out=sc[:, 0:2], in_=sc[:, 0:2], func=AF.Exp), reads=[Bsc], writes=[Bsc])
        S.op("dve", lambda e: e.scalar_tensor_tensor(out=sc[:, 2:3], in0=sc[:, 1:2], scalar=-lam_init, in1=sc[:, 0:1],
                                                     op0=ALU.add, op1=ALU.subtract), reads=[Bsc], writes=[Bsc])
        gcol = 4 * 64 + NH
        S.op("dve", lambda e: e.tensor_scalar(out=sc[:, 3:4], in0=ms[:, gcol:gcol + 1], scalar1=(1.0 - lam_init), scalar2=None, op0=ALU.mult),
             reads=[Bms, Bsc], writes=[Bsc])
        Mh = []
        gbt = K.sb(st, [128, 1024], F32); Bgbt = S.buf()
        for h in range(NH):
            m = K.sb(st, [128, 1024], F32); Bm = S.buf()
            S.dma("sp", lambda e: e.dma_start(out=gbt[:], in_=GB[h]), Bgbt, writes=[Bgbt])
            S.op("dve", lambda e: e.scalar_tensor_tensor(out=m[:], in0=gbt[:], scalar=ms[:, 4 * 64 + h: 4 * 64 + h + 1], in1=cm[:],
                                                         op0=ALU.subtract, op1=ALU.add), reads=[Bgbt, Bms, Bcm], writes=[Bm])
            Mh.append((m, Bm))
        hb = []
        for _ in range(2):
            hb.append(dict(k=K.sb(st, [128, seq], BF16), Bk=S.buf(), v=K.sb(st, [128, NCH, 128], BF16), Bv=S.buf(),
                           q=K.sb(st, [128, seq], BF16), Bq=S.buf()))
        pbuf = [(K.sb(st, [128, 2, TT], BF16), S.buf()) for _ in range(3)]
        sbuf = [(K.sb(st, [128, 2, TT], F32), S.buf()) for _ in range(2)]
        r1 = K.sb(st, [128, TT], F32); Br1 = S.buf()
        r2 = K.sb(st, [128, TT], F32); Br2 = S.buf()
        A = K.sb(st, [128, TT], F32); BA = S.buf()
        Bm_ = K.sb(st, [128, TT], F32); BB = S.buf()
        sq = K.sb(st, [128, TT], BF16); Bsq = S.buf()
        rs = K.sb(st, [128, TT], F32); Brs = S.buf()
        ost = [(K.sb(st, [128, TT], BF16), S.buf()) for _ in range(2)]
        (pO1, BO1), (pO2, BO2), (pD1, BD1), (pD2, BD2) = [K.ps_i(i) for i in (4, 5, 6, 7)]
        rr = [0]

        def psr():
            t = K.ps_i(rr[0] % 4)
            rr[0] += 1
            return t

        def loadh(h):
            b = hb[h % 2]
            for piece in range(4):
                c0 = piece * (seq // 4)
                S.dma("sp", lambda e: e.dma_start(out=b["k"][:, c0:c0 + seq // 4], in_=KT[h][:, c0:c0 + seq // 4]), b["Bk"], writes=[b["Bk"]])
            for piece in range(4):
                j0 = piece * (NCH // 4)
                S.dma("sp", lambda e: e.dma_start(out=b["v"][:, j0:j0 + NCH // 4, :], in_=V[h][:, j0:j0 + NCH // 4, :]),
                      b["Bv"], writes=[b["Bv"]])
            for piece in range(4):
                c0 = piece * (seq // 4)
                S.dma("sp", lambda e: e.dma_start(out=b["q"][:, c0:c0 + seq // 4], in_=QT[h][:, c0:c0 + seq // 4]), b["Bq"], writes=[b["Bq"]])
        loadh(0)
        ntile = seq // TT
        pi = 0
        for h in range(NH):
            b = hb[h % 2]
            if h + 1 < NH:
                loadh(h + 1)
            m, Bm = Mh[h]
            kt, vt, qt = b["k"], b["v"], b["q"]
            for t in range(ntile):
                q0 = t * TT
                nch = 4 * (t + 1)
                for j in range(nch):
                    diag = j >= 4 * t - 1
                    p, Bp = pbuf[pi % 3]
                    pi += 1
                    pss = []
                    for c in range(2):
                        ps, Bps = psr()
                        S.op("pe", lambda e: e.matmul(ps[:, :TT], kt[c * 64:(c + 1) * 64, j * 128:(j + 1) * 128], qt[c * 64:(c + 1) * 64, q0:q0 + TT],
                                                      start=True, stop=True), reads=[b["Bk"], b["Bq"]], writes=[Bps])
                        pss.append((ps, Bps))
                    if diag:
                        o = 128 * j - q0
                        c0 = 384 - o
                        s_, Bs_ = sbuf[j % 2]
                        for c in range(2):
                            ps, Bps = pss[c]
                            S.op("dve", lambda e: e.tensor_tensor(out=s_[:, c, :], in0=ps[:, :TT], in1=m[:, c0:c0 + TT], op=ALU.add),
                                 reads=[Bps, Bm], writes=[Bs_])
                        S.op("act", lambda e: e.activation(out=p[:], in_=s_[:], func=AF.Exp), reads=[Bs_], writes=[Bp])
                    else:
                        for c in range(2):
                            ps, Bps = pss[c]
                            S.op("act", lambda e: e.activation(out=p[:, c, :], in_=ps[:, :TT], func=AF.Exp), reads=[Bps], writes=[Bp])
                    first, last = (j == 0), (j == nch - 1)
                    S.op("pe", lambda e: e.matmul(pO1[:, :TT], vt[:, j, :], p[:, 0, :], start=first, stop=last), reads=[b["Bv"], Bp], writes=[BO1], inc=False)
                    S.op("pe", lambda e: e.matmul(pD1[:, :TT], K.ones1[:], p[:, 0, :], start=first, stop=last), reads=[K.Bc, Bp], writes=[BD1], inc=False)
                    S.op("pe", lambda e: e.matmul(pO2[:, :TT], vt[:, j, :], p[:, 1, :], start=first, stop=last), reads=[b["Bv"], Bp], writes=[BO2], inc=False)
                    S.op("pe", lambda e: e.matmul(pD2[:, :TT], K.ones1[:], p[:, 1, :], start=first, stop=last), reads=[K.Bc, Bp], writes=[BD2])
                S.op("dve", lambda e: e.reciprocal(out=r1[:], in_=pD1[:, :TT]), reads=[BD1], writes=[Br1])
                S.op("dve", lambda e: e.reciprocal(out=r2[:], in_=pD2[:, :TT]), reads=[BD2], writes=[Br2])
                S.op("dve", lambda e: e.tensor_tensor(out=A[:], in0=pO1[:, :TT], in1=r1[:], op=ALU.mult), reads=[BO1, Br1], writes=[BA])
                S.op("dve", lambda e: e.tensor_tensor(out=Bm_[:], in0=pO2[:, :TT], in1=r2[:], op=ALU.mult), reads=[BO2, Br2], writes=[BB])
                S.op("dve", lambda e: e.scalar_tensor_tensor(out=A[:], in0=Bm_[:], scalar=sc[:, 2:3], in1=A[:], op0=ALU.mult, op1=ALU.add),
                     reads=[BB, BA, Bsc], writes=[BA])
                S.op("act", lambda e: e.activation(out=sq[:], in_=A[:], func=AF.Square), reads=[BA], writes=[Bsq])
                ps, Bps = psr()
                S.op("pe", lambda e: e.matmul(ps[:, :TT], K.ones128[:], sq[:], start=True, stop=True), reads=[K.Bc, Bsq], writes=[Bps])
                S.op("act", lambda e: e.activation(out=rs[:], in_=ps[:, :TT], func=AF.Sqrt, bias=K.epsT[:], scale=1.0), reads=[Bps, K.Bc], writes=[Brs])
                S.op("dve", lambda e: e.reciprocal(out=rs[:], in_=rs[:]), reads=[Brs], writes=[Brs])
                o_, Bo_ = ost[t % 2]
                S.op("dve", lambda e: e.scalar_tensor_tensor(out=o_[:], in0=A[:], scalar=sc[:, 3:4], in1=rs[:], op0=ALU.mult, op1=ALU.mult),
                     reads=[BA, Bsc, Brs], writes=[Bo_])
                hf_ = (seq // 2) if half is None else half
                S.dma("sp", lambda e: e.dma_start(out=OT2[q0 // hf_, h, :, (q0 % hf_):(q0 % hf_) + TT], in_=o_[:]), Bo_, reads=[Bo_])
        S.barrier()


def stage_mix_out(K, XT, XO, OG, wout, nt):
    S = K.S
    with ExitStack() as st:
        w_s = K.sb(st, [128, KC, D], BF16); Bw = S.buf()
        load_w(K, w_s, Bw, wout)
        xs = [(K.sb(st, [128, KC, TT], F32), S.buf()) for _ in range(2)]
        os_ = [(K.sb(st, [128, KC, TT], BF16), S.buf()) for _ in range(2)]
        ntile = nt // TT

        def load(i):
            x, Bx = xs[i % 2]
            o, Bo = os_[i % 2]
            S.dma("sp", lambda e: e.dma_start(out=x[:], in_=xt_tile(XT, i * TT, TT)), Bx, writes=[Bx])
            for r in range(2):
                S.dma("sp", lambda e: e.dma_start(out=o[:, r * NH:(r + 1) * NH, :], in_=xt_tile(OG[r], i * TT, TT)), Bo, writes=[Bo])
        load(0)
        for i in range(ntile):
            x, Bx = xs[i % 2]
            o, Bo = os_[i % 2]
            if i + 1 < ntile:
                load(i + 1)

            def ev(j, ps, Bps):
                S.op("dve", lambda e: e.tensor_tensor(out=x[:, j, :], in0=ps[:, :TT], in1=x[:, j, :], op=ALU.add), reads=[Bps, Bx], writes=[Bx])
            linear_fm(K, w_s, Bw, o, Bo, TT, range(KC), ev)
            S.dma("sp", lambda e: e.dma_start(out=xt_tile(XO, i * TT, TT), in_=x[:]), Bx, reads=[Bx])
        S.barrier()


def stage_hg_in(K, XNF, wq, wf, wi, wg, oml, SQ, KKd, V, SG, seq, nt_half, layer_idx=1):
    S = K.S
    HW = NH * 128
    with ExitStack() as st:
        ws = []
        for w in (wq, wf, wi, wg):
            t = K.sb(st, [128, KC, HW], BF16); B = S.buf()
            load_w(K, t, B, w)
            ws.append((t, B))
        (wq_s, Bwq), (wf_s, Bwf), (wi_s, Bwi), (wg_s, Bwg) = ws
        nl = oml.shape[2]
        lraw = K.sb(st, [128, NH, nl], F32); Blraw = S.buf()
        load_vec(K, lraw, Blraw, oml)
        S.op("act", lambda e: e.activation(out=lraw[:], in_=lraw[:], func=AF.Exp), reads=[Blraw], writes=[Blraw])
        tot = K.sb(st, [128, NH], F32); Btot = S.buf()
        num = K.sb(st, [128, NH], F32); Bnum = S.buf()
        om = K.sb(st, [128, NH], F32); Bom = S.buf()
        S.op("dve", lambda e: e.reduce_sum(out=tot[:], in_=lraw[:], axis=AX.X), reads=[Blraw], writes=[Btot])
        S.op("dve", lambda e: e.reduce_sum(out=num[:], in_=lraw[:, :, 1:layer_idx + 1], axis=AX.X), reads=[Blraw], writes=[Bnum])
        S.op("dve", lambda e: e.reciprocal(out=tot[:], in_=tot[:]), reads=[Btot], writes=[Btot])
        S.op("dve", lambda e: e.tensor_tensor(out=num[:], in0=num[:], in1=tot[:], op=ALU.mult), reads=[Bnum, Btot], writes=[Bnum])
        S.op("dve", lambda e: e.tensor_scalar(out=om[:], in0=num[:], scalar1=-1.0, scalar2=1.0, op0=ALU.mult, op1=ALU.add), reads=[Bnum], writes=[Bom])
        xns = [(K.sb(st, [128, KC, TT], BF16), S.buf()) for _ in range(2)]
        qst = [(K.sb(st, [128, NH, TT], BF16), S.buf()) for _ in range(2)]
        gst = [(K.sb(st, [128, NH, TT], BF16), S.buf()) for _ in range(2)]
        kst = [(K.sb(st, [128, NH, TT], F32), S.buf()) for _ in range(2)]
        vst = [(K.sb(st, [128, NH, TT // 128, 128], BF16), S.buf()) for _ in range(2)]
        ntile = seq // TT
        per = nt_half // TT

        def load(i):
            xn, Bxn = xns[i % 2]
            S.dma("sp", lambda e: e.dma_start(out=xn[:], in_=xt_tile(XNF[i // per], (i % per) * TT, TT)), Bxn, writes=[Bxn])
        load(0)
        for i in range(ntile):
            xn, Bxn = xns[i % 2]
            q_, Bq_ = qst[i % 2]; g_, Bg_ = gst[i % 2]; k_, Bk_ = kst[i % 2]; v_, Bv_ = vst[i % 2]
            if i + 1 < ntile:
                load(i + 1)

            def ev_q(j, ps, Bps):
                S.op("act", lambda e: e.activation(out=q_[:, j, :], in_=ps[:, :TT], func=AF.Silu), reads=[Bps], writes=[Bq_])
            linear_fm(K, wq_s, Bwq, xn, Bxn, TT, range(NH), ev_q)

            def ev_g(j, ps, Bps):
                S.op("act", lambda e: e.activation(out=g_[:, j, :], in_=ps[:, :TT], func=AF.Silu), reads=[Bps], writes=[Bg_])
            linear_fm(K, wg_s, Bwg, xn, Bxn, TT, range(NH), ev_g)

            def ev_f(j, ps, Bps):
                S.op("act", lambda e: e.activation(out=k_[:, j, :], in_=ps[:, :TT], func=AF.Sigmoid, scale=-1.0), reads=[Bps], writes=[Bk_])
                S.op("dve", lambda e: e.tensor_scalar(out=k_[:, j, :], in0=k_[:, j, :], scalar1=om[:, j:j + 1], scalar2=None, op0=ALU.mult),
                     reads=[Bk_, Bom], writes=[Bk_])
            linear_fm(K, wf_s, Bwf, xn, Bxn, TT, range(NH), ev_f)
            for sub in range(TT // 128):
                ps, Bps = K.ps()
                for k in range(KC):
                    S.op("pe", lambda e, k=k: e.matmul(ps[:, :HW], xn[:, k, sub * 128:(sub + 1) * 128], wi_s[:, k, :],
                                                       start=(k == 0), stop=(k == KC - 1)),
                         reads=[Bxn, Bwi], writes=[Bps], inc=(k == KC - 1))
                psv = ps[:, :HW].rearrange("p (h d) -> p h d", h=NH)
                S.op("dve", lambda e: e.tensor_copy(out=v_[:, :, sub, :], in_=psv), reads=[Bps], writes=[Bv_])
            s0 = i * TT
            j0 = s0 // 128
            S.dma("sp", lambda e: e.dma_start(out=SQ[:, :, s0:s0 + TT].rearrange("h p t -> p h t"), in_=q_[:]), Bq_, reads=[Bq_])
            S.dma("sp", lambda e: e.dma_start(out=SG[:, :, s0:s0 + TT].rearrange("h p t -> p h t"), in_=g_[:]), Bg_, reads=[Bg_])
            S.dma("sp", lambda e: e.dma_start(out=KKd[:, :, s0:s0 + TT].rearrange("h p t -> p h t"), in_=k_[:]), Bk_, reads=[Bk_])
            S.dma("sp", lambda e: e.dma_start(out=V[:, :, j0:j0 + TT // 128, :].rearrange("h p j d -> p h j d"), in_=v_[:]), Bv_, reads=[Bv_])
        S.barrier()


def stage_hg_core(K, SQ, KKd, V, SG, gon, OT2, seq, half=None):
    S = K.S
    C = 64
    NCB = TT // C
    with ExitStack() as st:
        go = K.sb(st, [128, 1], F32); Bgo = S.buf()
        load_vec(K, go, Bgo, gon)
        mle = K.sb(st, [64, 64], F32); Bmle = S.buf()
        S.op("dve", lambda e: e.tensor_copy(out=mle[:], in_=K.cf[0:64, 0:64]), reads=[K.Bcf], writes=[Bmle])
        hd = []
        for h in range(NH):
            d = dict(Sf=K.sb(st, [128, 128], F32), BSf=S.buf(), Sb=K.sb(st, [128, 128], BF16), BSb=S.buf())
            S.op("dve", lambda e: e.memset(d["Sf"][:], 0.0), writes=[d["BSf"]])
            S.op("dve", lambda e: e.memset(d["Sb"][:], 0.0), writes=[d["BSb"]])
            for nm, shp, dt in (("sq", [128, TT], BF16), ("kk", [128, TT], F32), ("sg", [128, TT], BF16),
                                ("v", [64, NCB, 128], BF16), ("lf", [128, TT], F32), ("G", [128, TT], F32),
                                ("eG", [128, TT], F32), ("enG", [128, TT], F32), ("Qt", [128, TT], BF16),
                                ("Kt", [128, TT], BF16), ("Kh", [128, TT], BF16), ("KhT", [64, NCB, 128], BF16),
                                ("AT", [64, 2, 64], BF16), ("osq", [128, TT], BF16), ("rs", [128, TT], F32),
                                ("of", [128, TT], F32), ("ob", [128, TT], BF16)):
                d[nm] = K.sb(st, shp, dt)
                d["B" + nm] = S.buf()
            hd.append(d)
        nblk = seq // TT
        half = (seq // 2) if half is None else half
        rr = [0]

        def psr():
            t = K.ps_i(rr[0] % 4)
            rr[0] += 1
            return t
        for blk in range(nblk):
            s0 = blk * TT
            for h in range(NH):
                d = hd[h]
                S.dma("sp", lambda e: e.dma_start(out=d["sq"][:], in_=SQ[h][:, s0:s0 + TT]), d["Bsq"], writes=[d["Bsq"]])
                S.dma("sp", lambda e: e.dma_start(out=d["kk"][:], in_=KKd[h][:, s0:s0 + TT]), d["Bkk"], writes=[d["Bkk"]])
                S.dma("sp", lambda e: e.dma_start(out=d["sg"][:], in_=SG[h][:, s0:s0 + TT]), d["Bsg"], writes=[d["Bsg"]])
                for hh in range(2):
                    S.dma("sp", lambda e: e.dma_start(out=d["v"][:, hh::2, :], in_=V[h][hh * 64:(hh + 1) * 64, s0 // 128: s0 // 128 + TT // 128, :]),
                          d["Bv"], writes=[d["Bv"]])
            for h in range(NH):
                d = hd[h]
                S.op("act", lambda e: e.activation(out=d["lf"][:], in_=d["kk"][:], func=AF.Ln, scale=-1.0, bias=K.oneT[:]), reads=[d["Bkk"], K.Bc], writes=[d["Blf"]])
                S.op("dve", lambda e: e.tensor_tensor_scan(out=d["G"][:], data0=K.scanmask, data1=d["lf"][:], initial=0.0, op0=ALU.mult, op1=ALU.add),
                     reads=[d["Blf"], K.Bcf], writes=[d["BG"]])
                S.op("act", lambda e: e.activation(out=d["eG"][:], in_=d["G"][:], func=AF.Exp), reads=[d["BG"]], writes=[d["BeG"]])
                S.op("act", lambda e: e.activation(out=d["enG"][:], in_=d["G"][:], func=AF.Exp, scale=-1.0), reads=[d["BG"]], writes=[d["BenG"]])
                S.op("dve", lambda e: e.tensor_tensor(out=d["Qt"][:], in0=d["sq"][:], in1=d["eG"][:], op=ALU.mult), reads=[d["Bsq"], d["BeG"]], writes=[d["BQt"]])
                S.op("dve", lambda e: e.tensor_tensor(out=d["Kt"][:], in0=d["kk"][:], in1=d["enG"][:], op=ALU.mult), reads=[d["Bkk"], d["BenG"]], writes=[d["BKt"]])
                for c in range(NCB):
                    S.op("dve", lambda e: e.tensor_scalar(out=d["Kh"][:, c * C:(c + 1) * C], in0=d["Kt"][:, c * C:(c + 1) * C],
                                                          scalar1=d["eG"][:, (c + 1) * C - 1:(c + 1) * C], scalar2=None, op0=ALU.mult),
                         reads=[d["BKt"], d["BeG"]], writes=[d["BKh"]])
                for c in range(NCB):
                    ps, Bps = psr()
                    pst = ps.bitcast(BF16)
                    S.op("pe", lambda e: e.transpose(pst[0:64, 0:128], d["Kh"][:, c * C:(c + 1) * C], K.ident[:]), reads=[d["BKh"], K.Bc], writes=[Bps])
                    S.op("act", lambda e: e.copy(out=d["KhT"][:, c, :], in_=pst[0:64, 0:128]), reads=[Bps], writes=[d["BKhT"]])
            for c in range(NCB):
                for h in range(NH):
                    d = hd[h]
                    po, Bpo = K.ps_i(4 + h)
                    cs = slice(c * C, (c + 1) * C)
                    ps, Bps = psr()
                    S.op("pe", lambda e: e.matmul(ps[0:64, 0:64], d["Kt"][:, cs], d["Qt"][:, cs], start=True, stop=True),
                         reads=[d["BKt"], d["BQt"]], writes=[Bps])
                    S.op("dve", lambda e: e.tensor_tensor(out=d["AT"][:, c % 2, :], in0=ps[0:64, 0:64], in1=mle[:], op=ALU.mult),
                         reads=[Bps, Bmle], writes=[d["BAT"]])
                    S.op("pe", lambda e: e.matmul(po[:, cs], d["Sb"][:], d["Qt"][:, cs], start=True, stop=False),
                         reads=[d["BSb"], d["BQt"]], writes=[Bpo], inc=False)
                    S.op("pe", lambda e: e.matmul(po[:, cs], d["v"][:, c, :], d["AT"][:, c % 2, :], start=False, stop=True),
                         reads=[d["Bv"], d["BAT"]], writes=[Bpo])
                    ps2, Bps2 = psr()
                    S.op("pe", lambda e: e.matmul(ps2[:, 0:128], d["KhT"][:, c, :], d["v"][:, c, :], start=True, stop=True),
                         reads=[d["BKhT"], d["Bv"]], writes=[Bps2])
                    S.op("dve", lambda e: e.scalar_tensor_tensor(out=d["Sf"][:], in0=d["Sf"][:], scalar=d["eG"][:, (c + 1) * C - 1:(c + 1) * C],
                                                                 in1=ps2[:, 0:128], op0=ALU.mult, op1=ALU.add),
                         reads=[d["BSf"], d["BeG"], Bps2], writes=[d["BSf"]])
                    S.op("act", lambda e: e.copy(out=d["Sb"][:], in_=d["Sf"][:]), reads=[d["BSf"]], writes=[d["BSb"]])
            for h in range(NH):
                d = hd[h]
                po, Bpo = K.ps_i(4 + h)
                S.op("act", lambda e: e.activation(out=d["osq"][:], in_=po[:, :TT], func=AF.Square), reads=[Bpo], writes=[d["Bosq"]])
                ps, Bps = psr()
                S.op("pe", lambda e: e.matmul(ps[:, :TT], K.ones128[:], d["osq"][:], start=True, stop=True), reads=[K.Bc, d["Bosq"]], writes=[Bps])
                S.op("act", lambda e: e.activation(out=d["rs"][:], in_=ps[:, :TT], func=AF.Sqrt, bias=K.epsT[:], scale=1.0), reads=[Bps, K.Bc], writes=[d["Brs"]])
                S.op("dve", lambda e: e.reciprocal(out=d["rs"][:], in_=d["rs"][:]), reads=[d["Brs"]], writes=[d["Brs"]])
                S.op("dve", lambda e: e.scalar_tensor_tensor(out=d["of"][:], in0=po[:, :TT], scalar=go[:, 0:1], in1=d["rs"][:], op0=ALU.mult, op1=ALU.mult),
                     reads=[Bpo, Bgo, d["Brs"]], writes=[d["Bof"]])
                S.op("dve", lambda e: e.tensor_tensor(out=d["ob"][:], in0=d["of"][:], in1=d["sg"][:], op=ALU.mult), reads=[d["Bof"], d["Bsg"]], writes=[d["Bob"]])
                S.dma("sp", lambda e: e.dma_start(out=OT2[s0 // half, h, :, (s0 % half):(s0 % half) + TT], in_=d["ob"][:]), d["Bob"], reads=[d["Bob"]])
        S.barrier()


def stage_sgu(K, XT, XO, gx, wu, wv, gvb, wsT, bs, wout, nt):
    S = K.S
    with ExitStack() as st:
        wu_s = K.sb(st, [128, KC, D], BF16); Bwu = S.buf()
        wv_s = K.sb(st, [128, KC, D], BF16); Bwv = S.buf()
        wo_s = K.sb(st, [128, KC, D], BF16); Bwo = S.buf()
        load_w(K, wu_s, Bwu, wu)
        load_w(K, wv_s, Bwv, wv)
        load_w(K, wo_s, Bwo, wout)
        g = K.sb(st, [128, KC], F32); Bg = S.buf()
        load_vec(K, g, Bg, gx)
        gv = K.sb(st, [128, D], F32); Bgv = S.buf()
        load_vec(K, gv, Bgv, gvb)
        wsf = K.sb(st, [128, 8, 128], F32); Bwsf = S.buf()
        load_vec(K, wsf, Bwsf, wsT)
        wsm = K.sb(st, [128, 8, 128], BF16); Bwsm = S.buf()
        for gi in range(8):
            S.op("dve", lambda e: e.tensor_tensor(out=wsm[:, gi, :], in0=wsf[:, gi, :], in1=K.cf[:, 0:128], op=ALU.mult),
                 reads=[Bwsf, K.Bcf], writes=[Bwsm])
        bsf = K.sb(st, [1, D], F32); Bbsf = S.buf()
        load_vec(K, bsf, Bbsf, bs)
        bsb = K.sb(st, [1, D], BF16); Bbsb = S.buf()
        S.op("dve", lambda e: e.tensor_copy(out=bsb[:], in_=bsf[:]), reads=[Bbsf], writes=[Bbsb])
        xs = [(K.sb(st, [128, KC, TT], F32), S.buf()) for _ in range(2)]
        sq = K.sb(st, [128, KC, TT], BF16); Bsq = S.buf()
        r = K.sb(st, [128, TT], F32); Br = S.buf()
        xn = K.sb(st, [128, KC, TT], BF16); Bxn = S.buf()
        uT = K.sb(st, [128, KC, TT], BF16); BuT = S.buf()
        zT = K.sb(st, [128, KC, TT], BF16); BzT = S.buf()
        vf = [(K.sb(st, [128, D], F32), S.buf()) for _ in range(2)]
        junk = K.sb(st, [128, D], BF16); Bjunk = S.buf()
        ssq = [(K.sb(st, [128, 2], F32), S.buf()) for _ in range(2)]
        vn = [(K.sb(st, [128, D], BF16), S.buf()) for _ in range(2)]
        ntile = nt // TT

        def load(i):
            x, Bx = xs[i % 2]
            S.dma("sp", lambda e: e.dma_start(out=x[:], in_=xt_tile(XT, i * TT, TT)), Bx, writes=[Bx])
        load(0)
        it = 0
        for i in range(ntile):
            x, Bx = xs[i % 2]
            if i + 1 < ntile:
                load(i + 1)
            rmsnorm_fm(K, x, Bx, g, Bg, sq, Bsq, r, Br, xn, Bxn, TT)

            def ev_u(j, ps, Bps):
                S.op("act", lambda e: e.activation(out=uT[:, j, :], in_=ps[:, :TT], func=AF.Gelu), reads=[Bps], writes=[BuT])
            linear_fm(K, wu_s, Bwu, xn, Bxn, TT, range(KC), ev_u)
            for sub in range(TT // 128):
                v_, Bv_ = vf[it % 2]; s2, Bs2 = ssq[it % 2]; n_, Bn_ = vn[it % 2]
                it += 1
                ts = slice(sub * 128, (sub + 1) * 128)
                for hf in range(2):
                    ps, Bps = K.ps()
                    for k in range(KC):
                        S.op("pe", lambda e, k=k: e.matmul(ps[:, :512], xn[:, k, ts], wv_s[:, k, hf * 512:(hf + 1) * 512],
                                                           start=(k == 0), stop=(k == KC - 1)), reads=[Bxn, Bwv], writes=[Bps], inc=(k == KC - 1))
                    S.op("act", lambda e: e.activation(out=v_[:, hf * 512:(hf + 1) * 512], in_=ps[:, :512], func=AF.Gelu), reads=[Bps], writes=[Bv_])
                S.op("act", lambda e: e.activation(out=junk[:], in_=v_[:], func=AF.Square, accum_out=s2[:, 0:1]), reads=[Bv_], writes=[Bjunk, Bs2])
                S.op("dve", lambda e: e.tensor_scalar(out=s2[:, 1:2], in0=s2[:, 0:1], scalar1=1.0 / D, scalar2=EPS, op0=ALU.mult, op1=ALU.add),
                     reads=[Bs2], writes=[Bs2])
                S.op("act", lambda e: e.activation(out=s2[:, 1:2], in_=s2[:, 1:2], func=AF.Sqrt), reads=[Bs2], writes=[Bs2])
                S.op("dve", lambda e: e.reciprocal(out=s2[:, 1:2], in_=s2[:, 1:2]), reads=[Bs2], writes=[Bs2])
                S.op("dve", lambda e: e.scalar_tensor_tensor(out=n_[:], in0=v_[:], scalar=s2[:, 1:2], in1=gv[:], op0=ALU.mult, op1=ALU.mult),
                     reads=[Bv_, Bs2, Bgv], writes=[Bn_])
                for g0 in (0, 4):
                    ps, Bps = K.ps()
                    for gg in range(4):
                        gi = g0 + gg
                        S.op("pe", lambda e: e.matmul(ps[:, gg * 128:(gg + 1) * 128], n_[:, gi * 128:(gi + 1) * 128], wsm[:, gi, :], start=True, stop=False),
                             reads=[Bn_, Bwsm], writes=[Bps], inc=False)
                        S.op("pe", lambda e: e.matmul(ps[:, gg * 128:(gg + 1) * 128], K.onesrow[0:1, :], bsb[0:1, gi * 128:(gi + 1) * 128], start=False, stop=True),
                             reads=[K.Bc, Bbsb], writes=[Bps], inc=(gg == 3))
                    S.op("dve", lambda e: e.tensor_tensor(out=zT[:, g0:g0 + 4, ts], in0=ps[:, :512].rearrange("p (g t) -> p g t", g=4),
                                                          in1=uT[:, g0:g0 + 4, ts], op=ALU.mult), reads=[Bps, BuT], writes=[BzT])

            def ev_o(j, ps, Bps):
                S.op("dve", lambda e: e.tensor_tensor(out=x[:, j, :], in0=ps[:, :TT], in1=x[:, j, :], op=ALU.add), reads=[Bps, Bx], writes=[Bx])
            linear_fm(K, wo_s, Bwo, zT, BzT, TT, range(KC), ev_o)
            S.dma("sp", lambda e: e.dma_start(out=xt_tile(XO, i * TT, TT), in_=x[:]), Bx, reads=[Bx])
        S.barrier()


def fm(x):
    T, F_ = x.shape
    return np.ascontiguousarray(x.T.reshape(F_ // 128, 128, T))


def unfm(a):
    c, p, T = a.shape
    return np.ascontiguousarray(a.reshape(c * p, T).T)


def vec8(g):
    return np.ascontiguousarray(np.asarray(g, np.float32).reshape(-1, 128).T)


def _bucket_table():
    kk_ = np.arange(128)[:, None]
    jj = np.arange(1024)[None, :]
    d = jj - 384 - kk_
    n = np.maximum(d, 0)
    ex = 16
    nf = np.maximum(n, ex).astype(np.float32)
    large = ex + (np.log(nf / ex) / math.log(128 / ex) * (32 - ex)).astype(np.int32)
    large = np.minimum(large, 31)
    bk = np.where(n < ex, n, large)
    cm = np.where(d < 0, -30000.0, 0.0).astype(np.float32)
    return bk, cm


class Prog:
    def __init__(self, ncores=8):
        self.nc = bass.Bass("TRN2", target_bir_lowering=False)
        self.ncores = ncores
        self.in_maps = [dict() for _ in range(ncores)]
        self.out_names = []

    def inp(self, name, arrs):
        if not isinstance(arrs, (list, tuple)):
            arrs = [arrs] * self.ncores
        a0 = arrs[0]
        dt = BF16 if a0.dtype == NPBF else F32
        ap = self.nc.dram_tensor(name, list(a0.shape), dt, kind="ExternalInput").ap()
        for c in range(self.ncores):
            self.in_maps[c][name] = np.ascontiguousarray(arrs[c])
        return ap

    def out(self, name, shape, dt):
        self.out_names.append(name)
        return self.nc.dram_tensor(name, list(shape), dt, kind="ExternalOutput").ap()

    def tmp(self, name, shape, dt):
        return self.nc.dram_tensor(name, list(shape), dt, kind="Internal").ap()

    def run(self):
        res = run_bass_kernel_spmd(self.nc, self.in_maps, core_ids=list(range(self.ncores)))
        return res.results


class Host:
    def __init__(self, inp):
        self.i = {k: np.asarray(v) for k, v in inp.items()}
        self.bk, self.cm = _bucket_table()

    def core(self, c):
        return c // 2, c % 2

    def xT(self):
        x = self.i["x"]
        return [fm(x[c // 2, (c % 2) * NT:(c % 2 + 1) * NT]) for c in range(8)]

    def memT(self):
        return [fm(self.i["mem"][c // 2]) for c in range(8)]

    def heads(self, c):
        r = c % 2
        return list(range(r * NH, (r + 1) * NH))

    def hcols(self, c):
        return np.concatenate([np.arange(h * 128, (h + 1) * 128) for h in self.heads(c)])

    def da(self, j):
        I = self.i
        w = I["da_w_in"][j]
        out = {}
        for nm, off in (("wq", 0), ("wk", D), ("wv", 2 * D)):
            out[nm] = [np.ascontiguousarray(w[:, off + self.hcols(c)]) for c in range(8)]
        out["GB"] = [np.ascontiguousarray(np.stack([I["rel_bias"][h][self.bk] for h in self.heads(c)]).astype(np.float32)) for c in range(8)]
        out["CM"] = self.cm
        miscs = []
        for c in range(8):
            m = np.zeros((128, 4 * 64 + NH + 1), np.float32)
            m[:, 0:64] = I["da_lq1"][j]; m[:, 64:128] = I["da_lk1"][j]; m[:, 128:192] = I["da_lq2"][j]; m[:, 192:256] = I["da_lk2"][j]
            for hi, h in enumerate(self.heads(c)):
                m[:, 256 + hi] = I["rel_bias"][h, 31]
            m[:, 256 + NH] = I["da_subln"][j]
            miscs.append(m)
        out["misc"] = miscs
        out["wout"] = I["da_w_out"][j]
        return out

    def hg(self, j):
        I = self.i
        w = I["hg_w_in"][j]
        out = {}
        for k, nm in enumerate(("wq", "wf", "wi", "wg")):
            out[nm] = [np.ascontiguousarray(w[:, k * D + self.hcols(c)]) for c in range(8)]
        lbr = I["hg_lower_bounds"]
        out["oml"] = [np.ascontiguousarray(lbr[:, self.hcols(c)].reshape(lbr.shape[0], NH, 128).transpose(2, 1, 0)) for c in range(8)]
        out["gon"] = np.ascontiguousarray(I["hg_onorm"][j].reshape(128, 1))
        out["wout"] = I["hg_w_out"][j]
        return out

    def sg(self, j):
        I = self.i
        w = I["sg_w_in"][j]
        return dict(wu=np.ascontiguousarray(w[:, :D]), wv=np.ascontiguousarray(w[:, D:]),
                    gvb=np.ascontiguousarray(np.broadcast_to(I["sg_vnorm"][j], (128, D))),
                    wsT=np.ascontiguousarray(I["sg_w_s"][j].transpose(2, 0, 1)),
                    bs=np.ascontiguousarray(I["sg_b_s"][j].reshape(1, D)), wout=I["sg_w_out"][j])


def lam_init_of(layer_idx):
    return 0.8 - 0.6 * math.exp(-0.3 * layer_idx)


def emit_tail(P, K, H, i, XT_in, OG, wout, pre, next_norm_g, final, tagp):
    I = H.i
    mk = P.tmp
    cur = XT_in
    if OG is not None:
        X1 = mk(tagp + "X1", [KC, 128, NT], F32)
        stage_mix_out(K, cur, X1, OG, P.inp(tagp + "wout", wout), NT)
        cur = X1
    X2 = mk(tagp + "X2", [KC, 128, NT], F32)
    stage_cross(K, cur, X2, pre["memT"], pre["gmem"], P.inp(tagp + "gcx", vec8(I["norm_cross"][i])),
                P.inp(tagp + "cwq", I["ca_w_q"][i]), P.inp(tagp + "cwkv", I["ca_w_kv"][i]), P.inp(tagp + "cwo", I["ca_w_o"][i]), NT)
    X3 = mk(tagp + "X3", [KC, 128, NT], F32)
    XNs = mk(tagp + "XNs", [KC, 128, NT], BF16)
    gf = P.inp(tagp + "gfx", vec8(I["norm_ffn"][i]))
    wgu = P.inp(tagp + "wgu", I["ffn_w_gu"][i])
    wdn = P.inp(tagp + "wdn", I["ffn_w_down"][i])
    stage_ffn(K, X2, X3, XNs, gf, wgu, wdn, 0, NT)
    return X3, XNs, gf, wgu, wdn


def kernel_multi(**inp):
    H = Host(inp)
    I = H.i
    consts = make_consts()
    xT = H.xT()
    memT = H.memT()
    gmem = vec8(I["norm_mem"])
    depth = I["norm_mix"].shape[0]

    def newprog():
        P = Prog()
        st = ExitStack()
        K = Ctx(P.nc, st, P.inp("consts", consts))
        return P, K, st

    def pair_gather(xn):
        return [np.stack([xn[2 * (c // 2)], xn[2 * (c // 2) + 1]]) for c in range(8)]

    def pair_a2a(ot2):
        return [np.stack([ot2[2 * (c // 2) + rp][c % 2] for rp in range(2)]) for c in range(8)]

    P, K, st = newprog()
    with st:
        XT = P.inp("XT", xT)
        XN = P.out("XN", [KC, 128, NT], BF16)
        stage_norm(K, XT, P.inp("g", vec8(I["norm_mix"][0])), XN, NT)
    res = P.run()
    xn = [r["XN"] for r in res]
    OG = None
    wout = None
    for i in range(depth):
        kind, j = i % 3, i // 3
        if kind == 0:
            d = H.da(j)
            P, K, st = newprog()
            with st:
                XNF = P.inp("XNF", pair_gather(xn))
                QT = P.tmp("QT", [NH, 128, SEQ], BF16); KT = P.tmp("KT", [NH, 128, SEQ], BF16)
                V = P.tmp("V", [NH, 128, SEQ // 128, 128], BF16)
                OT2 = P.out("OT2", [2, NH, 128, NT], BF16)
                stage_da_in(K, XNF, P.inp("wq", d["wq"]), P.inp("wk", d["wk"]), P.inp("wv", d["wv"]), QT, KT, V, SEQ, NT)
                stage_da_core(K, QT, KT, V, P.inp("GB", d["GB"]), P.inp("CM", d["CM"]), P.inp("misc", d["misc"]), OT2, SEQ, lam_init_of(i))
            res = P.run()
            OG = pair_a2a([r["OT2"] for r in res]); wout = d["wout"]
        elif kind == 1:
            d = H.hg(j)
            P, K, st = newprog()
            with st:
                XNF = P.inp("XNF", pair_gather(xn))
                SQ = P.tmp("SQ", [NH, 128, SEQ], BF16); KKd = P.tmp("KK", [NH, 128, SEQ], F32)
                V = P.tmp("V", [NH, 128, SEQ // 128, 128], BF16); SG = P.tmp("SG", [NH, 128, SEQ], BF16)
                OT2 = P.out("OT2", [2, NH, 128, NT], BF16)
                stage_hg_in(K, XNF, P.inp("wq", d["wq"]), P.inp("wf", d["wf"]), P.inp("wi", d["wi"]), P.inp("wg", d["wg"]),
                            P.inp("oml", d["oml"]), SQ, KKd, V, SG, SEQ, NT, layer_idx=i)
                stage_hg_core(K, SQ, KKd, V, SG, P.inp("gon", d["gon"]), OT2, SEQ)
            res = P.run()
            OG = pair_a2a([r["OT2"] for r in res]); wout = d["wout"]
        P, K, st = newprog()
        with st:
            XT = P.inp("XT", xT)
            pre = dict(memT=P.inp("memT", memT), gmem=P.inp("gmem", gmem))
            cur = XT
            if kind == 2:
                d = H.sg(j)
                Xs = P.tmp("Xs", [KC, 128, NT], F32)
                stage_sgu(K, cur, Xs, P.inp("gmx", vec8(I["norm_mix"][i])), P.inp("wu", d["wu"]), P.inp("wv", d["wv"]), P.inp("gvb", d["gvb"]),
                          P.inp("wsT", d["wsT"]), P.inp("bs", d["bs"]), P.inp("swout", d["wout"]), NT)
                cur = Xs
                ogap = None
            else:
                ogap = P.inp("OG", OG)
            X3, XNs, gf, wgu, wdn = emit_tail(P, K, H, i, cur, ogap, wout, pre, None, False, "t_")
            last = (i == depth - 1)
            XO = P.out("XO", [KC, 128, NT], F32) if not last else P.tmp("XO", [KC, 128, NT], F32)
            stage_ffn(K, X3, XO, XNs, gf, wgu, wdn, 1, NT)
            if last:
                OUT = P.out("OUT", [KC, 128, NT], F32)
                stage_norm(K, XO, P.inp("gfin", vec8(I["norm_final"])), OUT, NT, out_f32=True)
            elif (i + 1) % 3 != 2:
                XN = P.out("XN", [KC, 128, NT], BF16)
                stage_norm(K, XO, P.inp("gnx", vec8(I["norm_mix"][i + 1])), XN, NT)
        res = P.run()
        if last:
            outT = [r["OUT"] for r in res]
        else:
            xT = [r["XO"] for r in res]
            if (i + 1) % 3 != 2:
                xn = [r["XN"] for r in res]
    B = I["x"].shape[0]
    out = np.empty((B, SEQ, D), np.float32)
    for c in range(8):
        out[c // 2, (c % 2) * NT:(c % 2 + 1) * NT] = unfm(outT[c])
    return out


def kernel_fused(**inp):
    H = Host(inp)
    I = H.i
    depth = I["norm_mix"].shape[0]
    B = I["x"].shape[0]
    S_ = SEQ
    P = Prog()
    nc = P.nc
    bmap = [c % B for c in range(8)]
    with ExitStack() as st:
        K = Ctx(nc, st, P.inp("consts", make_consts()))
        XT = P.inp("XT", [fm(I["x"][b]) for b in bmap])
        memT = P.inp("memT", [fm(I["mem"][b]) for b in bmap])
        gmem = P.inp("gmem", vec8(I["norm_mem"]))
        XA = P.tmp("XA", [KC, 128, S_], F32)
        XB = P.tmp("XB", [KC, 128, S_], F32)
        XN = P.tmp("XN", [1, KC, 128, S_], BF16)
        XNs = P.tmp("XNs", [KC, 128, S_], BF16)
        OT = [P.tmp("OT%d" % g, [1, NH, 128, S_], BF16) for g in range(2)]
        QT = P.tmp("QT", [NH, 128, S_], BF16)
        KT = P.tmp("KT", [NH, 128, S_], BF16)
        V = P.tmp("V", [NH, 128, S_ // 128, 128], BF16)
        KKd = P.tmp("KK", [NH, 128, S_], F32)
        SGd = P.tmp("SG", [NH, 128, S_], BF16)
        OUT = P.out("OUT", [KC, 128, S_], F32)
        cur = XT
        bufs = [XA, XB]
        nb = [0]

        def nxt():
            t = bufs[nb[0] % 2]
            nb[0] += 1
            return t
        for i in range(depth):
            kind, j = i % 3, i // 3
            tg = "L%d_" % i
            if kind in (0, 1):
                stage_norm(K, cur, P.inp(tg + "gmix", vec8(I["norm_mix"][i])), XN[0], S_)
                d = H.da(j) if kind == 0 else H.hg(j)
                if kind == 0:
                    CM = P.inp(tg + "CM", d["CM"])
                else:
                    gon = P.inp(tg + "gon", d["gon"])
                for g in range(2):
                    tgg = tg + "g%d_" % g
                    if kind == 0:
                        stage_da_in(K, XN, P.inp(tgg + "wq", d["wq"][g]), P.inp(tgg + "wk", d["wk"][g]), P.inp(tgg + "wv", d["wv"][g]),
                                    QT, KT, V, S_, S_)
                        stage_da_core(K, QT, KT, V, P.inp(tgg + "GB", d["GB"][g]), CM, P.inp(tgg + "misc", d["misc"][g]), OT[g], S_,
                                      lam_init_of(i), half=S_)
                    else:
                        stage_hg_in(K, XN, P.inp(tgg + "wq", d["wq"][g]), P.inp(tgg + "wf", d["wf"][g]), P.inp(tgg + "wi", d["wi"][g]),
                                    P.inp(tgg + "wg", d["wg"][g]), P.inp(tgg + "oml", d["oml"][g]), QT, KKd, V, SGd, S_, S_, layer_idx=i)
                        stage_hg_core(K, QT, KKd, V, SGd, gon, OT[g], S_, half=S_)
                o = nxt()
                stage_mix_out(K, cur, o, [OT[0][0], OT[1][0]], P.inp(tg + "wout", d["wout"]), S_)
                cur = o
            else:
                d = H.sg(j)
                o = nxt()
                stage_sgu(K, cur, o, P.inp(tg + "gmix", vec8(I["norm_mix"][i])), P.inp(tg + "wu", d["wu"]), P.inp(tg + "wv", d["wv"]),
                          P.inp(tg + "gvb", d["gvb"]), P.inp(tg + "wsT", d["wsT"]), P.inp(tg + "bs", d["bs"]), P.inp(tg + "wout", d["wout"]), S_)
                cur = o
            o = nxt()
            stage_cross(K, cur, o, memT, gmem, P.inp(tg + "gcx", vec8(I["norm_cross"][i])), P.inp(tg + "cwq", I["ca_w_q"][i]),
                        P.inp(tg + "cwkv", I["ca_w_kv"][i]), P.inp(tg + "cwo", I["ca_w_o"][i]), S_)
            cur = o
            gf = P.inp(tg + "gfx", vec8(I["norm_ffn"][i]))
            wgu = P.inp(tg + "wgu", I["ffn_w_gu"][i])
            wdn = P.inp(tg + "wdn", I["ffn_w_down"][i])
            for hf in range(2):
                o = nxt()
                stage_ffn(K, cur, o, XNs, gf, wgu, wdn, hf, S_)
                cur = o
        stage_norm(K, cur, P.inp("gfin", vec8(I["norm_final"])), OUT, S_, out_f32=True)
        K.S.barrier()
        print("fused program: %d instructions, %d dma sems" % (K.S.ninst, K.S.nsem))
    res = P.run()
    out = np.empty((B, SEQ, D), np.float32)
    for b in range(B):
        out[b] = unfm(res[b]["OUT"])
    return out


def kernel(**inputs):
    return kernel_fused(**inputs)
```

```python
import math
import numpy as np
import ml_dtypes
from contextlib import ExitStack
import concourse.bass as bass
import concourse.mybir as mybir
from concourse.bass_utils import run_bass_kernel_spmd

F32 = mybir.dt.float32
BF16 = mybir.dt.bfloat16
AF = mybir.ActivationFunctionType
ALU = mybir.AluOpType
AX = mybir.AxisListType
NPBF = ml_dtypes.bfloat16

D = 1024
KC = 8
SEQ = 8192
NT = 4096
NH = 4
TT = 512
EPS = 1e-6
DFF = 2816
FFH = DFF // 2
MEM = 256


class Buf:
    __slots__ = ("name", "w", "r", "dsem", "dcnt")

    def __init__(self, name=""):
        self.name = name
        self.w = None
        self.r = {}
        self.dsem = None


class Sched:
    ENGS = ("pe", "act", "dve", "pool", "sp")

    def __init__(self, nc, stack):
        self.nc = nc
        self.stack = stack
        self.esem, self.cnt, self.waited = {}, {}, {}
        self.semown = {}
        for e in self.ENGS:
            self.esem[e] = stack.enter_context(nc.semaphore("es_" + e))
            self.cnt[e] = 0
            self.waited[e] = {}
            self.semown[id(self.esem[e])] = e
        self.em = {"pe": nc.tensor, "act": nc.scalar, "dve": nc.vector, "pool": nc.gpsimd, "sp": nc.sync}
        self.ninst = 0
        self.nsem = 0
        self.free_dsems = []
        self.local_bufs = []

    def buf(self, name=""):
        b = Buf(name)
        self.local_bufs.append(b)
        return b

    def _deps(self, eng, reads, writes):
        deps = {}

        def add(tok, raw):
            sem, val = tok
            own = self.semown.get(id(sem))
            if own == eng and (eng == "pe" or not raw):
                return
            k = id(sem)
            if k not in deps or deps[k][1] < val:
                deps[k] = (sem, val)

        for b in reads:
            if b.w is not None:
                add(b.w, True)
        for b in writes:
            if b.w is not None:
                add(b.w, False)
            for k, tok in b.r.items():
                add(tok, False)
        waits = []
        wd = self.waited[eng]
        for k, (sem, val) in deps.items():
            if wd.get(k, 0) < val:
                wd[k] = val
                waits.append((sem, val))
        return waits

    def _mark(self, tok, reads, writes):
        sem, val = tok
        k = id(sem)
        for b in reads:
            if k not in b.r or b.r[k][1] < val:
                b.r[k] = (sem, val)
        for b in writes:
            b.w = tok
            b.r = {}

    def _emit(self, eng, waits, fn, sem, inc):
        e = self.em[eng]
        for (s, v) in waits:
            e.wait_ge(s, v)
        if fn is None:
            return
        ins = fn(e)
        if sem is not None:
            ins.then_inc(sem, inc)
        self.ninst += 1

    def op(self, eng, fn, reads=(), writes=(), inc=True):
        waits = self._deps(eng, reads, writes)
        if inc:
            self.cnt[eng] += 1
            tok = (self.esem[eng], self.cnt[eng])
            self._emit(eng, waits, fn, self.esem[eng], 1)
        else:
            tok = (self.esem[eng], self.cnt[eng] + 1)
            self._emit(eng, waits, fn, None, 0)
        self._mark(tok, reads, writes)

    def dma(self, q, fn, sb, reads=(), writes=()):
        if sb.dsem is None:
            if self.free_dsems:
                sb.dsem = self.free_dsems.pop()
            else:
                sem = self.stack.enter_context(self.nc.semaphore("ds_%d" % self.nsem))
                self.nsem += 1
                sb.dsem = [sem, 0]
        waits = self._deps(q, reads, writes)
        sb.dsem[1] += 16
        tok = (sb.dsem[0], sb.dsem[1])
        self._emit(q, waits, fn, sb.dsem[0], 16)
        self._mark(tok, reads, writes)

    def wait_all(self, eng, bufs):
        waits = self._deps(eng, bufs, bufs)
        self._emit(eng, waits, None, None, 0)

    def barrier(self, extra=()):
        bufs = list(self.local_bufs) + list(extra)
        for e in self.ENGS:
            waits = self._deps(e, bufs, bufs)
            wd = self.waited[e]
            for f in self.ENGS:
                if f == e:
                    continue
                k = id(self.esem[f])
                if wd.get(k, 0) < self.cnt[f]:
                    wd[k] = self.cnt[f]
                    waits.append((self.esem[f], self.cnt[f]))
            self._emit(e, waits, None, None, 0)
        for b in self.local_bufs:
            if b.dsem is not None:
                self.free_dsems.append(b.dsem)
        self.local_bufs = []


class Ctx:
    def __init__(self, nc, st, consts_ap):
        self.nc = nc
        self.S = Sched(nc, st)
        self.uid = 0
        S = self.S
        self.psum = []
        self.pd = []
        for i in range(4):
            t = st.enter_context(nc.psum_tensor("psd%d" % i, [128, 1024], F32))
            self.pd.append(t)
            for hf in range(2):
                self.psum.append((t[:, hf * 512:(hf + 1) * 512], Buf("ps%d" % (2 * i + hf))))
        self.ps_rr = 0
        self.Bc = Buf("const")
        self.onesD = self.sb(st, [128, 128], BF16)
        self.ones128 = self.sb(st, [128, 128], BF16)
        self.ones1 = self.sb(st, [128, 128], BF16)
        self.epsT = self.sb(st, [128, 1], F32)
        self.cf = self.sb(st, [128, 2 * 128 + 512], F32)
        self.maskle = self.sb(st, [128, 128], BF16)
        self.ident = self.sb(st, [128, 128], BF16)
        Bl = Buf("cload")
        S.dma("sp", lambda e: e.dma_start(out=self.cf[:], in_=consts_ap), Bl, writes=[Bl])
        S.op("dve", lambda e: e.memset(self.onesD[:], 1.0 / D), writes=[self.Bc])
        S.op("dve", lambda e: e.memset(self.ones128[:], 1.0 / 128), writes=[self.Bc])
        S.op("dve", lambda e: e.memset(self.ones1[:], 1.0), writes=[self.Bc])
        S.op("dve", lambda e: e.memset(self.epsT[:], EPS), writes=[self.Bc])
        self.ones1f = self.sb(st, [128, 128], F32)
        S.op("dve", lambda e: e.memset(self.ones1f[:], 1.0), writes=[self.Bc])
        self.oneT = self.sb(st, [128, 1], F32)
        S.op("dve", lambda e: e.memset(self.oneT[:], 1.0), writes=[self.Bc])
        self.onesrow = self.sb(st, [1, 128], BF16)
        S.op("dve", lambda e: e.memset(self.onesrow[:], 1.0), writes=[self.Bc])
        S.op("dve", lambda e: e.tensor_copy(out=self.maskle[:], in_=self.cf[:, 0:128]), reads=[Bl], writes=[self.Bc])
        S.op("dve", lambda e: e.tensor_copy(out=self.ident[:], in_=self.cf[:, 128:256]), reads=[Bl], writes=[self.Bc])
        self.scanmask = self.cf[:, 256:768]
        self.Bcf = Bl

    def sb(self, st, shape, dt):
        self.uid += 1
        return st.enter_context(self.nc.sbuf_tensor("t%d" % self.uid, list(shape), dt))

    def ps(self):
        t, b = self.psum[self.ps_rr % 8]
        self.ps_rr += 1
        return t, b

    def ps_i(self, i):
        return self.psum[i]


def make_consts():
    c = np.zeros((128, 2 * 128 + 512), np.float32)
    c[:, 0:128] = np.triu(np.ones((128, 128), np.float32))
    c[:, 128:256] = np.eye(128, dtype=np.float32)
    m = np.ones(512, np.float32)
    m[::64] = 0.0
    c[:, 256:768] = m[None, :]
    return c


def load_w(K, dst, Bdst, w_ap, q="pool"):
    kc = w_ap.shape[0] // 128
    src = w_ap.rearrange("(c p) n -> p c n", p=128)
    for c in range(kc):
        K.S.dma(q, lambda e, c=c: e.dma_start(out=dst[:, c, :], in_=src[:, c, :]), Bdst, writes=[Bdst])


def load_vec(K, dst, Bdst, v_ap, q="sp"):
    K.S.dma(q, lambda e: e.dma_start(out=dst[:], in_=v_ap), Bdst, writes=[Bdst])


def rmsnorm_a(K, x, Bx, sq, Bsq, n, kc=KC):
    K.S.op("act", lambda e: e.activation(out=sq[:, :kc, :n], in_=x[:, :kc, :n], func=AF.Square), reads=[Bx], writes=[Bsq])


def rmsnorm_b(K, x, Bx, g, Bg, sq, Bsq, rstd, Brstd, xn, Bxn, n, kc=KC, ones=None, xn_eng="dve"):
    S = K.S
    ones = K.onesD if ones is None else ones
    ps, Bps = K.ps()
    for c in range(kc):
        S.op("pe", lambda e, c=c: e.matmul(ps[:, :n], ones[:], sq[:, c, :n], start=(c == 0), stop=(c == kc - 1)),
             reads=[Bsq, K.Bc], writes=[Bps], inc=(c == kc - 1))
    S.op("act", lambda e: e.activation(out=rstd[:, :n], in_=ps[:, :n], func=AF.Ln, bias=K.epsT[:], scale=1.0),
         reads=[Bps, K.Bc], writes=[Brstd])
    S.op("act", lambda e: e.activation(out=rstd[:, :n], in_=rstd[:, :n], func=AF.Exp, scale=-0.5), reads=[Brstd], writes=[Brstd])
    for c in range(kc):
        S.op(xn_eng, lambda e, c=c: e.scalar_tensor_tensor(out=xn[:, c, :n], in0=x[:, c, :n], scalar=g[:, c:c + 1],
                                                           in1=rstd[:, :n], op0=ALU.mult, op1=ALU.mult),
             reads=[Bx, Bg, Brstd], writes=[Bxn])


def rmsnorm_fm(K, x, Bx, g, Bg, sq, Bsq, rstd, Brstd, xn, Bxn, n, kc=KC, ones=None, xn_eng="dve"):
    rmsnorm_a(K, x, Bx, sq, Bsq, n, kc)
    rmsnorm_b(K, x, Bx, g, Bg, sq, Bsq, rstd, Brstd, xn, Bxn, n, kc, ones, xn_eng)


def linear_fm(K, w, Bw, xin, Bxin, n, jlist, evac, kc=KC, wcol0=0):
    S = K.S
    for j in jlist:
        ps, Bps = K.ps()
        for k in range(kc):
            S.op("pe", lambda e, j=j, k=k, ps=ps: e.matmul(ps[:, :n], w[:, k, wcol0 + j * 128: wcol0 + (j + 1) * 128],
                                                           xin[:, k, :n], start=(k == 0), stop=(k == kc - 1)),
                 reads=[Bw, Bxin], writes=[Bps], inc=(k == kc - 1))
        evac(j, ps, Bps)


def xt_tile(ap3, t0, n):
    return ap3[:, :, t0:t0 + n].rearrange("c p t -> p c t")


def stage_norm(K, XT, gvec, OUT, nt, out_f32=False):
    S = K.S
    with ExitStack() as st:
        g = K.sb(st, [128, KC], F32); Bg = S.buf()
        load_vec(K, g, Bg, gvec)
        xs = [(K.sb(st, [128, KC, TT], F32), S.buf()) for _ in range(2)]
        sqs = [(K.sb(st, [128, KC, TT], BF16), S.buf()) for _ in range(2)]
        rs = [(K.sb(st, [128, TT], F32), S.buf()) for _ in range(2)]
        odt = F32 if out_f32 else BF16
        os_ = [(K.sb(st, [128, KC, TT], odt), S.buf()) for _ in range(2)]
        ntile = nt // TT
        for i in range(ntile):
            x, Bx = xs[i % 2]; sq, Bsq = sqs[i % 2]; r, Br = rs[i % 2]; o, Bo = os_[i % 2]
            S.dma("sp", lambda e, x=x, i=i: e.dma_start(out=x[:], in_=xt_tile(XT, i * TT, TT)), Bx, writes=[Bx])
            rmsnorm_fm(K, x, Bx, g, Bg, sq, Bsq, r, Br, o, Bo, TT)
            S.dma("sp", lambda e, o=o, i=i: e.dma_start(out=xt_tile(OUT, i * TT, TT), in_=o[:]), Bo, reads=[Bo])
        S.barrier()


def stage_cross(K, XT, XO, memT, gmem, gx, wq, wkv, wo, nt):
    S = K.S
    HC = 4
    with ExitStack() as st:
        wq_s = K.sb(st, [128, KC, D], BF16); Bwq = S.buf()
        wkv_s = K.sb(st, [128, KC, 2 * D], BF16); Bwkv = S.buf()
        wo_s = K.sb(st, [128, KC, D], BF16); Bwo = S.buf()
        gm = K.sb(st, [128, KC], F32); Bgm = S.buf()
        g = K.sb(st, [128, KC], F32); Bg = S.buf()
        load_vec(K, gm, Bgm, gmem)
        load_vec(K, g, Bg, gx)
        load_w(K, wkv_s, Bwkv, wkv)
        load_w(K, wq_s, Bwq, wq)
        load_w(K, wo_s, Bwo, wo)
        mx = K.sb(st, [128, KC, MEM], F32); Bmx = S.buf()
        msq = K.sb(st, [128, KC, MEM], BF16); Bmsq = S.buf()
        mr = K.sb(st, [128, MEM], F32); Bmr = S.buf()
        mn = K.sb(st, [128, KC, MEM], BF16); Bmn = S.buf()
        kT = K.sb(st, [128, KC, MEM], BF16); BkT = S.buf()
        vtok = K.sb(st, [128, 2, D], BF16); Bv = S.buf()
        S.dma("sp", lambda e: e.dma_start(out=mx[:], in_=memT.rearrange("c p t -> p c t")), Bmx, writes=[Bmx])
        rmsnorm_fm(K, mx, Bmx, gm, Bgm, msq, Bmsq, mr, Bmr, mn, Bmn, MEM)

        def ev_k(j, ps, Bps):
            S.op("act", lambda e: e.activation(out=kT[:, j, :], in_=ps[:, :MEM], func=AF.Copy), reads=[Bps], writes=[BkT])
        linear_fm(K, wkv_s, Bwkv, mn, Bmn, MEM, range(KC), ev_k)
        for mc in range(2):
            for half in range(2):
                ps, Bps = K.ps()
                for k in range(KC):
                    S.op("pe", lambda e, k=k, mc=mc, half=half, ps=ps: e.matmul(
                        ps[:, :512], mn[:, k, mc * 128:(mc + 1) * 128], wkv_s[:, k, D + half * 512: D + (half + 1) * 512],
                        start=(k == 0), stop=(k == KC - 1)), reads=[Bmn, Bwkv], writes=[Bps], inc=(k == KC - 1))
                S.op("dve", lambda e, mc=mc, half=half, ps=ps: e.tensor_copy(out=vtok[:, mc, half * 512:(half + 1) * 512], in_=ps[:, :512]),
                     reads=[Bps], writes=[Bv])
        xs = [(K.sb(st, [128, KC, TT], F32), S.buf()) for _ in range(2)]
        sq = K.sb(st, [128, KC, TT], BF16); Bsq = S.buf()
        r = K.sb(st, [128, TT], F32); Br = S.buf()
        xns = [(K.sb(st, [128, KC, TT], BF16), S.buf()) for _ in range(2)]
        qT = K.sb(st, [128, KC, TT], BF16); BqT = S.buf()
        pT = [(K.sb(st, [128, 2, TT], BF16), S.buf()) for _ in range(2)]
        rd = [(K.sb(st, [128, TT], F32), S.buf()) for _ in range(2)]
        oT = K.sb(st, [128, KC, TT], BF16); BoT = S.buf()
        ntile = nt // TT
        scale = 256 ** -0.5

        def load(i):
            x, Bx = xs[i % 2]
            S.dma("sp", lambda e: e.dma_start(out=x[:], in_=xt_tile(XT, i * TT, TT)), Bx, writes=[Bx])
        load(0)
        rmsnorm_fm(K, xs[0][0], xs[0][1], g, Bg, sq, Bsq, r, Br, xns[0][0], xns[0][1], TT)
        for i in range(ntile):
            x, Bx = xs[i % 2]
            xn, Bxn = xns[i % 2]
            if i + 1 < ntile:
                load(i + 1)
                rmsnorm_a(K, xs[(i + 1) % 2][0], xs[(i + 1) % 2][1], sq, Bsq, TT)

            def ev_q(j, ps, Bps):
                S.op("act", lambda e: e.activation(out=qT[:, j, :], in_=ps[:, :TT], func=AF.Copy), reads=[Bps], writes=[BqT])
            linear_fm(K, wq_s, Bwq, xn, Bxn, TT, range(KC), ev_q)
            if i + 1 < ntile:
                rmsnorm_b(K, xs[(i + 1) % 2][0], xs[(i + 1) % 2][1], g, Bg, sq, Bsq, r, Br, xns[(i + 1) % 2][0], xns[(i + 1) % 2][1], TT)
            for h in range(HC):
                p, Bp = pT[h % 2]
                rdt, Brd = rd[h % 2]
                for mc in range(2):
                    ps, Bps = K.ps()
                    for dc in range(2):
                        S.op("pe", lambda e, h=h, mc=mc, dc=dc, ps=ps: e.matmul(
                            ps[:, :TT], kT[:, 2 * h + dc, mc * 128:(mc + 1) * 128], qT[:, 2 * h + dc, :],
                            start=(dc == 0), stop=(dc == 1)), reads=[BkT, BqT], writes=[Bps], inc=(dc == 1))
                    S.op("act", lambda e, mc=mc, ps=ps, p=p: e.activation(out=p[:, mc, :], in_=ps[:, :TT], func=AF.Exp, scale=scale),
                         reads=[Bps], writes=[Bp])
                psd, Bpsd = K.ps()
                for mc in range(2):
                    S.op("pe", lambda e, mc=mc, psd=psd, p=p: e.matmul(psd[:, :TT], K.ones1[:], p[:, mc, :], start=(mc == 0), stop=(mc == 1)),
                         reads=[Bp, K.Bc], writes=[Bpsd], inc=(mc == 1))
                S.op("act", lambda e, psd=psd, rdt=rdt: e.activation(out=rdt[:], in_=psd[:, :TT], func=AF.Ln), reads=[Bpsd], writes=[Brd])
                S.op("act", lambda e, rdt=rdt: e.activation(out=rdt[:], in_=rdt[:], func=AF.Exp, scale=-1.0), reads=[Brd], writes=[Brd])
                for dv in range(2):
                    pso, Bpso = K.ps()
                    for mc in range(2):
                        S.op("pe", lambda e, h=h, mc=mc, dv=dv, pso=pso, p=p: e.matmul(
                            pso[:, :TT], vtok[:, mc, (2 * h + dv) * 128:(2 * h + dv + 1) * 128], p[:, mc, :],
                            start=(mc == 0), stop=(mc == 1)), reads=[Bv, Bp], writes=[Bpso], inc=(mc == 1))
                    S.op("dve", lambda e, h=h, dv=dv, pso=pso, rdt=rdt: e.tensor_tensor(out=oT[:, 2 * h + dv, :], in0=pso[:, :TT], in1=rdt[:], op=ALU.mult),
                         reads=[Bpso, Brd], writes=[BoT])

            def ev_o(j, ps, Bps):
                S.op("dve", lambda e: e.tensor_tensor(out=x[:, j, :], in0=ps[:, :TT], in1=x[:, j, :], op=ALU.add), reads=[Bps, Bx], writes=[Bx])
            linear_fm(K, wo_s, Bwo, oT, BoT, TT, range(KC), ev_o)
            S.dma("sp", lambda e, x=x, i=i: e.dma_start(out=xt_tile(XO, i * TT, TT), in_=x[:]), Bx, reads=[Bx])
        S.barrier()


def stage_ffn(K, XT, XO, XN, gx, wgu, wdn, half, nt, nxt=None):
    S = K.S
    NJ = FFH // 128
    with ExitStack() as st:
        wg_s = K.sb(st, [128, KC, FFH], BF16); Bwg = S.buf()
        wu_s = K.sb(st, [128, KC, FFH], BF16); Bwu = S.buf()
        wd_s = K.sb(st, [128, NJ, D], BF16); Bwd = S.buf()
        g = K.sb(st, [128, KC], F32); Bg = S.buf()
        load_vec(K, g, Bg, gx)
        load_w(K, wg_s, Bwg, wgu[:, half * FFH: (half + 1) * FFH])
        load_w(K, wu_s, Bwu, wgu[:, DFF + half * FFH: DFF + (half + 1) * FFH])
        load_w(K, wd_s, Bwd, wdn[half * FFH:(half + 1) * FFH, :])
        xs = [(K.sb(st, [128, KC, TT], F32), S.buf()) for _ in range(2)]
        xns = [(K.sb(st, [128, KC, TT], BF16), S.buf()) for _ in range(2)]
        sq = K.sb(st, [128, KC, TT], BF16); Bsq = S.buf()
        r = K.sb(st, [128, TT], F32); Br = S.buf()
        sg = [(K.sb(st, [128, TT], F32), S.buf()) for _ in range(2)]
        hT = K.sb(st, [128, NJ, TT], BF16); BhT = S.buf()
        if nxt is not None:
            gn_ap, OUTN, out_f32 = nxt
            gn = K.sb(st, [128, KC], F32); Bgn = S.buf()
            load_vec(K, gn, Bgn, gn_ap)
            on = [(K.sb(st, [128, KC, TT], F32 if out_f32 else BF16), S.buf()) for _ in range(2)]
        ntile = nt // TT

        def load(i):
            x, Bx = xs[i % 2]
            S.dma("sp", lambda e: e.dma_start(out=x[:], in_=xt_tile(XT, i * TT, TT)), Bx, writes=[Bx])
            if half == 1:
                xn, Bxn = xns[i % 2]
                S.dma("sp", lambda e: e.dma_start(out=xn[:], in_=xt_tile(XN, i * TT, TT)), Bxn, writes=[Bxn])

        def norm_a(i):
            rmsnorm_a(K, xs[i % 2][0], xs[i % 2][1], sq, Bsq, TT)

        def norm_b(i):
            xn, Bxn = xns[i % 2]
            rmsnorm_b(K, xs[i % 2][0], xs[i % 2][1], g, Bg, sq, Bsq, r, Br, xn, Bxn, TT)
            S.dma("sp", lambda e: e.dma_start(out=xt_tile(XN, i * TT, TT), in_=xn[:]), Bxn, reads=[Bxn])
        load(0)
        pend = []
        if half == 0:
            norm_a(0)
            norm_b(0)
        for i in range(ntile):
            x, Bx = xs[i % 2]
            xn, Bxn = xns[i % 2]
            if i + 1 < ntile and nxt is None:
                load(i + 1)
                if half == 0:
                    norm_a(i + 1)
            for j in range(NJ):
                psg, Bpsg = K.ps()
                for k in range(KC):
                    S.op("pe", lambda e: e.matmul(psg[:, :TT], wg_s[:, k, j * 128:(j + 1) * 128], xn[:, k, :], start=(k == 0), stop=(k == KC - 1)),
                         reads=[Bwg, Bxn], writes=[Bpsg], inc=(k == KC - 1))
                psu, Bpsu = K.ps()
                for k in range(KC):
                    S.op("pe", lambda e: e.matmul(psu[:, :TT], wu_s[:, k, j * 128:(j + 1) * 128], xn[:, k, :], start=(k == 0), stop=(k == KC - 1)),
                         reads=[Bwu, Bxn], writes=[Bpsu], inc=(k == KC - 1))
                s_, Bs_ = sg[j % 2]
                S.op("act", lambda e: e.activation(out=s_[:], in_=psg[:, :TT], func=AF.Silu), reads=[Bpsg], writes=[Bs_])
                S.op("dve", lambda e: e.tensor_tensor(out=hT[:, j, :], in0=psu[:, :TT], in1=s_[:], op=ALU.mult), reads=[Bpsu, Bs_], writes=[BhT])
                if j == NJ // 2 and half == 0 and i + 1 < ntile:
                    norm_b(i + 1)
                if j == 1:
                    while pend:
                        pend.pop(0)()
                    if i + 1 < ntile and nxt is not None:
                        load(i + 1)
                        if half == 0:
                            norm_a(i + 1)

            def ev_d(j, ps, Bps):
                S.op("dve", lambda e: e.tensor_tensor(out=x[:, j, :], in0=ps[:, :TT], in1=x[:, j, :], op=ALU.add), reads=[Bps, Bx], writes=[Bx])
            linear_fm(K, wd_s, Bwd, hT, BhT, TT, range(KC), ev_d, kc=NJ)
            S.dma("sp", lambda e: e.dma_start(out=xt_tile(XO, i * TT, TT), in_=x[:]), Bx, reads=[Bx])
            if nxt is not None:
                rmsnorm_a(K, x, Bx, sq, Bsq, TT)

                def post(i=i, x=x, Bx=Bx):
                    o, Bo = on[i % 2]
                    rmsnorm_b(K, x, Bx, gn, Bgn, sq, Bsq, r, Br, o, Bo, TT)
                    S.dma("sp", lambda e: e.dma_start(out=xt_tile(OUTN, i * TT, TT), in_=o[:]), Bo, reads=[Bo])
                pend.append(post)
        while pend:
            pend.pop(0)()
        S.barrier()


def stage_da_in(K, XNF, wq, wk, wv, QT, KT, V, seq, nt_half, qscale=0.125):
    S = K.S
    HW = NH * 128
    with ExitStack() as st:
        wq_s = K.sb(st, [128, KC, HW], BF16); Bwq = S.buf()
        wk_s = K.sb(st, [128, KC, HW], BF16); Bwk = S.buf()
        wv_s = K.sb(st, [128, KC, HW], BF16); Bwv = S.buf()
        load_w(K, wq_s, Bwq, wq)
        load_w(K, wk_s, Bwk, wk)
        load_w(K, wv_s, Bwv, wv)
        xns = [(K.sb(st, [128, KC, TT], BF16), S.buf()) for _ in range(2)]
        qst = [(K.sb(st, [128, NH, TT], BF16), S.buf()) for _ in range(2)]
        kst = [(K.sb(st, [128, NH, TT], BF16), S.buf()) for _ in range(2)]
        vst = [(K.sb(st, [128, NH, TT // 128, 128], BF16), S.buf()) for _ in range(2)]
        ntile = seq // TT
        per = nt_half // TT

        def load(i):
            xn, Bxn = xns[i % 2]
            S.dma("sp", lambda e: e.dma_start(out=xn[:], in_=xt_tile(XNF[i // per], (i % per) * TT, TT)), Bxn, writes=[Bxn])
        load(0)
        for i in range(ntile):
            xn, Bxn = xns[i % 2]
            q_, Bq_ = qst[i % 2]; k_, Bk_ = kst[i % 2]; v_, Bv_ = vst[i % 2]
            if i + 1 < ntile:
                load(i + 1)

            def ev_q(j, ps, Bps):
                S.op("act", lambda e: e.mul(out=q_[:, j, :], in_=ps[:, :TT], mul=qscale), reads=[Bps], writes=[Bq_])
            linear_fm(K, wq_s, Bwq, xn, Bxn, TT, range(NH), ev_q)

            def ev_k(j, ps, Bps):
                S.op("dve", lambda e: e.tensor_copy(out=k_[:, j, :], in_=ps[:, :TT]), reads=[Bps], writes=[Bk_])
            linear_fm(K, wk_s, Bwk, xn, Bxn, TT, range(NH), ev_k)
            for sub in range(TT // 128):
                ps, Bps = K.ps()
                for k in range(KC):
                    S.op("pe", lambda e, k=k: e.matmul(ps[:, :HW], xn[:, k, sub * 128:(sub + 1) * 128], wv_s[:, k, :],
                                                       start=(k == 0), stop=(k == KC - 1)),
                         reads=[Bxn, Bwv], writes=[Bps], inc=(k == KC - 1))
                psv = ps[:, :HW].rearrange("p (h d) -> p h d", h=NH)
                if sub % 2 == 0:
                    S.op("act", lambda e: e.copy(out=v_[:, :, sub, :], in_=psv), reads=[Bps], writes=[Bv_])
                else:
                    S.op("dve", lambda e: e.tensor_copy(out=v_[:, :, sub, :], in_=psv), reads=[Bps], writes=[Bv_])
            s0 = i * TT
            S.dma("sp", lambda e: e.dma_start(out=QT[:, :, s0:s0 + TT].rearrange("h p t -> p h t"), in_=q_[:]), Bq_, reads=[Bq_])
            S.dma("sp", lambda e: e.dma_start(out=KT[:, :, s0:s0 + TT].rearrange("h p t -> p h t"), in_=k_[:]), Bk_, reads=[Bk_])
            j0 = s0 // 128
            S.dma("sp", lambda e: e.dma_start(out=V[:, :, j0:j0 + TT // 128, :].rearrange("h p j d -> p h j d"), in_=v_[:]), Bv_, reads=[Bv_])
        S.barrier()


def stage_da_core(K, QT, KT, V, GB, CM, misc, OT2, seq, lam_init, half=None):
    S = K.S
    NCH = seq // 128
    with ExitStack() as st:
        ms = K.sb(st, [128, 4 * 64 + NH + 1], F32); Bms = S.buf()
        cm = K.sb(st, [128, 1024], F32); Bcm = S.buf()
        load_vec(K, ms, Bms, misc)
        load_vec(K, cm, Bcm, CM)
        sc = K.sb(st, [128, 8], F32); Bsc = S.buf()
        tmp64 = K.sb(st, [128, 64], F32); Bt64 = S.buf()
        for c in range(2):
            S.op("dve", lambda e: e.tensor_tensor(out=tmp64[:], in0=ms[:, (2 * c) * 64:(2 * c + 1) * 64],
                                                  in1=ms[:, (2 * c + 1) * 64:(2 * c + 2) * 64], op=ALU.mult), reads=[Bms], writes=[Bt64])
            S.op("dve", lambda e: e.reduce_sum(out=sc[:, c:c + 1], in_=tmp64[:], axis=AX.X), reads=[Bt64], writes=[Bsc])
        S.op("act", lambda e: e.activation(out=sc[:, 0:2], in_=sc[:, 0:2], func=AF.Exp), reads=[Bsc], writes=[Bsc])
        S.op("dve", lambda e: e.scalar_tensor_tensor(out=sc[:, 2:3], in0=sc[:, 1:2], scalar=-lam_init, in1=sc[:, 0:1],
                                                     op0=ALU.add, op1=ALU.subtract), reads=[Bsc], writes=[Bsc])
        gcol = 4 * 64 + NH
        S.op("dve", lambda e: e.tensor_scalar(out=sc[:, 3:4], in0=ms[:, gcol:gcol + 1], scalar1=(1.0 - lam_init), scalar2=None, op0=ALU.mult),
             reads=[Bms, Bsc], writes=[Bsc])
        Mh = []
        gbt = K.sb(st, [128, 1024], F32); Bgbt = S.buf()
        for h in range(NH):
            m = K.sb(st, [128, 1024], F32); Bm = S.buf()
            S.dma("sp", lambda e: e.dma_start(out=gbt[:], in_=GB[h]), Bgbt, writes=[Bgbt])
            S.op("dve", lambda e: e.scalar_tensor_tensor(out=m[:], in0=gbt[:], scalar=ms[:, 4 * 64 + h: 4 * 64 + h + 1], in1=cm[:],
                                                         op0=ALU.subtract, op1=ALU.add), reads=[Bgbt, Bms, Bcm], writes=[Bm])
            Mh.append((m, Bm))
        hb = []
        for _ in range(2):
            hb.append(dict(k=K.sb(st, [128, seq], BF16), Bk=S.buf(), v=K.sb(st, [128, NCH, 128], BF16), Bv=S.buf(),
                           q=K.sb(st, [128, seq], BF16), Bq=S.buf()))
        NPB = 6
        pbuf = [(K.sb(st, [128, 2, TT], BF16), S.buf()) for _ in range(NPB)]
        sums = [dict(a=K.sb(st, [128, 2, TT], BF16), Ba=S.buf(), b=K.sb(st, [128, 2, TT], BF16), Bb=S.buf(),
                     s=K.sb(st, [128, 2, TT], BF16), Bs=S.buf()) for _ in range(2)]
        O1s = K.sb(st, [128, 2, TT], F32); BO1s = S.buf()
        sbuf = [(K.sb(st, [128, 2, TT], F32), S.buf()) for _ in range(2)]
        rr_ = K.sb(st, [128, 2, TT], F32); Brr = S.buf()
        A = K.sb(st, [128, TT], F32); BA = S.buf()
        Bm_ = K.sb(st, [128, TT], F32); BB = S.buf()
        sq = K.sb(st, [128, TT], BF16); Bsq = S.buf()
        rs = K.sb(st, [128, TT], F32); Brs = S.buf()
        ost = [(K.sb(st, [128, TT], BF16), S.buf()) for _ in range(2)]
        (pO1, BO1), (pO2, BO2), (pD1, BD1), (pD2, BD2) = [K.ps_i(i) for i in (4, 5, 6, 7)]
        rr = [0]

        def psr():
            t = K.ps_i(rr[0] % 4)
            rr[0] += 1
            return t

        def loadh(h):
            b = hb[h % 2]
            for piece in range(4):
                c0 = piece * (seq // 4)
                S.dma("sp", lambda e: e.dma_start(out=b["k"][:, c0:c0 + seq // 4], in_=KT[h][:, c0:c0 + seq // 4]), b["Bk"], writes=[b["Bk"]])
            for piece in range(4):
                j0 = piece * (NCH // 4)
                S.dma("sp", lambda e: e.dma_start(out=b["v"][:, j0:j0 + NCH // 4, :], in_=V[h][:, j0:j0 + NCH // 4, :]),
                      b["Bv"], writes=[b["Bv"]])
            for piece in range(4):
                c0 = piece * (seq // 4)
                S.dma("sp", lambda e: e.dma_start(out=b["q"][:, c0:c0 + seq // 4], in_=QT[h][:, c0:c0 + seq // 4]), b["Bq"], writes=[b["Bq"]])
        loadh(0)
        ntile = seq // TT
        scb = [(K.pd[0], K.psum[0][1], K.psum[1][1]), (K.pd[1], K.psum[2][1], K.psum[3][1])]
        hf_ = (seq // 2) if half is None else half
        for h in range(NH):
            b = hb[h % 2]
            if h + 1 < NH:
                loadh(h + 1)
            m, Bm = Mh[h]
            kt, vt, qt = b["k"], b["v"], b["q"]
            pairs = [(t, j) for t in range(ntile) for j in range(4 * (t + 1))]
            N = len(pairs)

            def stA(n):
                t, j = pairs[n]
                q0 = t * TT
                sc, Bs0, Bs1 = scb[n % 2]
                for c, Bps in ((0, Bs0), (1, Bs1)):
                    S.op("pe", lambda e: e.matmul(sc[:, c * 512:(c + 1) * 512], kt[c * 64:(c + 1) * 64, j * 128:(j + 1) * 128],
                                                  qt[c * 64:(c + 1) * 64, q0:q0 + TT], start=True, stop=True),
                         reads=[b["Bk"], b["Bq"]], writes=[Bps], inc=(c == 1))

            def stB(n):
                t, j = pairs[n]
                q0 = t * TT
                sc, Bs0, Bs1 = scb[n % 2]
                p, Bp = pbuf[n % NPB]
                sc3 = sc[:, :].rearrange("p (c t) -> p c t", c=2)
                if j >= 4 * t - 1:
                    c0 = 384 - (128 * j - q0)
                    s_, Bs_ = sbuf[n % 2]
                    for c, Bps in ((0, Bs0), (1, Bs1)):
                        S.op("dve", lambda e: e.tensor_tensor(out=s_[:, c, :], in0=sc[:, c * 512:(c + 1) * 512], in1=m[:, c0:c0 + TT], op=ALU.add),
                             reads=[Bps, Bm], writes=[Bs_])
                    S.op("act", lambda e: e.activation(out=p[:], in_=s_[:], func=AF.Exp), reads=[Bs_], writes=[Bp])
                else:
                    S.op("act", lambda e: e.activation(out=p[:], in_=sc3, func=AF.Exp), reads=[Bs0, Bs1], writes=[Bp])

            pending = []

            def stC(n):
                t, j = pairs[n]
                nch = 4 * (t + 1)
                p, Bp = pbuf[n % NPB]
                first, last = (j == 0), (j == nch - 1)
                pO = K.pd[2]
                BOa, BOb = K.psum[4][1], K.psum[5][1]
                S.op("pe", lambda e: e.matmul(pO[:, 0:TT], vt[:, j, :], p[:, 0, :], start=first, stop=last), reads=[b["Bv"], Bp], writes=[BOa], inc=False)
                S.op("pe", lambda e: e.matmul(pO[:, 512:512 + TT], vt[:, j, :], p[:, 1, :], start=first, stop=last), reads=[b["Bv"], Bp], writes=[BOb])
                while pending:
                    pending.pop(0)()
                g = (j // 4)
                sm = sums[g % 2]
                jj = j % 4
                if jj == 1 or jj == 3:
                    pp, Bpp = pbuf[(n - 1) % NPB]
                    dst, Bdst = (sm["a"], sm["Ba"]) if jj == 1 else (sm["b"], sm["Bb"])
                    S.op("dve", lambda e: e.tensor_tensor(out=dst[:], in0=pp[:], in1=p[:], op=ALU.add), reads=[Bpp, Bp], writes=[Bdst])
                if jj == 3:
                    S.op("dve", lambda e: e.tensor_tensor(out=sm["s"][:], in0=sm["a"][:], in1=sm["b"][:], op=ALU.add),
                         reads=[sm["Ba"], sm["Bb"]], writes=[sm["Bs"]])
                    pD = K.pd[3]
                    BDa, BDb = K.psum[6][1], K.psum[7][1]
                    gfirst, glast = (j == 3), last

                    def den():
                        S.op("pe", lambda e: e.matmul(pD[:, 0:TT], K.ones1[:], sm["s"][:, 0, :], start=gfirst, stop=glast), reads=[K.Bc, sm["Bs"]], writes=[BDa], inc=False)
                        S.op("pe", lambda e: e.matmul(pD[:, 512:512 + TT], K.ones1[:], sm["s"][:, 1, :], start=gfirst, stop=glast), reads=[K.Bc, sm["Bs"]], writes=[BDb])
                    if last:
                        den()
                    else:
                        pending.append(den)
                if last:
                    epilogue(t, n)

            def epilogue(t, n):
                q0 = t * TT
                pO = K.pd[2]
                BOa, BOb = K.psum[4][1], K.psum[5][1]
                pD = K.pd[3]
                BDa, BDb = K.psum[6][1], K.psum[7][1]
                S.op("dve", lambda e: e.tensor_copy(out=O1s[:], in_=pO[:, :].rearrange("p (c t) -> p c t", c=2)), reads=[BOa, BOb], writes=[BO1s])
                S.op("act", lambda e: e.activation(out=rr_[:], in_=pD[:, :].rearrange("p (c t) -> p c t", c=2), func=AF.Ln), reads=[BDa, BDb], writes=[Brr])
                S.op("act", lambda e: e.activation(out=rr_[:], in_=rr_[:], func=AF.Exp, scale=-1.0), reads=[Brr], writes=[Brr])
                S.op("dve", lambda e: e.tensor_tensor(out=O1s[:], in0=O1s[:], in1=rr_[:], op=ALU.mult), reads=[BO1s, Brr], writes=[BO1s])
                S.op("dve", lambda e: e.scalar_tensor_tensor(out=A[:], in0=O1s[:, 1, :], scalar=sc[:, 2:3], in1=O1s[:, 0, :], op0=ALU.mult, op1=ALU.add),
                     reads=[BO1s, Bsc], writes=[BA])
                S.op("dve", lambda e: e.tensor_tensor(out=sq[:], in0=A[:], in1=A[:], op=ALU.mult), reads=[BA], writes=[Bsq])
                S.op("pe", lambda e: e.matmul(pD[:, 0:TT], K.ones128[:], sq[:], start=True, stop=True), reads=[K.Bc, Bsq], writes=[BDa])
                S.op("act", lambda e: e.activation(out=rs[:], in_=pD[:, 0:TT], func=AF.Ln, bias=K.epsT[:], scale=1.0), reads=[BDa, K.Bc], writes=[Brs])
                S.op("act", lambda e: e.activation(out=rs[:], in_=rs[:], func=AF.Exp, scale=-0.5), reads=[Brs], writes=[Brs])
                o_, Bo_ = ost[t % 2]
                S.op("dve", lambda e: e.scalar_tensor_tensor(out=o_[:], in0=A[:], scalar=sc[:, 3:4], in1=rs[:], op0=ALU.mult, op1=ALU.mult),
                     reads=[BA, Bsc, Brs], writes=[Bo_])
                S.dma("sp", lambda e: e.dma_start(out=OT2[q0 // hf_, h, :, (q0 % hf_):(q0 % hf_) + TT], in_=o_[:]), Bo_, reads=[Bo_])

            stA(0)
            stB(0)
            for n in range(N):
                if n + 1 < N:
                    stA(n + 1)
                    stB(n + 1)
                stC(n)
        S.barrier()


def stage_mix_out(K, XT, XO, OG, wout, nt):
    S = K.S
    with ExitStack() as st:
        w_s = K.sb(st, [128, KC, D], BF16); Bw = S.buf()
        load_w(K, w_s, Bw, wout)
        xs = [(K.sb(st, [128, KC, TT], F32), S.buf()) for _ in range(2)]
        os_ = [(K.sb(st, [128, KC, TT], BF16), S.buf()) for _ in range(2)]
        ntile = nt // TT

        def load(i):
            x, Bx = xs[i % 2]
            o, Bo = os_[i % 2]
            S.dma("sp", lambda e: e.dma_start(out=x[:], in_=xt_tile(XT, i * TT, TT)), Bx, writes=[Bx])
            for r in range(2):
                S.dma("sp", lambda e: e.dma_start(out=o[:, r * NH:(r + 1) * NH, :], in_=xt_tile(OG[r], i * TT, TT)), Bo, writes=[Bo])
        load(0)
        for i in range(ntile):
            x, Bx = xs[i % 2]
            o, Bo = os_[i % 2]
            if i + 1 < ntile:
                load(i + 1)

            def ev(j, ps, Bps):
                S.op("dve", lambda e: e.tensor_tensor(out=x[:, j, :], in0=ps[:, :TT], in1=x[:, j, :], op=ALU.add), reads=[Bps, Bx], writes=[Bx])
            linear_fm(K, w_s, Bw, o, Bo, TT, range(KC), ev)
            S.dma("sp", lambda e: e.dma_start(out=xt_tile(XO, i * TT, TT), in_=x[:]), Bx, reads=[Bx])
        S.barrier()


def stage_hg_in(K, XNF, wq, wf, wi, wg, oml, SQ, KKd, V, SG, seq, nt_half, layer_idx=1):
    S = K.S
    HW = NH * 128
    with ExitStack() as st:
        ws = []
        for w in (wq, wf, wi, wg):
            t = K.sb(st, [128, KC, HW], BF16); B = S.buf()
            load_w(K, t, B, w)
            ws.append((t, B))
        (wq_s, Bwq), (wf_s, Bwf), (wi_s, Bwi), (wg_s, Bwg) = ws
        nl = oml.shape[2]
        lraw = K.sb(st, [128, NH, nl], F32); Blraw = S.buf()
        load_vec(K, lraw, Blraw, oml)
        S.op("act", lambda e: e.activation(out=lraw[:], in_=lraw[:], func=AF.Exp), reads=[Blraw], writes=[Blraw])
        tot = K.sb(st, [128, NH], F32); Btot = S.buf()
        num = K.sb(st, [128, NH], F32); Bnum = S.buf()
        om = K.sb(st, [128, NH], F32); Bom = S.buf()
        S.op("dve", lambda e: e.reduce_sum(out=tot[:], in_=lraw[:], axis=AX.X), reads=[Blraw], writes=[Btot])
        S.op("dve", lambda e: e.reduce_sum(out=num[:], in_=lraw[:, :, 1:layer_idx + 1], axis=AX.X), reads=[Blraw], writes=[Bnum])
        S.op("dve", lambda e: e.reciprocal(out=tot[:], in_=tot[:]), reads=[Btot], writes=[Btot])
        S.op("dve", lambda e: e.tensor_tensor(out=num[:], in0=num[:], in1=tot[:], op=ALU.mult), reads=[Bnum, Btot], writes=[Bnum])
        S.op("dve", lambda e: e.tensor_scalar(out=om[:], in0=num[:], scalar1=-1.0, scalar2=1.0, op0=ALU.mult, op1=ALU.add), reads=[Bnum], writes=[Bom])
        xns = [(K.sb(st, [128, KC, TT], BF16), S.buf()) for _ in range(2)]
        qst = [(K.sb(st, [128, NH, TT], BF16), S.buf()) for _ in range(2)]
        gst = [(K.sb(st, [128, NH, TT], BF16), S.buf()) for _ in range(2)]
        kst = [(K.sb(st, [128, NH, TT], F32), S.buf()) for _ in range(2)]
        vst = [(K.sb(st, [128, NH, TT // 128, 128], BF16), S.buf()) for _ in range(2)]
        ntile = seq // TT
        per = nt_half // TT

        def load(i):
            xn, Bxn = xns[i % 2]
            S.dma("sp", lambda e: e.dma_start(out=xn[:], in_=xt_tile(XNF[i // per], (i % per) * TT, TT)), Bxn, writes=[Bxn])
        load(0)
        for i in range(ntile):
            xn, Bxn = xns[i % 2]
            q_, Bq_ = qst[i % 2]; g_, Bg_ = gst[i % 2]; k_, Bk_ = kst[i % 2]; v_, Bv_ = vst[i % 2]
            if i + 1 < ntile:
                load(i + 1)

            def ev_q(j, ps, Bps):
                S.op("act", lambda e: e.activation(out=q_[:, j, :], in_=ps[:, :TT], func=AF.Silu), reads=[Bps], writes=[Bq_])
            linear_fm(K, wq_s, Bwq, xn, Bxn, TT, range(NH), ev_q)

            def ev_g(j, ps, Bps):
                S.op("act", lambda e: e.activation(out=g_[:, j, :], in_=ps[:, :TT], func=AF.Silu), reads=[Bps], writes=[Bg_])
            linear_fm(K, wg_s, Bwg, xn, Bxn, TT, range(NH), ev_g)

            def ev_f(j, ps, Bps):
                S.op("act", lambda e: e.activation(out=k_[:, j, :], in_=ps[:, :TT], func=AF.Sigmoid, scale=-1.0), reads=[Bps], writes=[Bk_])
                S.op("dve", lambda e: e.tensor_scalar(out=k_[:, j, :], in0=k_[:, j, :], scalar1=om[:, j:j + 1], scalar2=None, op0=ALU.mult),
                     reads=[Bk_, Bom], writes=[Bk_])
            linear_fm(K, wf_s, Bwf, xn, Bxn, TT, range(NH), ev_f)
            for sub in range(TT // 128):
                ps, Bps = K.ps()
                for k in range(KC):
                    S.op("pe", lambda e, k=k: e.matmul(ps[:, :HW], xn[:, k, sub * 128:(sub + 1) * 128], wi_s[:, k, :],
                                                       start=(k == 0), stop=(k == KC - 1)),
                         reads=[Bxn, Bwi], writes=[Bps], inc=(k == KC - 1))
                psv = ps[:, :HW].rearrange("p (h d) -> p h d", h=NH)
                S.op("dve", lambda e: e.tensor_copy(out=v_[:, :, sub, :], in_=psv), reads=[Bps], writes=[Bv_])
            s0 = i * TT
            j0 = s0 // 128
            S.dma("sp", lambda e: e.dma_start(out=SQ[:, :, s0:s0 + TT].rearrange("h p t -> p h t"), in_=q_[:]), Bq_, reads=[Bq_])
            S.dma("sp", lambda e: e.dma_start(out=SG[:, :, s0:s0 + TT].rearrange("h p t -> p h t"), in_=g_[:]), Bg_, reads=[Bg_])
            S.dma("sp", lambda e: e.dma_start(out=KKd[:, :, s0:s0 + TT].rearrange("h p t -> p h t"), in_=k_[:]), Bk_, reads=[Bk_])
            S.dma("sp", lambda e: e.dma_start(out=V[:, :, j0:j0 + TT // 128, :].rearrange("h p j d -> p h j d"), in_=v_[:]), Bv_, reads=[Bv_])
        S.barrier()


def stage_hg_core(K, SQ, KKd, V, SG, gon, OT2, seq, half=None):
    S = K.S
    C = 64
    NCB = TT // C
    with ExitStack() as st:
        go = K.sb(st, [128, 1], F32); Bgo = S.buf()
        load_vec(K, go, Bgo, gon)
        mle = K.sb(st, [64, 64], F32); Bmle = S.buf()
        S.op("dve", lambda e: e.tensor_copy(out=mle[:], in_=K.cf[0:64, 0:64]), reads=[K.Bcf], writes=[Bmle])
        hd = []
        for h in range(NH):
            d = dict(Sf=K.sb(st, [128, 128], F32), BSf=S.buf(), Sb=K.sb(st, [128, 128], BF16), BSb=S.buf())
            S.op("dve", lambda e: e.memset(d["Sf"][:], 0.0), writes=[d["BSf"]])
            S.op("dve", lambda e: e.memset(d["Sb"][:], 0.0), writes=[d["BSb"]])
            for nm, shp, dt in (("sq", [128, TT], BF16), ("kk", [128, TT], F32), ("sg", [128, TT], BF16),
                                ("v", [64, NCB, 128], BF16), ("lf", [128, TT], F32), ("G", [128, TT], F32),
                                ("eG", [128, TT], F32), ("enG", [128, TT], F32), ("Qt", [128, TT], BF16),
                                ("Kt", [128, TT], BF16), ("Kh", [128, TT], BF16), ("KhT", [64, NCB, 128], BF16),
                                ("AT", [64, 2, 64], BF16), ("osq", [128, TT], BF16), ("rs", [128, TT], F32),
                                ("of", [128, TT], F32), ("ob", [128, TT], BF16)):
                d[nm] = K.sb(st, shp, dt)
                d["B" + nm] = S.buf()
            hd.append(d)
        nblk = seq // TT
        half = (seq // 2) if half is None else half
        rr = [0]

        def psr():
            t = K.ps_i(rr[0] % 4)
            rr[0] += 1
            return t
        for blk in range(nblk):
            s0 = blk * TT
            for h in range(NH):
                d = hd[h]
                S.dma("sp", lambda e: e.dma_start(out=d["sq"][:], in_=SQ[h][:, s0:s0 + TT]), d["Bsq"], writes=[d["Bsq"]])
                S.dma("sp", lambda e: e.dma_start(out=d["kk"][:], in_=KKd[h][:, s0:s0 + TT]), d["Bkk"], writes=[d["Bkk"]])
                S.dma("sp", lambda e: e.dma_start(out=d["sg"][:], in_=SG[h][:, s0:s0 + TT]), d["Bsg"], writes=[d["Bsg"]])
                for hh in range(2):
                    S.dma("sp", lambda e: e.dma_start(out=d["v"][:, hh::2, :], in_=V[h][hh * 64:(hh + 1) * 64, s0 // 128: s0 // 128 + TT // 128, :]),
                          d["Bv"], writes=[d["Bv"]])
            for h in range(NH):
                d = hd[h]
                S.op("act", lambda e: e.activation(out=d["lf"][:], in_=d["kk"][:], func=AF.Ln, scale=-1.0, bias=K.oneT[:]), reads=[d["Bkk"], K.Bc], writes=[d["Blf"]])
                S.op("dve", lambda e: e.tensor_tensor_scan(out=d["G"][:], data0=K.scanmask, data1=d["lf"][:], initial=0.0, op0=ALU.mult, op1=ALU.add),
                     reads=[d["Blf"], K.Bcf], writes=[d["BG"]])
                S.op("act", lambda e: e.activation(out=d["eG"][:], in_=d["G"][:], func=AF.Exp), reads=[d["BG"]], writes=[d["BeG"]])
                S.op("act", lambda e: e.activation(out=d["enG"][:], in_=d["G"][:], func=AF.Exp, scale=-1.0), reads=[d["BG"]], writes=[d["BenG"]])
                S.op("dve", lambda e: e.tensor_tensor(out=d["Qt"][:], in0=d["sq"][:], in1=d["eG"][:], op=ALU.mult), reads=[d["Bsq"], d["BeG"]], writes=[d["BQt"]])
                S.op("dve", lambda e: e.tensor_tensor(out=d["Kt"][:], in0=d["kk"][:], in1=d["enG"][:], op=ALU.mult), reads=[d["Bkk"], d["BenG"]], writes=[d["BKt"]])
                for c in range(NCB):
                    S.op("dve", lambda e: e.tensor_scalar(out=d["Kh"][:, c * C:(c + 1) * C], in0=d["Kt"][:, c * C:(c + 1) * C],
                                                          scalar1=d["eG"][:, (c + 1) * C - 1:(c + 1) * C], scalar2=None, op0=ALU.mult),
                         reads=[d["BKt"], d["BeG"]], writes=[d["BKh"]])
                for c in range(NCB):
                    idx = rr[0] % 4
                    ps, Bps = psr()
                    pst = K.pd[idx // 2].bitcast(BF16)
                    cb = (idx % 2) * 1024
                    S.op("pe", lambda e: e.transpose(pst[0:64, cb:cb + 128], d["Kh"][:, c * C:(c + 1) * C], K.ident[:]), reads=[d["BKh"], K.Bc], writes=[Bps])
                    S.op("act", lambda e: e.copy(out=d["KhT"][:, c, :], in_=pst[0:64, cb:cb + 128]), reads=[Bps], writes=[d["BKhT"]])
            for c in range(NCB):
                for h in range(NH):
                    d = hd[h]
                    po, Bpo = K.ps_i(4 + h)
                    cs = slice(c * C, (c + 1) * C)
                    ps, Bps = psr()
                    S.op("pe", lambda e: e.matmul(ps[0:64, 0:64], d["Kt"][:, cs], d["Qt"][:, cs], start=True, stop=True),
                         reads=[d["BKt"], d["BQt"]], writes=[Bps])
                    S.op("dve", lambda e: e.tensor_tensor(out=d["AT"][:, c % 2, :], in0=ps[0:64, 0:64], in1=mle[:], op=ALU.mult),
                         reads=[Bps, Bmle], writes=[d["BAT"]])
                    S.op("pe", lambda e: e.matmul(po[:, cs], d["Sb"][:], d["Qt"][:, cs], start=True, stop=False),
                         reads=[d["BSb"], d["BQt"]], writes=[Bpo], inc=False)
                    S.op("pe", lambda e: e.matmul(po[:, cs], d["v"][:, c, :], d["AT"][:, c % 2, :], start=False, stop=True),
                         reads=[d["Bv"], d["BAT"]], writes=[Bpo])
                    ps2, Bps2 = psr()
                    S.op("pe", lambda e: e.matmul(ps2[:, 0:128], d["KhT"][:, c, :], d["v"][:, c, :], start=True, stop=True),
                         reads=[d["BKhT"], d["Bv"]], writes=[Bps2])
                    S.op("dve", lambda e: e.scalar_tensor_tensor(out=d["Sf"][:], in0=d["Sf"][:], scalar=d["eG"][:, (c + 1) * C - 1:(c + 1) * C],
                                                                 in1=ps2[:, 0:128], op0=ALU.mult, op1=ALU.add),
                         reads=[d["BSf"], d["BeG"], Bps2], writes=[d["BSf"]])
                    S.op("act", lambda e: e.copy(out=d["Sb"][:], in_=d["Sf"][:]), reads=[d["BSf"]], writes=[d["BSb"]])
            for h in range(NH):
                d = hd[h]
                po, Bpo = K.ps_i(4 + h)
                S.op("act", lambda e: e.activation(out=d["osq"][:], in_=po[:, :TT], func=AF.Square), reads=[Bpo], writes=[d["Bosq"]])
                ps, Bps = psr()
                S.op("pe", lambda e: e.matmul(ps[:, :TT], K.ones128[:], d["osq"][:], start=True, stop=True), reads=[K.Bc, d["Bosq"]], writes=[Bps])
                S.op("act", lambda e: e.activation(out=d["rs"][:], in_=ps[:, :TT], func=AF.Sqrt, bias=K.epsT[:], scale=1.0), reads=[Bps, K.Bc], writes=[d["Brs"]])
                S.op("dve", lambda e: e.reciprocal(out=d["rs"][:], in_=d["rs"][:]), reads=[d["Brs"]], writes=[d["Brs"]])
                S.op("dve", lambda e: e.scalar_tensor_tensor(out=d["of"][:], in0=po[:, :TT], scalar=go[:, 0:1], in1=d["rs"][:], op0=ALU.mult, op1=ALU.mult),
                     reads=[Bpo, Bgo, d["Brs"]], writes=[d["Bof"]])
                S.op("dve", lambda e: e.tensor_tensor(out=d["ob"][:], in0=d["of"][:], in1=d["sg"][:], op=ALU.mult), reads=[d["Bof"], d["Bsg"]], writes=[d["Bob"]])
                S.dma("sp", lambda e: e.dma_start(out=OT2[s0 // half, h, :, (s0 % half):(s0 % half) + TT], in_=d["ob"][:]), d["Bob"], reads=[d["Bob"]])
        S.barrier()


def stage_sgu(K, XT, XO, gx, wu, wv, gvb, wsT, bs, wout, nt):
    S = K.S
    with ExitStack() as st:
        wu_s = K.sb(st, [128, KC, D], BF16); Bwu = S.buf()
        wv_s = K.sb(st, [128, KC, D], BF16); Bwv = S.buf()
        wo_s = K.sb(st, [128, KC, D], BF16); Bwo = S.buf()
        load_w(K, wu_s, Bwu, wu)
        load_w(K, wv_s, Bwv, wv)
        load_w(K, wo_s, Bwo, wout)
        g = K.sb(st, [128, KC], F32); Bg = S.buf()
        load_vec(K, g, Bg, gx)
        gv = K.sb(st, [128, D], F32); Bgv = S.buf()
        load_vec(K, gv, Bgv, gvb)
        wsf = K.sb(st, [128, 8, 128], F32); Bwsf = S.buf()
        load_vec(K, wsf, Bwsf, wsT)
        wsm = K.sb(st, [128, 8, 128], BF16); Bwsm = S.buf()
        for gi in range(8):
            S.op("dve", lambda e: e.tensor_tensor(out=wsm[:, gi, :], in0=wsf[:, gi, :], in1=K.cf[:, 0:128], op=ALU.mult),
                 reads=[Bwsf, K.Bcf], writes=[Bwsm])
        bsf = K.sb(st, [1, D], F32); Bbsf = S.buf()
        load_vec(K, bsf, Bbsf, bs)
        bsb = K.sb(st, [1, D], BF16); Bbsb = S.buf()
        S.op("dve", lambda e: e.tensor_copy(out=bsb[:], in_=bsf[:]), reads=[Bbsf], writes=[Bbsb])
        xs = [(K.sb(st, [128, KC, TT], F32), S.buf()) for _ in range(2)]
        sq = K.sb(st, [128, KC, TT], BF16); Bsq = S.buf()
        r = K.sb(st, [128, TT], F32); Br = S.buf()
        xn = K.sb(st, [128, KC, TT], BF16); Bxn = S.buf()
        uT = K.sb(st, [128, KC, TT], BF16); BuT = S.buf()
        zT = K.sb(st, [128, KC, TT], BF16); BzT = S.buf()
        vf = [(K.sb(st, [128, D], F32), S.buf()) for _ in range(2)]
        junk = K.sb(st, [128, D], BF16); Bjunk = S.buf()
        ssq = [(K.sb(st, [128, 2], F32), S.buf()) for _ in range(2)]
        vn = [(K.sb(st, [128, D], BF16), S.buf()) for _ in range(2)]
        ntile = nt // TT

        def load(i):
            x, Bx = xs[i % 2]
            S.dma("sp", lambda e: e.dma_start(out=x[:], in_=xt_tile(XT, i * TT, TT)), Bx, writes=[Bx])
        load(0)
        it = 0
        for i in range(ntile):
            x, Bx = xs[i % 2]
            if i + 1 < ntile:
                load(i + 1)
            rmsnorm_fm(K, x, Bx, g, Bg, sq, Bsq, r, Br, xn, Bxn, TT)

            def ev_u(j, ps, Bps):
                S.op("act", lambda e: e.activation(out=uT[:, j, :], in_=ps[:, :TT], func=AF.Gelu), reads=[Bps], writes=[BuT])
            linear_fm(K, wu_s, Bwu, xn, Bxn, TT, range(KC), ev_u)
            for sub in range(TT // 128):
                v_, Bv_ = vf[it % 2]; s2, Bs2 = ssq[it % 2]; n_, Bn_ = vn[it % 2]
                it += 1
                ts = slice(sub * 128, (sub + 1) * 128)
                for hf in range(2):
                    ps, Bps = K.ps()
                    for k in range(KC):
                        S.op("pe", lambda e, k=k: e.matmul(ps[:, :512], xn[:, k, ts], wv_s[:, k, hf * 512:(hf + 1) * 512],
                                                           start=(k == 0), stop=(k == KC - 1)), reads=[Bxn, Bwv], writes=[Bps], inc=(k == KC - 1))
                    S.op("act", lambda e: e.activation(out=v_[:, hf * 512:(hf + 1) * 512], in_=ps[:, :512], func=AF.Gelu), reads=[Bps], writes=[Bv_])
                S.op("act", lambda e: e.activation(out=junk[:], in_=v_[:], func=AF.Square, accum_out=s2[:, 0:1]), reads=[Bv_], writes=[Bjunk, Bs2])
                S.op("dve", lambda e: e.tensor_scalar(out=s2[:, 1:2], in0=s2[:, 0:1], scalar1=1.0 / D, scalar2=EPS, op0=ALU.mult, op1=ALU.add),
                     reads=[Bs2], writes=[Bs2])
                S.op("act", lambda e: e.activation(out=s2[:, 1:2], in_=s2[:, 1:2], func=AF.Sqrt), reads=[Bs2], writes=[Bs2])
                S.op("dve", lambda e: e.reciprocal(out=s2[:, 1:2], in_=s2[:, 1:2]), reads=[Bs2], writes=[Bs2])
                S.op("dve", lambda e: e.scalar_tensor_tensor(out=n_[:], in0=v_[:], scalar=s2[:, 1:2], in1=gv[:], op0=ALU.mult, op1=ALU.mult),
                     reads=[Bv_, Bs2, Bgv], writes=[Bn_])
                for g0 in (0, 4):
                    ps, Bps = K.ps()
                    for gg in range(4):
                        gi = g0 + gg
                        S.op("pe", lambda e: e.matmul(ps[:, gg * 128:(gg + 1) * 128], n_[:, gi * 128:(gi + 1) * 128], wsm[:, gi, :], start=True, stop=False),
                             reads=[Bn_, Bwsm], writes=[Bps], inc=False)
                        S.op("pe", lambda e: e.matmul(ps[:, gg * 128:(gg + 1) * 128], K.onesrow[0:1, :], bsb[0:1, gi * 128:(gi + 1) * 128], start=False, stop=True),
                             reads=[K.Bc, Bbsb], writes=[Bps], inc=(gg == 3))
                    S.op("dve", lambda e: e.tensor_tensor(out=zT[:, g0:g0 + 4, ts], in0=ps[:, :512].rearrange("p (g t) -> p g t", g=4),
                                                          in1=uT[:, g0:g0 + 4, ts], op=ALU.mult), reads=[Bps, BuT], writes=[BzT])

            def ev_o(j, ps, Bps):
                S.op("dve", lambda e: e.tensor_tensor(out=x[:, j, :], in0=ps[:, :TT], in1=x[:, j, :], op=ALU.add), reads=[Bps, Bx], writes=[Bx])
            linear_fm(K, wo_s, Bwo, zT, BzT, TT, range(KC), ev_o)
            S.dma("sp", lambda e: e.dma_start(out=xt_tile(XO, i * TT, TT), in_=x[:]), Bx, reads=[Bx])
        S.barrier()


def fm(x):
    T, F_ = x.shape
    return np.ascontiguousarray(x.T.reshape(F_ // 128, 128, T))


def unfm(a):
    c, p, T = a.shape
    return np.ascontiguousarray(a.reshape(c * p, T).T)


def vec8(g):
    return np.ascontiguousarray(np.asarray(g, np.float32).reshape(-1, 128).T)


def _bucket_table():
    kk_ = np.arange(128)[:, None]
    jj = np.arange(1024)[None, :]
    d = jj - 384 - kk_
    n = np.maximum(d, 0)
    ex = 16
    nf = np.maximum(n, ex).astype(np.float32)
    large = ex + (np.log(nf / ex) / math.log(128 / ex) * (32 - ex)).astype(np.int32)
    large = np.minimum(large, 31)
    bk = np.where(n < ex, n, large)
    cm = np.where(d < 0, -30000.0, 0.0).astype(np.float32)
    return bk, cm


class Prog:
    def __init__(self, ncores=8):
        self.nc = bass.Bass("TRN2", target_bir_lowering=False)
        self.ncores = ncores
        self.in_maps = [dict() for _ in range(ncores)]
        self.out_names = []

    def inp(self, name, arrs):
        if not isinstance(arrs, (list, tuple)):
            arrs = [arrs] * self.ncores
        a0 = arrs[0]
        dt = BF16 if a0.dtype == NPBF else F32
        ap = self.nc.dram_tensor(name, list(a0.shape), dt, kind="ExternalInput").ap()
        for c in range(self.ncores):
            self.in_maps[c][name] = np.ascontiguousarray(arrs[c])
        return ap

    def out(self, name, shape, dt):
        self.out_names.append(name)
        return self.nc.dram_tensor(name, list(shape), dt, kind="ExternalOutput").ap()

    def tmp(self, name, shape, dt):
        return self.nc.dram_tensor(name, list(shape), dt, kind="Internal").ap()

    def run(self):
        res = run_bass_kernel_spmd(self.nc, self.in_maps, core_ids=list(range(self.ncores)))
        return res.results


class Host:
    def __init__(self, inp):
        self.i = {k: np.asarray(v) for k, v in inp.items()}
        self.bk, self.cm = _bucket_table()

    def core(self, c):
        return c // 2, c % 2

    def xT(self):
        x = self.i["x"]
        return [fm(x[c // 2, (c % 2) * NT:(c % 2 + 1) * NT]) for c in range(8)]

    def memT(self):
        return [fm(self.i["mem"][c // 2]) for c in range(8)]

    def heads(self, c):
        r = c % 2
        return list(range(r * NH, (r + 1) * NH))

    def hcols(self, c):
        return np.concatenate([np.arange(h * 128, (h + 1) * 128) for h in self.heads(c)])

    def da(self, j):
        I = self.i
        w = I["da_w_in"][j]
        out = {}
        for nm, off in (("wq", 0), ("wk", D), ("wv", 2 * D)):
            out[nm] = [np.ascontiguousarray(w[:, off + self.hcols(c)]) for c in range(8)]
        out["GB"] = [np.ascontiguousarray(np.stack([I["rel_bias"][h][self.bk] for h in self.heads(c)]).astype(np.float32)) for c in range(8)]
        out["CM"] = self.cm
        miscs = []
        for c in range(8):
            m = np.zeros((128, 4 * 64 + NH + 1), np.float32)
            m[:, 0:64] = I["da_lq1"][j]; m[:, 64:128] = I["da_lk1"][j]; m[:, 128:192] = I["da_lq2"][j]; m[:, 192:256] = I["da_lk2"][j]
            for hi, h in enumerate(self.heads(c)):
                m[:, 256 + hi] = I["rel_bias"][h, 31]
            m[:, 256 + NH] = I["da_subln"][j]
            miscs.append(m)
        out["misc"] = miscs
        out["wout"] = I["da_w_out"][j]
        return out

    def hg(self, j):
        I = self.i
        w = I["hg_w_in"][j]
        out = {}
        for k, nm in enumerate(("wq", "wf", "wi", "wg")):
            out[nm] = [np.ascontiguousarray(w[:, k * D + self.hcols(c)]) for c in range(8)]
        lbr = I["hg_lower_bounds"]
        out["oml"] = [np.ascontiguousarray(lbr[:, self.hcols(c)].reshape(lbr.shape[0], NH, 128).transpose(2, 1, 0)) for c in range(8)]
        out["gon"] = np.ascontiguousarray(I["hg_onorm"][j].reshape(128, 1))
        out["wout"] = I["hg_w_out"][j]
        return out

    def sg(self, j):
        I = self.i
        w = I["sg_w_in"][j]
        return dict(wu=np.ascontiguousarray(w[:, :D]), wv=np.ascontiguousarray(w[:, D:]),
                    gvb=np.ascontiguousarray(np.broadcast_to(I["sg_vnorm"][j], (128, D))),
                    wsT=np.ascontiguousarray(I["sg_w_s"][j].transpose(2, 0, 1)),
                    bs=np.ascontiguousarray(I["sg_b_s"][j].reshape(1, D)), wout=I["sg_w_out"][j])


def lam_init_of(layer_idx):
    return 0.8 - 0.6 * math.exp(-0.3 * layer_idx)


def emit_tail(P, K, H, i, XT_in, OG, wout, pre, next_norm_g, final, tagp):
    I = H.i
    mk = P.tmp
    cur = XT_in
    if OG is not None:
        X1 = mk(tagp + "X1", [KC, 128, NT], F32)
        stage_mix_out(K, cur, X1, OG, P.inp(tagp + "wout", wout), NT)
        cur = X1
    X2 = mk(tagp + "X2", [KC, 128, NT], F32)
    stage_cross(K, cur, X2, pre["memT"], pre["gmem"], P.inp(tagp + "gcx", vec8(I["norm_cross"][i])),
                P.inp(tagp + "cwq", I["ca_w_q"][i]), P.inp(tagp + "cwkv", I["ca_w_kv"][i]), P.inp(tagp + "cwo", I["ca_w_o"][i]), NT)
    X3 = mk(tagp + "X3", [KC, 128, NT], F32)
    XNs = mk(tagp + "XNs", [KC, 128, NT], BF16)
    gf = P.inp(tagp + "gfx", vec8(I["norm_ffn"][i]))
    wgu = P.inp(tagp + "wgu", I["ffn_w_gu"][i])
    wdn = P.inp(tagp + "wdn", I["ffn_w_down"][i])
    stage_ffn(K, X2, X3, XNs, gf, wgu, wdn, 0, NT)
    return X3, XNs, gf, wgu, wdn


def kernel_multi(**inp):
    H = Host(inp)
    I = H.i
    consts = make_consts()
    xT = H.xT()
    memT = H.memT()
    gmem = vec8(I["norm_mem"])
    depth = I["norm_mix"].shape[0]

    def newprog():
        P = Prog()
        st = ExitStack()
        K = Ctx(P.nc, st, P.inp("consts", consts))
        return P, K, st

    def pair_gather(xn):
        return [np.stack([xn[2 * (c // 2)], xn[2 * (c // 2) + 1]]) for c in range(8)]

    def pair_a2a(ot2):
        return [np.stack([ot2[2 * (c // 2) + rp][c % 2] for rp in range(2)]) for c in range(8)]

    P, K, st = newprog()
    with st:
        XT = P.inp("XT", xT)
        XN = P.out("XN", [KC, 128, NT], BF16)
        stage_norm(K, XT, P.inp("g", vec8(I["norm_mix"][0])), XN, NT)
    res = P.run()
    xn = [r["XN"] for r in res]
    OG = None
    wout = None
    for i in range(depth):
        kind, j = i % 3, i // 3
        if kind == 0:
            d = H.da(j)
            P, K, st = newprog()
            with st:
                XNF = P.inp("XNF", pair_gather(xn))
                QT = P.tmp("QT", [NH, 128, SEQ], BF16); KT = P.tmp("KT", [NH, 128, SEQ], BF16)
                V = P.tmp("V", [NH, 128, SEQ // 128, 128], BF16)
                OT2 = P.out("OT2", [2, NH, 128, NT], BF16)
                stage_da_in(K, XNF, P.inp("wq", d["wq"]), P.inp("wk", d["wk"]), P.inp("wv", d["wv"]), QT, KT, V, SEQ, NT)
                stage_da_core(K, QT, KT, V, P.inp("GB", d["GB"]), P.inp("CM", d["CM"]), P.inp("misc", d["misc"]), OT2, SEQ, lam_init_of(i))
            res = P.run()
            OG = pair_a2a([r["OT2"] for r in res]); wout = d["wout"]
        elif kind == 1:
            d = H.hg(j)
            P, K, st = newprog()
            with st:
                XNF = P.inp("XNF", pair_gather(xn))
                SQ = P.tmp("SQ", [NH, 128, SEQ], BF16); KKd = P.tmp("KK", [NH, 128, SEQ], F32)
                V = P.tmp("V", [NH, 128, SEQ // 128, 128], BF16); SG = P.tmp("SG", [NH, 128, SEQ], BF16)
                OT2 = P.out("OT2", [2, NH, 128, NT], BF16)
                stage_hg_in(K, XNF, P.inp("wq", d["wq"]), P.inp("wf", d["wf"]), P.inp("wi", d["wi"]), P.inp("wg", d["wg"]),
                            P.inp("oml", d["oml"]), SQ, KKd, V, SG, SEQ, NT, layer_idx=i)
                stage_hg_core(K, SQ, KKd, V, SG, P.inp("gon", d["gon"]), OT2, SEQ)
            res = P.run()
            OG = pair_a2a([r["OT2"] for r in res]); wout = d["wout"]
        P, K, st = newprog()
        with st:
            XT = P.inp("XT", xT)
            pre = dict(memT=P.inp("memT", memT), gmem=P.inp("gmem", gmem))
            cur = XT
            if kind == 2:
                d = H.sg(j)
                Xs = P.tmp("Xs", [KC, 128, NT], F32)
                stage_sgu(K, cur, Xs, P.inp("gmx", vec8(I["norm_mix"][i])), P.inp("wu", d["wu"]), P.inp("wv", d["wv"]), P.inp("gvb", d["gvb"]),
                          P.inp("wsT", d["wsT"]), P.inp("bs", d["bs"]), P.inp("swout", d["wout"]), NT)
                cur = Xs
                ogap = None
            else:
                ogap = P.inp("OG", OG)
            X3, XNs, gf, wgu, wdn = emit_tail(P, K, H, i, cur, ogap, wout, pre, None, False, "t_")
            last = (i == depth - 1)
            XO = P.out("XO", [KC, 128, NT], F32) if not last else P.tmp("XO", [KC, 128, NT], F32)
            stage_ffn(K, X3, XO, XNs, gf, wgu, wdn, 1, NT)
            if last:
                OUT = P.out("OUT", [KC, 128, NT], F32)
                stage_norm(K, XO, P.inp("gfin", vec8(I["norm_final"])), OUT, NT, out_f32=True)
            elif (i + 1) % 3 != 2:
                XN = P.out("XN", [KC, 128, NT], BF16)
                stage_norm(K, XO, P.inp("gnx", vec8(I["norm_mix"][i + 1])), XN, NT)
        res = P.run()
        if last:
            outT = [r["OUT"] for r in res]
        else:
            xT = [r["XO"] for r in res]
            if (i + 1) % 3 != 2:
                xn = [r["XN"] for r in res]
    B = I["x"].shape[0]
    out = np.empty((B, SEQ, D), np.float32)
    for c in range(8):
        out[c // 2, (c % 2) * NT:(c % 2 + 1) * NT] = unfm(outT[c])
    return out


def kernel_fused(**inp):
    H = Host(inp)
    I = H.i
    depth = I["norm_mix"].shape[0]
    B = I["x"].shape[0]
    S_ = SEQ
    P = Prog()
    nc = P.nc
    bmap = [c % B for c in range(8)]
    with ExitStack() as st:
        K = Ctx(nc, st, P.inp("consts", make_consts()))
        XT = P.inp("XT", [fm(I["x"][b]) for b in bmap])
        memT = P.inp("memT", [fm(I["mem"][b]) for b in bmap])
        gmem = P.inp("gmem", vec8(I["norm_mem"]))
        XA = P.tmp("XA", [KC, 128, S_], F32)
        XB = P.tmp("XB", [KC, 128, S_], F32)
        XN = P.tmp("XN", [1, KC, 128, S_], BF16)
        XNs = P.tmp("XNs", [KC, 128, S_], BF16)
        OT = [P.tmp("OT%d" % g, [1, NH, 128, S_], BF16) for g in range(2)]
        QT = P.tmp("QT", [NH, 128, S_], BF16)
        KT = P.tmp("KT", [NH, 128, S_], BF16)
        V = P.tmp("V", [NH, 128, S_ // 128, 128], BF16)
        KKd = P.tmp("KK", [NH, 128, S_], F32)
        SGd = P.tmp("SG", [NH, 128, S_], BF16)
        OUT = P.out("OUT", [KC, 128, S_], F32)
        cur = XT
        bufs = [XA, XB]
        nb = [0]

        def nxt():
            t = bufs[nb[0] % 2]
            nb[0] += 1
            return t
        for i in range(depth):
            kind, j = i % 3, i // 3
            tg = "L%d_" % i
            if kind in (0, 1):
                if i == 0:
                    stage_norm(K, cur, P.inp(tg + "gmix", vec8(I["norm_mix"][i])), XN[0], S_)
                d = H.da(j) if kind == 0 else H.hg(j)
                if kind == 0:
                    CM = P.inp(tg + "CM", d["CM"])
                else:
                    gon = P.inp(tg + "gon", d["gon"])
                for g in range(2):
                    tgg = tg + "g%d_" % g
                    if kind == 0:
                        stage_da_in(K, XN, P.inp(tgg + "wq", d["wq"][g]), P.inp(tgg + "wk", d["wk"][g]), P.inp(tgg + "wv", d["wv"][g]),
                                    QT, KT, V, S_, S_)
                        stage_da_core(K, QT, KT, V, P.inp(tgg + "GB", d["GB"][g]), CM, P.inp(tgg + "misc", d["misc"][g]), OT[g], S_,
                                      lam_init_of(i), half=S_)
                    else:
                        stage_hg_in(K, XN, P.inp(tgg + "wq", d["wq"][g]), P.inp(tgg + "wf", d["wf"][g]), P.inp(tgg + "wi", d["wi"][g]),
                                    P.inp(tgg + "wg", d["wg"][g]), P.inp(tgg + "oml", d["oml"][g]), QT, KKd, V, SGd, S_, S_, layer_idx=i)
                        stage_hg_core(K, QT, KKd, V, SGd, gon, OT[g], S_, half=S_)
                o = nxt()
                stage_mix_out(K, cur, o, [OT[0][0], OT[1][0]], P.inp(tg + "wout", d["wout"]), S_)
                cur = o
            else:
                d = H.sg(j)
                o = nxt()
                stage_sgu(K, cur, o, P.inp(tg + "gmix", vec8(I["norm_mix"][i])), P.inp(tg + "wu", d["wu"]), P.inp(tg + "wv", d["wv"]),
                          P.inp(tg + "gvb", d["gvb"]), P.inp(tg + "wsT", d["wsT"]), P.inp(tg + "bs", d["bs"]), P.inp(tg + "wout", d["wout"]), S_)
                cur = o
            o = nxt()
            stage_cross(K, cur, o, memT, gmem, P.inp(tg + "gcx", vec8(I["norm_cross"][i])), P.inp(tg + "cwq", I["ca_w_q"][i]),
                        P.inp(tg + "cwkv", I["ca_w_kv"][i]), P.inp(tg + "cwo", I["ca_w_o"][i]), S_)
            cur = o
            gf = P.inp(tg + "gfx", vec8(I["norm_ffn"][i]))
            wgu = P.inp(tg + "wgu", I["ffn_w_gu"][i])
            wdn = P.inp(tg + "wdn", I["ffn_w_down"][i])
            for hf in range(2):
                o = nxt()
                nx = None
                if hf == 1:
                    if i == depth - 1:
                        nx = (P.inp("gfin", vec8(I["norm_final"])), OUT, True)
                    elif (i + 1) % 3 != 2:
                        nx = (P.inp(tg + "gnext", vec8(I["norm_mix"][i + 1])), XN[0], False)
                stage_ffn(K, cur, o, XNs, gf, wgu, wdn, hf, S_, nxt=nx)
                cur = o
        K.S.barrier()
        print("fused program: %d instructions, %d dma sems" % (K.S.ninst, K.S.nsem))
    res = P.run()
    out = np.empty((B, SEQ, D), np.float32)
    for b in range(B):
        out[b] = unfm(res[b]["OUT"])
    return out


def kernel(**inputs):
    return kernel_fused(**inputs)
```

```python
import math
import numpy as np
import ml_dtypes
from contextlib import ExitStack
import concourse.bass as bass
import concourse.mybir as mybir
from concourse.bass_utils import run_bass_kernel_spmd

F32 = mybir.dt.float32
BF16 = mybir.dt.bfloat16
AF = mybir.ActivationFunctionType
ALU = mybir.AluOpType
AX = mybir.AxisListType
NPBF = ml_dtypes.bfloat16

D = 1024
KC = 8
SEQ = 8192
NT = 4096
NH = 4
TT = 512
EPS = 1e-6
DFF = 2816
FFH = DFF // 2
MEM = 256


class Buf:
    __slots__ = ("name", "w", "r", "dsem", "dcnt")

    def __init__(self, name=""):
        self.name = name
        self.w = None
        self.r = {}
        self.dsem = None


class Sched:
    ENGS = ("pe", "act", "dve", "pool", "sp")

    def __init__(self, nc, stack):
        self.nc = nc
        self.stack = stack
        self.esem, self.cnt, self.waited = {}, {}, {}
        self.semown = {}
        for e in self.ENGS:
            self.esem[e] = stack.enter_context(nc.semaphore("es_" + e))
            self.cnt[e] = 0
            self.waited[e] = {}
            self.semown[id(self.esem[e])] = e
        self.em = {"pe": nc.tensor, "act": nc.scalar, "dve": nc.vector, "pool": nc.gpsimd, "sp": nc.sync}
        self.ninst = 0
        self.nsem = 0
        self.free_dsems = []
        self.local_bufs = []

    def buf(self, name=""):
        b = Buf(name)
        self.local_bufs.append(b)
        return b

    def _deps(self, eng, reads, writes):
        deps = {}

        def add(tok, raw):
            sem, val = tok
            own = self.semown.get(id(sem))
            if own == eng and (eng == "pe" or not raw):
                return
            k = id(sem)
            if k not in deps or deps[k][1] < val:
                deps[k] = (sem, val)

        for b in reads:
            if b.w is not None:
                add(b.w, True)
        for b in writes:
            if b.w is not None:
                add(b.w, False)
            for k, tok in b.r.items():
                add(tok, False)
        waits = []
        wd = self.waited[eng]
        for k, (sem, val) in deps.items():
            if wd.get(k, 0) < val:
                wd[k] = val
                waits.append((sem, val))
        return waits

    def _mark(self, tok, reads, writes):
        sem, val = tok
        k = id(sem)
        for b in reads:
            if k not in b.r or b.r[k][1] < val:
                b.r[k] = (sem, val)
        for b in writes:
            b.w = tok
            b.r = {}

    def _emit(self, eng, waits, fn, sem, inc):
        e = self.em[eng]
        for (s, v) in waits:
            e.wait_ge(s, v)
        if fn is None:
            return
        ins = fn(e)
        if sem is not None:
            ins.then_inc(sem, inc)
        self.ninst += 1

    def op(self, eng, fn, reads=(), writes=(), inc=True):
        waits = self._deps(eng, reads, writes)
        if inc:
            self.cnt[eng] += 1
            tok = (self.esem[eng], self.cnt[eng])
            self._emit(eng, waits, fn, self.esem[eng], 1)
        else:
            tok = (self.esem[eng], self.cnt[eng] + 1)
            self._emit(eng, waits, fn, None, 0)
        self._mark(tok, reads, writes)

    def dma(self, q, fn, sb, reads=(), writes=()):
        if sb.dsem is None:
            if self.free_dsems:
                sb.dsem = self.free_dsems.pop()
            else:
                sem = self.stack.enter_context(self.nc.semaphore("ds_%d" % self.nsem))
                self.nsem += 1
                sb.dsem = [sem, 0]
        waits = self._deps(q, reads, writes)
        sb.dsem[1] += 16
        tok = (sb.dsem[0], sb.dsem[1])
        self._emit(q, waits, fn, sb.dsem[0], 16)
        self._mark(tok, reads, writes)

    def wait_all(self, eng, bufs):
        waits = self._deps(eng, bufs, bufs)
        self._emit(eng, waits, None, None, 0)

    def barrier(self, extra=()):
        bufs = list(self.local_bufs) + list(extra)
        for e in self.ENGS:
            waits = self._deps(e, bufs, bufs)
            wd = self.waited[e]
            for f in self.ENGS:
                if f == e:
                    continue
                k = id(self.esem[f])
                if wd.get(k, 0) < self.cnt[f]:
                    wd[k] = self.cnt[f]
                    waits.append((self.esem[f], self.cnt[f]))
            self._emit(e, waits, None, None, 0)
        for b in self.local_bufs:
            if b.dsem is not None:
                self.free_dsems.append(b.dsem)
        self.local_bufs = []


class Ctx:
    def __init__(self, nc, st, consts_ap):
        self.nc = nc
        self.S = Sched(nc, st)
        self.uid = 0
        S = self.S
        self.psum = []
        self.pd = []
        for i in range(4):
            t = st.enter_context(nc.psum_tensor("psd%d" % i, [128, 1024], F32))
            self.pd.append(t)
            for hf in range(2):
                self.psum.append((t[:, hf * 512:(hf + 1) * 512], Buf("ps%d" % (2 * i + hf))))
        self.ps_rr = 0
        self.Bc = Buf("const")
        self.onesD = self.sb(st, [128, 128], BF16)
        self.ones128 = self.sb(st, [128, 128], BF16)
        self.ones1 = self.sb(st, [128, 128], BF16)
        self.epsT = self.sb(st, [128, 1], F32)
        self.cf = self.sb(st, [128, 2 * 128 + 512], F32)
        self.maskle = self.sb(st, [128, 128], BF16)
        self.ident = self.sb(st, [128, 128], BF16)
        Bl = Buf("cload")
        S.dma("sp", lambda e: e.dma_start(out=self.cf[:], in_=consts_ap), Bl, writes=[Bl])
        S.op("dve", lambda e: e.memset(self.onesD[:], 1.0 / D), writes=[self.Bc])
        S.op("dve", lambda e: e.memset(self.ones128[:], 1.0 / 128), writes=[self.Bc])
        S.op("dve", lambda e: e.memset(self.ones1[:], 1.0), writes=[self.Bc])
        S.op("dve", lambda e: e.memset(self.epsT[:], EPS), writes=[self.Bc])
        self.ones1f = self.sb(st, [128, 128], F32)
        S.op("dve", lambda e: e.memset(self.ones1f[:], 1.0), writes=[self.Bc])
        self.oneT = self.sb(st, [128, 1], F32)
        S.op("dve", lambda e: e.memset(self.oneT[:], 1.0), writes=[self.Bc])
        self.onesrow = self.sb(st, [1, 128], BF16)
        S.op("dve", lambda e: e.memset(self.onesrow[:], 1.0), writes=[self.Bc])
        S.op("dve", lambda e: e.tensor_copy(out=self.maskle[:], in_=self.cf[:, 0:128]), reads=[Bl], writes=[self.Bc])
        S.op("dve", lambda e: e.tensor_copy(out=self.ident[:], in_=self.cf[:, 128:256]), reads=[Bl], writes=[self.Bc])
        self.scanmask = self.cf[:, 256:768]
        self.Bcf = Bl

    def sb(self, st, shape, dt):
        self.uid += 1
        return st.enter_context(self.nc.sbuf_tensor("t%d" % self.uid, list(shape), dt))

    def ps(self):
        t, b = self.psum[self.ps_rr % 8]
        self.ps_rr += 1
        return t, b

    def ps_i(self, i):
        return self.psum[i]


def make_consts():
    c = np.zeros((128, 2 * 128 + 512), np.float32)
    c[:, 0:128] = np.triu(np.ones((128, 128), np.float32))
    c[:, 128:256] = np.eye(128, dtype=np.float32)
    m = np.ones(512, np.float32)
    m[::64] = 0.0
    c[:, 256:768] = m[None, :]
    return c


def load_w(K, dst, Bdst, w_ap, q="pool"):
    kc = w_ap.shape[0] // 128
    src = w_ap.rearrange("(c p) n -> p c n", p=128)
    for c in range(kc):
        K.S.dma(q, lambda e, c=c: e.dma_start(out=dst[:, c, :], in_=src[:, c, :]), Bdst, writes=[Bdst])


def load_vec(K, dst, Bdst, v_ap, q="sp"):
    K.S.dma(q, lambda e: e.dma_start(out=dst[:], in_=v_ap), Bdst, writes=[Bdst])


def rmsnorm_a(K, x, Bx, sq, Bsq, n, kc=KC):
    K.S.op("act", lambda e: e.activation(out=sq[:, :kc, :n], in_=x[:, :kc, :n], func=AF.Square), reads=[Bx], writes=[Bsq])


def rmsnorm_b(K, x, Bx, g, Bg, sq, Bsq, rstd, Brstd, xn, Bxn, n, kc=KC, ones=None, xn_eng="dve"):
    S = K.S
    ones = K.onesD if ones is None else ones
    ps, Bps = K.ps()
    for c in range(kc):
        S.op("pe", lambda e, c=c: e.matmul(ps[:, :n], ones[:], sq[:, c, :n], start=(c == 0), stop=(c == kc - 1)),
             reads=[Bsq, K.Bc], writes=[Bps], inc=(c == kc - 1))
    S.op("act", lambda e: e.activation(out=rstd[:, :n], in_=ps[:, :n], func=AF.Ln, bias=K.epsT[:], scale=1.0),
         reads=[Bps, K.Bc], writes=[Brstd])
    S.op("act", lambda e: e.activation(out=rstd[:, :n], in_=rstd[:, :n], func=AF.Exp, scale=-0.5), reads=[Brstd], writes=[Brstd])
    for c in range(kc):
        S.op(xn_eng, lambda e, c=c: e.scalar_tensor_tensor(out=xn[:, c, :n], in0=x[:, c, :n], scalar=g[:, c:c + 1],
                                                           in1=rstd[:, :n], op0=ALU.mult, op1=ALU.mult),
             reads=[Bx, Bg, Brstd], writes=[Bxn])


def rmsnorm_fm(K, x, Bx, g, Bg, sq, Bsq, rstd, Brstd, xn, Bxn, n, kc=KC, ones=None, xn_eng="dve"):
    rmsnorm_a(K, x, Bx, sq, Bsq, n, kc)
    rmsnorm_b(K, x, Bx, g, Bg, sq, Bsq, rstd, Brstd, xn, Bxn, n, kc, ones, xn_eng)


def linear_fm(K, w, Bw, xin, Bxin, n, jlist, evac, kc=KC, wcol0=0):
    S = K.S
    for j in jlist:
        ps, Bps = K.ps()
        for k in range(kc):
            S.op("pe", lambda e, j=j, k=k, ps=ps: e.matmul(ps[:, :n], w[:, k, wcol0 + j * 128: wcol0 + (j + 1) * 128],
                                                           xin[:, k, :n], start=(k == 0), stop=(k == kc - 1)),
                 reads=[Bw, Bxin], writes=[Bps], inc=(k == kc - 1))
        evac(j, ps, Bps)


def xt_tile(ap3, t0, n):
    return ap3[:, :, t0:t0 + n].rearrange("c p t -> p c t")


def stage_norm(K, XT, gvec, OUT, nt, out_f32=False):
    S = K.S
    with ExitStack() as st:
        g = K.sb(st, [128, KC], F32); Bg = S.buf()
        load_vec(K, g, Bg, gvec)
        xs = [(K.sb(st, [128, KC, TT], F32), S.buf()) for _ in range(2)]
        sqs = [(K.sb(st, [128, KC, TT], BF16), S.buf()) for _ in range(2)]
        rs = [(K.sb(st, [128, TT], F32), S.buf()) for _ in range(2)]
        odt = F32 if out_f32 else BF16
        os_ = [(K.sb(st, [128, KC, TT], odt), S.buf()) for _ in range(2)]
        ntile = nt // TT
        for i in range(ntile):
            x, Bx = xs[i % 2]; sq, Bsq = sqs[i % 2]; r, Br = rs[i % 2]; o, Bo = os_[i % 2]
            S.dma("sp", lambda e, x=x, i=i: e.dma_start(out=x[:], in_=xt_tile(XT, i * TT, TT)), Bx, writes=[Bx])
            rmsnorm_fm(K, x, Bx, g, Bg, sq, Bsq, r, Br, o, Bo, TT)
            S.dma("sp", lambda e, o=o, i=i: e.dma_start(out=xt_tile(OUT, i * TT, TT), in_=o[:]), Bo, reads=[Bo])
        S.barrier()


def stage_cross(K, XT, XO, memT, gmem, gx, wq, wkv, wo, nt):
    S = K.S
    HC = 4
    with ExitStack() as st:
        wq_s = K.sb(st, [128, KC, D], BF16); Bwq = S.buf()
        wkv_s = K.sb(st, [128, KC, 2 * D], BF16); Bwkv = S.buf()
        wo_s = K.sb(st, [128, KC, D], BF16); Bwo = S.buf()
        gm = K.sb(st, [128, KC], F32); Bgm = S.buf()
        g = K.sb(st, [128, KC], F32); Bg = S.buf()
        load_vec(K, gm, Bgm, gmem)
        load_vec(K, g, Bg, gx)
        load_w(K, wkv_s, Bwkv, wkv)
        load_w(K, wq_s, Bwq, wq)
        load_w(K, wo_s, Bwo, wo)
        mx = K.sb(st, [128, KC, MEM], F32); Bmx = S.buf()
        msq = K.sb(st, [128, KC, MEM], BF16); Bmsq = S.buf()
        mr = K.sb(st, [128, MEM], F32); Bmr = S.buf()
        mn = K.sb(st, [128, KC, MEM], BF16); Bmn = S.buf()
        kT = K.sb(st, [128, KC, MEM], BF16); BkT = S.buf()
        vtok = K.sb(st, [128, 2, D], BF16); Bv = S.buf()
        S.dma("sp", lambda e: e.dma_start(out=mx[:], in_=memT.rearrange("c p t -> p c t")), Bmx, writes=[Bmx])
        rmsnorm_fm(K, mx, Bmx, gm, Bgm, msq, Bmsq, mr, Bmr, mn, Bmn, MEM)

        def ev_k(j, ps, Bps):
            S.op("act", lambda e: e.activation(out=kT[:, j, :], in_=ps[:, :MEM], func=AF.Copy), reads=[Bps], writes=[BkT])
        linear_fm(K, wkv_s, Bwkv, mn, Bmn, MEM, range(KC), ev_k)
        for mc in range(2):
            for half in range(2):
                ps, Bps = K.ps()
                for k in range(KC):
                    S.op("pe", lambda e, k=k, mc=mc, half=half, ps=ps: e.matmul(
                        ps[:, :512], mn[:, k, mc * 128:(mc + 1) * 128], wkv_s[:, k, D + half * 512: D + (half + 1) * 512],
                        start=(k == 0), stop=(k == KC - 1)), reads=[Bmn, Bwkv], writes=[Bps], inc=(k == KC - 1))
                S.op("dve", lambda e, mc=mc, half=half, ps=ps: e.tensor_copy(out=vtok[:, mc, half * 512:(half + 1) * 512], in_=ps[:, :512]),
                     reads=[Bps], writes=[Bv])
        xs = [(K.sb(st, [128, KC, TT], F32), S.buf()) for _ in range(2)]
        sq = K.sb(st, [128, KC, TT], BF16); Bsq = S.buf()
        r = K.sb(st, [128, TT], F32); Br = S.buf()
        xns = [(K.sb(st, [128, KC, TT], BF16), S.buf()) for _ in range(2)]
        qT = K.sb(st, [128, KC, TT], BF16); BqT = S.buf()
        pT = [(K.sb(st, [128, 2, TT], BF16), S.buf()) for _ in range(2)]
        rd = [(K.sb(st, [128, TT], F32), S.buf()) for _ in range(2)]
        oT = K.sb(st, [128, KC, TT], BF16); BoT = S.buf()
        ntile = nt // TT
        scale = 256 ** -0.5

        def load(i):
            x, Bx = xs[i % 2]
            S.dma("sp", lambda e: e.dma_start(out=x[:], in_=xt_tile(XT, i * TT, TT)), Bx, writes=[Bx])
        load(0)
        rmsnorm_fm(K, xs[0][0], xs[0][1], g, Bg, sq, Bsq, r, Br, xns[0][0], xns[0][1], TT)
        for i in range(ntile):
            x, Bx = xs[i % 2]
            xn, Bxn = xns[i % 2]
            if i + 1 < ntile:
                load(i + 1)
                rmsnorm_a(K, xs[(i + 1) % 2][0], xs[(i + 1) % 2][1], sq, Bsq, TT)

            def ev_q(j, ps, Bps):
                S.op("act", lambda e: e.activation(out=qT[:, j, :], in_=ps[:, :TT], func=AF.Copy), reads=[Bps], writes=[BqT])
            linear_fm(K, wq_s, Bwq, xn, Bxn, TT, range(KC), ev_q)
            if i + 1 < ntile:
                rmsnorm_b(K, xs[(i + 1) % 2][0], xs[(i + 1) % 2][1], g, Bg, sq, Bsq, r, Br, xns[(i + 1) % 2][0], xns[(i + 1) % 2][1], TT)
            for h in range(HC):
                p, Bp = pT[h % 2]
                rdt, Brd = rd[h % 2]
                for mc in range(2):
                    ps, Bps = K.ps()
                    for dc in range(2):
                        S.op("pe", lambda e, h=h, mc=mc, dc=dc, ps=ps: e.matmul(
                            ps[:, :TT], kT[:, 2 * h + dc, mc * 128:(mc + 1) * 128], qT[:, 2 * h + dc, :],
                            start=(dc == 0), stop=(dc == 1)), reads=[BkT, BqT], writes=[Bps], inc=(dc == 1))
                    S.op("act", lambda e, mc=mc, ps=ps, p=p: e.activation(out=p[:, mc, :], in_=ps[:, :TT], func=AF.Exp, scale=scale),
                         reads=[Bps], writes=[Bp])
                psd, Bpsd = K.ps()
                for mc in range(2):
                    S.op("pe", lambda e, mc=mc, psd=psd, p=p: e.matmul(psd[:, :TT], K.ones1[:], p[:, mc, :], start=(mc == 0), stop=(mc == 1)),
                         reads=[Bp, K.Bc], writes=[Bpsd], inc=(mc == 1))
                S.op("act", lambda e, psd=psd, rdt=rdt: e.activation(out=rdt[:], in_=psd[:, :TT], func=AF.Ln), reads=[Bpsd], writes=[Brd])
                S.op("act", lambda e, rdt=rdt: e.activation(out=rdt[:], in_=rdt[:], func=AF.Exp, scale=-1.0), reads=[Brd], writes=[Brd])
                for dv in range(2):
                    pso, Bpso = K.ps()
                    for mc in range(2):
                        S.op("pe", lambda e, h=h, mc=mc, dv=dv, pso=pso, p=p: e.matmul(
                            pso[:, :TT], vtok[:, mc, (2 * h + dv) * 128:(2 * h + dv + 1) * 128], p[:, mc, :],
                            start=(mc == 0), stop=(mc == 1)), reads=[Bv, Bp], writes=[Bpso], inc=(mc == 1))
                    S.op("dve", lambda e, h=h, dv=dv, pso=pso, rdt=rdt: e.tensor_tensor(out=oT[:, 2 * h + dv, :], in0=pso[:, :TT], in1=rdt[:], op=ALU.mult),
                         reads=[Bpso, Brd], writes=[BoT])

            def ev_o(j, ps, Bps):
                S.op("dve", lambda e: e.tensor_tensor(out=x[:, j, :], in0=ps[:, :TT], in1=x[:, j, :], op=ALU.add), reads=[Bps, Bx], writes=[Bx])
            linear_fm(K, wo_s, Bwo, oT, BoT, TT, range(KC), ev_o)
            S.dma("sp", lambda e, x=x, i=i: e.dma_start(out=xt_tile(XO, i * TT, TT), in_=x[:]), Bx, reads=[Bx])
        S.barrier()


def stage_ffn(K, XT, XO, XN, gx, wgu, wdn, half, nt, nxt=None):
    S = K.S
    NJ = FFH // 128
    with ExitStack() as st:
        wg_s = K.sb(st, [128, KC, FFH], BF16); Bwg = S.buf()
        wu_s = K.sb(st, [128, KC, FFH], BF16); Bwu = S.buf()
        wd_s = K.sb(st, [128, NJ, D], BF16); Bwd = S.buf()
        g = K.sb(st, [128, KC], F32); Bg = S.buf()
        load_vec(K, g, Bg, gx)
        load_w(K, wg_s, Bwg, wgu[:, half * FFH: (half + 1) * FFH])
        load_w(K, wu_s, Bwu, wgu[:, DFF + half * FFH: DFF + (half + 1) * FFH])
        load_w(K, wd_s, Bwd, wdn[half * FFH:(half + 1) * FFH, :])
        xs = [(K.sb(st, [128, KC, TT], F32), S.buf()) for _ in range(2)]
        xns = [(K.sb(st, [128, KC, TT], BF16), S.buf()) for _ in range(2)]
        sq = K.sb(st, [128, KC, TT], BF16); Bsq = S.buf()
        r = K.sb(st, [128, TT], F32); Br = S.buf()
        sg = [(K.sb(st, [128, TT], F32), S.buf()) for _ in range(2)]
        hT = K.sb(st, [128, NJ, TT], BF16); BhT = S.buf()
        if nxt is not None:
            gn_ap, OUTN, out_f32 = nxt
            gn = K.sb(st, [128, KC], F32); Bgn = S.buf()
            load_vec(K, gn, Bgn, gn_ap)
            on = [(K.sb(st, [128, KC, TT], F32 if out_f32 else BF16), S.buf()) for _ in range(2)]
        ntile = nt // TT

        def load(i):
            x, Bx = xs[i % 2]
            S.dma("sp", lambda e: e.dma_start(out=x[:], in_=xt_tile(XT, i * TT, TT)), Bx, writes=[Bx])
            if half == 1:
                xn, Bxn = xns[i % 2]
                S.dma("sp", lambda e: e.dma_start(out=xn[:], in_=xt_tile(XN, i * TT, TT)), Bxn, writes=[Bxn])

        def norm_a(i):
            rmsnorm_a(K, xs[i % 2][0], xs[i % 2][1], sq, Bsq, TT)

        def norm_b(i):
            xn, Bxn = xns[i % 2]
            rmsnorm_b(K, xs[i % 2][0], xs[i % 2][1], g, Bg, sq, Bsq, r, Br, xn, Bxn, TT)
            S.dma("sp", lambda e: e.dma_start(out=xt_tile(XN, i * TT, TT), in_=xn[:]), Bxn, reads=[Bxn])
        load(0)
        pend = []
        if half == 0:
            norm_a(0)
            norm_b(0)
        for i in range(ntile):
            x, Bx = xs[i % 2]
            xn, Bxn = xns[i % 2]
            if i + 1 < ntile and nxt is None:
                load(i + 1)
                if half == 0:
                    norm_a(i + 1)
            for j in range(NJ):
                psg, Bpsg = K.ps()
                for k in range(KC):
                    S.op("pe", lambda e: e.matmul(psg[:, :TT], wg_s[:, k, j * 128:(j + 1) * 128], xn[:, k, :], start=(k == 0), stop=(k == KC - 1)),
                         reads=[Bwg, Bxn], writes=[Bpsg], inc=(k == KC - 1))
                psu, Bpsu = K.ps()
                for k in range(KC):
                    S.op("pe", lambda e: e.matmul(psu[:, :TT], wu_s[:, k, j * 128:(j + 1) * 128], xn[:, k, :], start=(k == 0), stop=(k == KC - 1)),
                         reads=[Bwu, Bxn], writes=[Bpsu], inc=(k == KC - 1))
                s_, Bs_ = sg[j % 2]
                S.op("act", lambda e: e.activation(out=s_[:], in_=psg[:, :TT], func=AF.Silu), reads=[Bpsg], writes=[Bs_])
                S.op("dve", lambda e: e.tensor_tensor(out=hT[:, j, :], in0=psu[:, :TT], in1=s_[:], op=ALU.mult), reads=[Bpsu, Bs_], writes=[BhT])
                if j == NJ // 2 and half == 0 and i + 1 < ntile:
                    norm_b(i + 1)
                if j == 1:
                    while pend:
                        pend.pop(0)()
                    if i + 1 < ntile and nxt is not None:
                        load(i + 1)
                        if half == 0:
                            norm_a(i + 1)

            def ev_d(j, ps, Bps):
                S.op("dve", lambda e: e.tensor_tensor(out=x[:, j, :], in0=ps[:, :TT], in1=x[:, j, :], op=ALU.add), reads=[Bps, Bx], writes=[Bx])
            linear_fm(K, wd_s, Bwd, hT, BhT, TT, range(KC), ev_d, kc=NJ)
            S.dma("sp", lambda e: e.dma_start(out=xt_tile(XO, i * TT, TT), in_=x[:]), Bx, reads=[Bx])
            if nxt is not None:
                rmsnorm_a(K, x, Bx, sq, Bsq, TT)

                def post(i=i, x=x, Bx=Bx):
                    o, Bo = on[i % 2]
                    rmsnorm_b(K, x, Bx, gn, Bgn, sq, Bsq, r, Br, o, Bo, TT)
                    S.dma("sp", lambda e: e.dma_start(out=xt_tile(OUTN, i * TT, TT), in_=o[:]), Bo, reads=[Bo])
                pend.append(post)
        while pend:
            pend.pop(0)()
        S.barrier()


def stage_da_in(K, XNF, wq, wk, wv, QT, KT, V, seq, nt_half, qscale=0.125, norm=None):
    S = K.S
    HW = NH * 128
    with ExitStack() as st:
        wq_s = K.sb(st, [128, KC, HW], BF16); Bwq = S.buf()
        wk_s = K.sb(st, [128, KC, HW], BF16); Bwk = S.buf()
        wv_s = K.sb(st, [128, KC, HW], BF16); Bwv = S.buf()
        load_w(K, wq_s, Bwq, wq)
        load_w(K, wk_s, Bwk, wk)
        load_w(K, wv_s, Bwv, wv)
        xns = [(K.sb(st, [128, KC, TT], BF16), S.buf()) for _ in range(2)]
        qst = [(K.sb(st, [128, NH, TT], BF16), S.buf()) for _ in range(2)]
        kst = [(K.sb(st, [128, NH, TT], BF16), S.buf()) for _ in range(2)]
        vst = [(K.sb(st, [128, NH, TT // 128, 128], BF16), S.buf()) for _ in range(2)]
        ntile = seq // TT
        per = nt_half // TT
        if norm is not None:
            XTn, gvec = norm
            g = K.sb(st, [128, KC], F32); Bg = S.buf()
            load_vec(K, g, Bg, gvec)
            xs = [(K.sb(st, [128, KC, TT], F32), S.buf()) for _ in range(2)]
            sq = K.sb(st, [128, KC, TT], BF16); Bsq = S.buf()
            r = K.sb(st, [128, TT], F32); Br = S.buf()

        def load(i):
            if norm is None:
                xn, Bxn = xns[i % 2]
                S.dma("sp", lambda e: e.dma_start(out=xn[:], in_=xt_tile(XNF[i // per], (i % per) * TT, TT)), Bxn, writes=[Bxn])
            else:
                x, Bx = xs[i % 2]
                S.dma("sp", lambda e: e.dma_start(out=x[:], in_=xt_tile(XTn, i * TT, TT)), Bx, writes=[Bx])

        def norm_b(i):
            xn, Bxn = xns[i % 2]
            rmsnorm_b(K, xs[i % 2][0], xs[i % 2][1], g, Bg, sq, Bsq, r, Br, xn, Bxn, TT)
            S.dma("sp", lambda e: e.dma_start(out=xt_tile(XNF[i // per], (i % per) * TT, TT), in_=xn[:]), Bxn, reads=[Bxn])
        load(0)
        if norm is not None:
            rmsnorm_a(K, xs[0][0], xs[0][1], sq, Bsq, TT)
            norm_b(0)
        for i in range(ntile):
            xn, Bxn = xns[i % 2]
            q_, Bq_ = qst[i % 2]; k_, Bk_ = kst[i % 2]; v_, Bv_ = vst[i % 2]
            if i + 1 < ntile:
                load(i + 1)
                if norm is not None:
                    rmsnorm_a(K, xs[(i + 1) % 2][0], xs[(i + 1) % 2][1], sq, Bsq, TT)

            def ev_q(j, ps, Bps):
                S.op("act", lambda e: e.mul(out=q_[:, j, :], in_=ps[:, :TT], mul=qscale), reads=[Bps], writes=[Bq_])
            linear_fm(K, wq_s, Bwq, xn, Bxn, TT, range(NH), ev_q)
            if norm is not None and i + 1 < ntile:
                norm_b(i + 1)

            def ev_k(j, ps, Bps):
                S.op("dve", lambda e: e.tensor_copy(out=k_[:, j, :], in_=ps[:, :TT]), reads=[Bps], writes=[Bk_])
            linear_fm(K, wk_s, Bwk, xn, Bxn, TT, range(NH), ev_k)
            for sub in range(TT // 128):
                ps, Bps = K.ps()
                for k in range(KC):
                    S.op("pe", lambda e, k=k: e.matmul(ps[:, :HW], xn[:, k, sub * 128:(sub + 1) * 128], wv_s[:, k, :],
                                                       start=(k == 0), stop=(k == KC - 1)),
                         reads=[Bxn, Bwv], writes=[Bps], inc=(k == KC - 1))
                psv = ps[:, :HW].rearrange("p (h d) -> p h d", h=NH)
                if sub % 2 == 0:
                    S.op("act", lambda e: e.copy(out=v_[:, :, sub, :], in_=psv), reads=[Bps], writes=[Bv_])
                else:
                    S.op("dve", lambda e: e.tensor_copy(out=v_[:, :, sub, :], in_=psv), reads=[Bps], writes=[Bv_])
            s0 = i * TT
            S.dma("sp", lambda e: e.dma_start(out=QT[:, :, s0:s0 + TT].rearrange("h p t -> p h t"), in_=q_[:]), Bq_, reads=[Bq_])
            S.dma("sp", lambda e: e.dma_start(out=KT[:, :, s0:s0 + TT].rearrange("h p t -> p h t"), in_=k_[:]), Bk_, reads=[Bk_])
            j0 = s0 // 128
            S.dma("sp", lambda e: e.dma_start(out=V[:, :, j0:j0 + TT // 128, :].rearrange("h p j d -> p h j d"), in_=v_[:]), Bv_, reads=[Bv_])
        S.barrier()


def stage_da_core(K, QT, KT, V, GB, CM, misc, OT2, seq, lam_init, half=None):
    S = K.S
    NCH = seq // 128
    with ExitStack() as st:
        ms = K.sb(st, [128, 4 * 64 + NH + 1], F32); Bms = S.buf()
        cm = K.sb(st, [128, 1024], F32); Bcm = S.buf()
        load_vec(K, ms, Bms, misc)
        load_vec(K, cm, Bcm, CM)
        sc = K.sb(st, [128, 8], F32); Bsc = S.buf()
        tmp64 = K.sb(st, [128, 64], F32); Bt64 = S.buf()
        for c in range(2):
            S.op("dve", lambda e: e.tensor_tensor(out=tmp64[:], in0=ms[:, (2 * c) * 64:(2 * c + 1) * 64],
                                                  in1=ms[:, (2 * c + 1) * 64:(2 * c + 2) * 64], op=ALU.mult), reads=[Bms], writes=[Bt64])
            S.op("dve", lambda e: e.reduce_sum(out=sc[:, c:c + 1], in_=tmp64[:], axis=AX.X), reads=[Bt64], writes=[Bsc])
        S.op("act", lambda e: e.activation(out=sc[:, 0:2], in_=sc[:, 0:2], func=AF.Exp), reads=[Bsc], writes=[Bsc])
        S.op("dve", lambda e: e.scalar_tensor_tensor(out=sc[:, 2:3], in0=sc[:, 1:2], scalar=-lam_init, in1=sc[:, 0:1],
                                                     op0=ALU.add, op1=ALU.subtract), reads=[Bsc], writes=[Bsc])
        gcol = 4 * 64 + NH
        S.op("dve", lambda e: e.tensor_scalar(out=sc[:, 3:4], in0=ms[:, gcol:gcol + 1], scalar1=(1.0 - lam_init), scalar2=None, op0=ALU.mult),
             reads=[Bms, Bsc], writes=[Bsc])
        Mh = []
        gbt = K.sb(st, [128, 1024], F32); Bgbt = S.buf()
        for h in range(NH):
            m = K.sb(st, [128, 1024], F32); Bm = S.buf()
            S.dma("sp", lambda e: e.dma_start(out=gbt[:], in_=GB[h]), Bgbt, writes=[Bgbt])
            S.op("dve", lambda e: e.scalar_tensor_tensor(out=m[:], in0=gbt[:], scalar=ms[:, 4 * 64 + h: 4 * 64 + h + 1], in1=cm[:],
                                                         op0=ALU.subtract, op1=ALU.add), reads=[Bgbt, Bms, Bcm], writes=[Bm])
            Mh.append((m, Bm))
        hb = []
        for _ in range(2):
            hb.append(dict(k=K.sb(st, [128, seq], BF16), Bk=S.buf(), v=K.sb(st, [128, NCH, 128], BF16), Bv=S.buf(),
                           q=K.sb(st, [128, seq], BF16), Bq=S.buf()))
        NPB = 6
        pbuf = [(K.sb(st, [128, 2, TT], BF16), S.buf()) for _ in range(NPB)]
        sums = [dict(a=K.sb(st, [128, 2, TT], BF16), Ba=S.buf(), b=K.sb(st, [128, 2, TT], BF16), Bb=S.buf(),
                     s=K.sb(st, [128, 2, TT], BF16), Bs=S.buf()) for _ in range(2)]
        O1s = K.sb(st, [128, 2, TT], F32); BO1s = S.buf()
        sbuf = [(K.sb(st, [128, 2, TT], F32), S.buf()) for _ in range(2)]
        rr_ = K.sb(st, [128, 2, TT], F32); Brr = S.buf()
        A = K.sb(st, [128, TT], F32); BA = S.buf()
        Bm_ = K.sb(st, [128, TT], F32); BB = S.buf()
        sq = K.sb(st, [128, TT], BF16); Bsq = S.buf()
        rs = K.sb(st, [128, TT], F32); Brs = S.buf()
        ost = [(K.sb(st, [128, TT], BF16), S.buf()) for _ in range(2)]
        (pO1, BO1), (pO2, BO2), (pD1, BD1), (pD2, BD2) = [K.ps_i(i) for i in (4, 5, 6, 7)]
        rr = [0]

        def psr():
            t = K.ps_i(rr[0] % 4)
            rr[0] += 1
            return t

        def loadh(h):
            b = hb[h % 2]
            for piece in range(4):
                c0 = piece * (seq // 4)
                S.dma("sp", lambda e: e.dma_start(out=b["k"][:, c0:c0 + seq // 4], in_=KT[h][:, c0:c0 + seq // 4]), b["Bk"], writes=[b["Bk"]])
            for piece in range(4):
                j0 = piece * (NCH // 4)
                S.dma("sp", lambda e: e.dma_start(out=b["v"][:, j0:j0 + NCH // 4, :], in_=V[h][:, j0:j0 + NCH // 4, :]),
                      b["Bv"], writes=[b["Bv"]])
            for piece in range(4):
                c0 = piece * (seq // 4)
                S.dma("sp", lambda e: e.dma_start(out=b["q"][:, c0:c0 + seq // 4], in_=QT[h][:, c0:c0 + seq // 4]), b["Bq"], writes=[b["Bq"]])
        loadh(0)
        ntile = seq // TT
        scb = [(K.pd[0], K.psum[0][1], K.psum[1][1]), (K.pd[1], K.psum[2][1], K.psum[3][1])]
        hf_ = (seq // 2) if half is None else half
        for h in range(NH):
            b = hb[h % 2]
            if h + 1 < NH:
                loadh(h + 1)
            m, Bm = Mh[h]
            kt, vt, qt = b["k"], b["v"], b["q"]
            pairs = [(t, j) for t in range(ntile) for j in range(4 * (t + 1))]
            N = len(pairs)

            def stA(n):
                t, j = pairs[n]
                q0 = t * TT
                sc, Bs0, Bs1 = scb[n % 2]
                for c, Bps in ((0, Bs0), (1, Bs1)):
                    S.op("pe", lambda e: e.matmul(sc[:, c * 512:(c + 1) * 512], kt[c * 64:(c + 1) * 64, j * 128:(j + 1) * 128],
                                                  qt[c * 64:(c + 1) * 64, q0:q0 + TT], start=True, stop=True),
                         reads=[b["Bk"], b["Bq"]], writes=[Bps], inc=(c == 1))

            def stB(n):
                t, j = pairs[n]
                q0 = t * TT
                sc, Bs0, Bs1 = scb[n % 2]
                p, Bp = pbuf[n % NPB]
                sc3 = sc[:, :].rearrange("p (c t) -> p c t", c=2)
                if j >= 4 * t - 1:
                    c0 = 384 - (128 * j - q0)
                    s_, Bs_ = sbuf[n % 2]
                    for c, Bps in ((0, Bs0), (1, Bs1)):
                        S.op("dve", lambda e: e.tensor_tensor(out=s_[:, c, :], in0=sc[:, c * 512:(c + 1) * 512], in1=m[:, c0:c0 + TT], op=ALU.add),
                             reads=[Bps, Bm], writes=[Bs_])
                    S.op("act", lambda e: e.activation(out=p[:], in_=s_[:], func=AF.Exp), reads=[Bs_], writes=[Bp])
                else:
                    S.op("act", lambda e: e.activation(out=p[:], in_=sc3, func=AF.Exp), reads=[Bs0, Bs1], writes=[Bp])

            pending = []

            def stC(n):
                t, j = pairs[n]
                nch = 4 * (t + 1)
                p, Bp = pbuf[n % NPB]
                first, last = (j == 0), (j == nch - 1)
                pO = K.pd[2]
                BOa, BOb = K.psum[4][1], K.psum[5][1]
                S.op("pe", lambda e: e.matmul(pO[:, 0:TT], vt[:, j, :], p[:, 0, :], start=first, stop=last), reads=[b["Bv"], Bp], writes=[BOa], inc=False)
                S.op("pe", lambda e: e.matmul(pO[:, 512:512 + TT], vt[:, j, :], p[:, 1, :], start=first, stop=last), reads=[b["Bv"], Bp], writes=[BOb])
                while pending:
                    pending.pop(0)()
                g = (j // 4)
                sm = sums[g % 2]
                jj = j % 4
                if jj == 1 or jj == 3:
                    pp, Bpp = pbuf[(n - 1) % NPB]
                    dst, Bdst = (sm["a"], sm["Ba"]) if jj == 1 else (sm["b"], sm["Bb"])
                    S.op("dve", lambda e: e.tensor_tensor(out=dst[:], in0=pp[:], in1=p[:], op=ALU.add), reads=[Bpp, Bp], writes=[Bdst])
                if jj == 3:
                    S.op("dve", lambda e: e.tensor_tensor(out=sm["s"][:], in0=sm["a"][:], in1=sm["b"][:], op=ALU.add),
                         reads=[sm["Ba"], sm["Bb"]], writes=[sm["Bs"]])
                    pD = K.pd[3]
                    BDa, BDb = K.psum[6][1], K.psum[7][1]
                    gfirst, glast = (j == 3), last

                    def den():
                        S.op("pe", lambda e: e.matmul(pD[:, 0:TT], K.ones1[:], sm["s"][:, 0, :], start=gfirst, stop=glast), reads=[K.Bc, sm["Bs"]], writes=[BDa], inc=False)
                        S.op("pe", lambda e: e.matmul(pD[:, 512:512 + TT], K.ones1[:], sm["s"][:, 1, :], start=gfirst, stop=glast), reads=[K.Bc, sm["Bs"]], writes=[BDb])
                    if last:
                        den()
                    else:
                        pending.append(den)
                if last:
                    epilogue(t, n)

            def epilogue(t, n):
                q0 = t * TT
                pO = K.pd[2]
                BOa, BOb = K.psum[4][1], K.psum[5][1]
                pD = K.pd[3]
                BDa, BDb = K.psum[6][1], K.psum[7][1]
                S.op("dve", lambda e: e.tensor_copy(out=O1s[:], in_=pO[:, :].rearrange("p (c t) -> p c t", c=2)), reads=[BOa, BOb], writes=[BO1s])
                S.op("act", lambda e: e.activation(out=rr_[:], in_=pD[:, :].rearrange("p (c t) -> p c t", c=2), func=AF.Ln), reads=[BDa, BDb], writes=[Brr])
                S.op("act", lambda e: e.activation(out=rr_[:], in_=rr_[:], func=AF.Exp, scale=-1.0), reads=[Brr], writes=[Brr])
                S.op("dve", lambda e: e.tensor_tensor(out=O1s[:], in0=O1s[:], in1=rr_[:], op=ALU.mult), reads=[BO1s, Brr], writes=[BO1s])
                S.op("dve", lambda e: e.scalar_tensor_tensor(out=A[:], in0=O1s[:, 1, :], scalar=sc[:, 2:3], in1=O1s[:, 0, :], op0=ALU.mult, op1=ALU.add),
                     reads=[BO1s, Bsc], writes=[BA])
                S.op("dve", lambda e: e.tensor_tensor(out=sq[:], in0=A[:], in1=A[:], op=ALU.mult), reads=[BA], writes=[Bsq])
                S.op("pe", lambda e: e.matmul(pD[:, 0:TT], K.ones128[:], sq[:], start=True, stop=True), reads=[K.Bc, Bsq], writes=[BDa])
                S.op("act", lambda e: e.activation(out=rs[:], in_=pD[:, 0:TT], func=AF.Ln, bias=K.epsT[:], scale=1.0), reads=[BDa, K.Bc], writes=[Brs])
                S.op("act", lambda e: e.activation(out=rs[:], in_=rs[:], func=AF.Exp, scale=-0.5), reads=[Brs], writes=[Brs])
                o_, Bo_ = ost[t % 2]
                S.op("dve", lambda e: e.scalar_tensor_tensor(out=o_[:], in0=A[:], scalar=sc[:, 3:4], in1=rs[:], op0=ALU.mult, op1=ALU.mult),
                     reads=[BA, Bsc, Brs], writes=[Bo_])
                S.dma("sp", lambda e: e.dma_start(out=OT2[q0 // hf_, h, :, (q0 % hf_):(q0 % hf_) + TT], in_=o_[:]), Bo_, reads=[Bo_])

            stA(0)
            stB(0)
            for n in range(N):
                if n + 1 < N:
                    stA(n + 1)
                    stB(n + 1)
                stC(n)
        S.barrier()


def stage_mix_out(K, XT, XO, OG, wout, nt):
    S = K.S
    with ExitStack() as st:
        w_s = K.sb(st, [128, KC, D], BF16); Bw = S.buf()
        load_w(K, w_s, Bw, wout)
        xs = [(K.sb(st, [128, KC, TT], F32), S.buf()) for _ in range(2)]
        os_ = [(K.sb(st, [128, KC, TT], BF16), S.buf()) for _ in range(2)]
        ntile = nt // TT

        def load(i):
            x, Bx = xs[i % 2]
            o, Bo = os_[i % 2]
            S.dma("sp", lambda e: e.dma_start(out=x[:], in_=xt_tile(XT, i * TT, TT)), Bx, writes=[Bx])
            for r in range(2):
                S.dma("sp", lambda e: e.dma_start(out=o[:, r * NH:(r + 1) * NH, :], in_=xt_tile(OG[r], i * TT, TT)), Bo, writes=[Bo])
        load(0)
        for i in range(ntile):
            x, Bx = xs[i % 2]
            o, Bo = os_[i % 2]
            if i + 1 < ntile:
                load(i + 1)

            def ev(j, ps, Bps):
                S.op("dve", lambda e: e.tensor_tensor(out=x[:, j, :], in0=ps[:, :TT], in1=x[:, j, :], op=ALU.add), reads=[Bps, Bx], writes=[Bx])
            linear_fm(K, w_s, Bw, o, Bo, TT, range(KC), ev)
            S.dma("sp", lambda e: e.dma_start(out=xt_tile(XO, i * TT, TT), in_=x[:]), Bx, reads=[Bx])
        S.barrier()


def stage_hg_in(K, XNF, wq, wf, wi, wg, oml, SQ, KKd, V, SG, seq, nt_half, layer_idx=1):
    S = K.S
    HW = NH * 128
    with ExitStack() as st:
        ws = []
        for w in (wq, wf, wi, wg):
            t = K.sb(st, [128, KC, HW], BF16); B = S.buf()
            load_w(K, t, B, w)
            ws.append((t, B))
        (wq_s, Bwq), (wf_s, Bwf), (wi_s, Bwi), (wg_s, Bwg) = ws
        nl = oml.shape[2]
        lraw = K.sb(st, [128, NH, nl], F32); Blraw = S.buf()
        load_vec(K, lraw, Blraw, oml)
        S.op("act", lambda e: e.activation(out=lraw[:], in_=lraw[:], func=AF.Exp), reads=[Blraw], writes=[Blraw])
        tot = K.sb(st, [128, NH], F32); Btot = S.buf()
        num = K.sb(st, [128, NH], F32); Bnum = S.buf()
        om = K.sb(st, [128, NH], F32); Bom = S.buf()
        S.op("dve", lambda e: e.reduce_sum(out=tot[:], in_=lraw[:], axis=AX.X), reads=[Blraw], writes=[Btot])
        S.op("dve", lambda e: e.reduce_sum(out=num[:], in_=lraw[:, :, 1:layer_idx + 1], axis=AX.X), reads=[Blraw], writes=[Bnum])
        S.op("dve", lambda e: e.reciprocal(out=tot[:], in_=tot[:]), reads=[Btot], writes=[Btot])
        S.op("dve", lambda e: e.tensor_tensor(out=num[:], in0=num[:], in1=tot[:], op=ALU.mult), reads=[Bnum, Btot], writes=[Bnum])
        S.op("dve", lambda e: e.tensor_scalar(out=om[:], in0=num[:], scalar1=-1.0, scalar2=1.0, op0=ALU.mult, op1=ALU.add), reads=[Bnum], writes=[Bom])
        xns = [(K.sb(st, [128, KC, TT], BF16), S.buf()) for _ in range(2)]
        qst = [(K.sb(st, [128, NH, TT], BF16), S.buf()) for _ in range(2)]
        gst = [(K.sb(st, [128, NH, TT], BF16), S.buf()) for _ in range(2)]
        kst = [(K.sb(st, [128, NH, TT], F32), S.buf()) for _ in range(2)]
        vst = [(K.sb(st, [128, NH, TT // 128, 128], BF16), S.buf()) for _ in range(2)]
        ntile = seq // TT
        per = nt_half // TT

        def load(i):
            xn, Bxn = xns[i % 2]
            S.dma("sp", lambda e: e.dma_start(out=xn[:], in_=xt_tile(XNF[i // per], (i % per) * TT, TT)), Bxn, writes=[Bxn])
        load(0)
        for i in range(ntile):
            xn, Bxn = xns[i % 2]
            q_, Bq_ = qst[i % 2]; g_, Bg_ = gst[i % 2]; k_, Bk_ = kst[i % 2]; v_, Bv_ = vst[i % 2]
            if i + 1 < ntile:
                load(i + 1)

            def ev_q(j, ps, Bps):
                S.op("act", lambda e: e.activation(out=q_[:, j, :], in_=ps[:, :TT], func=AF.Silu), reads=[Bps], writes=[Bq_])
            linear_fm(K, wq_s, Bwq, xn, Bxn, TT, range(NH), ev_q)

            def ev_g(j, ps, Bps):
                S.op("act", lambda e: e.activation(out=g_[:, j, :], in_=ps[:, :TT], func=AF.Silu), reads=[Bps], writes=[Bg_])
            linear_fm(K, wg_s, Bwg, xn, Bxn, TT, range(NH), ev_g)

            def ev_f(j, ps, Bps):
                S.op("act", lambda e: e.activation(out=k_[:, j, :], in_=ps[:, :TT], func=AF.Sigmoid, scale=-1.0), reads=[Bps], writes=[Bk_])
                S.op("dve", lambda e: e.tensor_scalar(out=k_[:, j, :], in0=k_[:, j, :], scalar1=om[:, j:j + 1], scalar2=None, op0=ALU.mult),
                     reads=[Bk_, Bom], writes=[Bk_])
            linear_fm(K, wf_s, Bwf, xn, Bxn, TT, range(NH), ev_f)
            for sub in range(TT // 128):
                ps, Bps = K.ps()
                for k in range(KC):
                    S.op("pe", lambda e, k=k: e.matmul(ps[:, :HW], xn[:, k, sub * 128:(sub + 1) * 128], wi_s[:, k, :],
                                                       start=(k == 0), stop=(k == KC - 1)),
                         reads=[Bxn, Bwi], writes=[Bps], inc=(k == KC - 1))
                psv = ps[:, :HW].rearrange("p (h d) -> p h d", h=NH)
                S.op("dve", lambda e: e.tensor_copy(out=v_[:, :, sub, :], in_=psv), reads=[Bps], writes=[Bv_])
            s0 = i * TT
            j0 = s0 // 128
            S.dma("sp", lambda e: e.dma_start(out=SQ[:, :, s0:s0 + TT].rearrange("h p t -> p h t"), in_=q_[:]), Bq_, reads=[Bq_])
            S.dma("sp", lambda e: e.dma_start(out=SG[:, :, s0:s0 + TT].rearrange("h p t -> p h t"), in_=g_[:]), Bg_, reads=[Bg_])
            S.dma("sp", lambda e: e.dma_start(out=KKd[:, :, s0:s0 + TT].rearrange("h p t -> p h t"), in_=k_[:]), Bk_, reads=[Bk_])
            S.dma("sp", lambda e: e.dma_start(out=V[:, :, j0:j0 + TT // 128, :].rearrange("h p j d -> p h j d"), in_=v_[:]), Bv_, reads=[Bv_])
        S.barrier()


def stage_hg_core(K, SQ, KKd, V, SG, gon, OT2, seq, half=None):
    S = K.S
    C = 64
    NCB = TT // C
    with ExitStack() as st:
        go = K.sb(st, [128, 1], F32); Bgo = S.buf()
        load_vec(K, go, Bgo, gon)
        mle = K.sb(st, [64, 64], F32); Bmle = S.buf()
        S.op("dve", lambda e: e.tensor_copy(out=mle[:], in_=K.cf[0:64, 0:64]), reads=[K.Bcf], writes=[Bmle])
        hd = []
        for h in range(NH):
            d = dict(Sf=K.sb(st, [128, 128], F32), BSf=S.buf(), Sb=K.sb(st, [128, 128], BF16), BSb=S.buf())
            S.op("dve", lambda e: e.memset(d["Sf"][:], 0.0), writes=[d["BSf"]])
            S.op("dve", lambda e: e.memset(d["Sb"][:], 0.0), writes=[d["BSb"]])
            for nm, shp, dt in (("sq0", [128, TT], BF16), ("kk0", [128, TT], F32), ("sg0", [128, TT], BF16), ("v0", [64, NCB, 128], BF16),
                                ("sq1", [128, TT], BF16), ("kk1", [128, TT], F32), ("sg1", [128, TT], BF16), ("v1", [64, NCB, 128], BF16),
                                ("lf", [128, TT], F32), ("G", [128, TT], F32),
                                ("eG", [128, TT], F32), ("enG", [128, TT], F32), ("Qt", [128, TT], BF16),
                                ("Kt", [128, TT], BF16), ("Kh", [128, TT], BF16), ("KhT", [64, NCB, 128], BF16),
                                ("AT", [64, 2, 64], BF16), ("osq", [128, TT], BF16), ("rs", [128, TT], F32),
                                ("of", [128, TT], F32), ("ob", [128, TT], BF16)):
                d[nm] = K.sb(st, shp, dt)
                d["B" + nm] = S.buf()
            hd.append(d)
        nblk = seq // TT
        half = (seq // 2) if half is None else half
        rr = [0]

        def psr():
            t = K.ps_i(rr[0] % 4)
            rr[0] += 1
            return t
        def loadblk(blk):
            s0 = blk * TT
            sfx = str(blk % 2)
            for h in range(NH):
                d = hd[h]
                S.dma("sp", lambda e: e.dma_start(out=d["sq" + sfx][:], in_=SQ[h][:, s0:s0 + TT]), d["Bsq" + sfx], writes=[d["Bsq" + sfx]])
                S.dma("sp", lambda e: e.dma_start(out=d["kk" + sfx][:], in_=KKd[h][:, s0:s0 + TT]), d["Bkk" + sfx], writes=[d["Bkk" + sfx]])
                S.dma("sp", lambda e: e.dma_start(out=d["sg" + sfx][:], in_=SG[h][:, s0:s0 + TT]), d["Bsg" + sfx], writes=[d["Bsg" + sfx]])
                for hh in range(2):
                    S.dma("sp", lambda e: e.dma_start(out=d["v" + sfx][:, hh::2, :], in_=V[h][hh * 64:(hh + 1) * 64, s0 // 128: s0 // 128 + TT // 128, :]),
                          d["Bv" + sfx], writes=[d["Bv" + sfx]])
        loadblk(0)
        for blk in range(nblk):
            s0 = blk * TT
            sfx = str(blk % 2)
            for h in range(NH):
                d = hd[h]
                for nm in ("sq", "kk", "sg", "v"):
                    d[nm] = d[nm + sfx]
                    d["B" + nm] = d["B" + nm + sfx]
            if blk + 1 < nblk:
                loadblk(blk + 1)
            for h in range(NH):
                d = hd[h]
                S.op("act", lambda e: e.activation(out=d["lf"][:], in_=d["kk"][:], func=AF.Ln, scale=-1.0, bias=K.oneT[:]), reads=[d["Bkk"], K.Bc], writes=[d["Blf"]])
                S.op("dve", lambda e: e.tensor_tensor_scan(out=d["G"][:], data0=K.scanmask, data1=d["lf"][:], initial=0.0, op0=ALU.mult, op1=ALU.add),
                     reads=[d["Blf"], K.Bcf], writes=[d["BG"]])
                S.op("act", lambda e: e.activation(out=d["eG"][:], in_=d["G"][:], func=AF.Exp), reads=[d["BG"]], writes=[d["BeG"]])
                S.op("act", lambda e: e.activation(out=d["enG"][:], in_=d["G"][:], func=AF.Exp, scale=-1.0), reads=[d["BG"]], writes=[d["BenG"]])
                S.op("dve", lambda e: e.tensor_tensor(out=d["Qt"][:], in0=d["sq"][:], in1=d["eG"][:], op=ALU.mult), reads=[d["Bsq"], d["BeG"]], writes=[d["BQt"]])
                S.op("dve", lambda e: e.tensor_tensor(out=d["Kt"][:], in0=d["kk"][:], in1=d["enG"][:], op=ALU.mult), reads=[d["Bkk"], d["BenG"]], writes=[d["BKt"]])
                for c in range(NCB):
                    S.op("dve", lambda e: e.tensor_scalar(out=d["Kh"][:, c * C:(c + 1) * C], in0=d["Kt"][:, c * C:(c + 1) * C],
                                                          scalar1=d["eG"][:, (c + 1) * C - 1:(c + 1) * C], scalar2=None, op0=ALU.mult),
                         reads=[d["BKt"], d["BeG"]], writes=[d["BKh"]])
                for c in range(NCB):
                    idx = rr[0] % 4
                    ps, Bps = psr()
                    pst = K.pd[idx // 2].bitcast(BF16)
                    cb = (idx % 2) * 1024
                    S.op("pe", lambda e: e.transpose(pst[0:64, cb:cb + 128], d["Kh"][:, c * C:(c + 1) * C], K.ident[:]), reads=[d["BKh"], K.Bc], writes=[Bps])
                    S.op("act", lambda e: e.copy(out=d["KhT"][:, c, :], in_=pst[0:64, cb:cb + 128]), reads=[Bps], writes=[d["BKhT"]])
            for c in range(NCB):
                for h in range(NH):
                    d = hd[h]
                    po, Bpo = K.ps_i(4 + h)
                    cs = slice(c * C, (c + 1) * C)
                    ps, Bps = psr()
                    S.op("pe", lambda e: e.matmul(ps[0:64, 0:64], d["Kt"][:, cs], d["Qt"][:, cs], start=True, stop=True),
                         reads=[d["BKt"], d["BQt"]], writes=[Bps])
                    S.op("dve", lambda e: e.tensor_tensor(out=d["AT"][:, c % 2, :], in0=ps[0:64, 0:64], in1=mle[:], op=ALU.mult),
                         reads=[Bps, Bmle], writes=[d["BAT"]])
                    S.op("pe", lambda e: e.matmul(po[:, cs], d["Sb"][:], d["Qt"][:, cs], start=True, stop=False),
                         reads=[d["BSb"], d["BQt"]], writes=[Bpo], inc=False)
                    S.op("pe", lambda e: e.matmul(po[:, cs], d["v"][:, c, :], d["AT"][:, c % 2, :], start=False, stop=True),
                         reads=[d["Bv"], d["BAT"]], writes=[Bpo])
                    ps2, Bps2 = psr()
                    S.op("pe", lambda e: e.matmul(ps2[:, 0:128], d["KhT"][:, c, :], d["v"][:, c, :], start=True, stop=True),
                         reads=[d["BKhT"], d["Bv"]], writes=[Bps2])
                    S.op("dve", lambda e: e.scalar_tensor_tensor(out=d["Sf"][:], in0=d["Sf"][:], scalar=d["eG"][:, (c + 1) * C - 1:(c + 1) * C],
                                                                 in1=ps2[:, 0:128], op0=ALU.mult, op1=ALU.add),
                         reads=[d["BSf"], d["BeG"], Bps2], writes=[d["BSf"]])
                    S.op("act", lambda e: e.copy(out=d["Sb"][:], in_=d["Sf"][:]), reads=[d["BSf"]], writes=[d["BSb"]])
            for h in range(NH):
                d = hd[h]
                po, Bpo = K.ps_i(4 + h)
                S.op("act", lambda e: e.activation(out=d["osq"][:], in_=po[:, :TT], func=AF.Square), reads=[Bpo], writes=[d["Bosq"]])
                ps, Bps = psr()
                S.op("pe", lambda e: e.matmul(ps[:, :TT], K.ones128[:], d["osq"][:], start=True, stop=True), reads=[K.Bc, d["Bosq"]], writes=[Bps])
                S.op("act", lambda e: e.activation(out=d["rs"][:], in_=ps[:, :TT], func=AF.Sqrt, bias=K.epsT[:], scale=1.0), reads=[Bps, K.Bc], writes=[d["Brs"]])
                S.op("dve", lambda e: e.reciprocal(out=d["rs"][:], in_=d["rs"][:]), reads=[d["Brs"]], writes=[d["Brs"]])
                S.op("dve", lambda e: e.scalar_tensor_tensor(out=d["of"][:], in0=po[:, :TT], scalar=go[:, 0:1], in1=d["rs"][:], op0=ALU.mult, op1=ALU.mult),
                     reads=[Bpo, Bgo, d["Brs"]], writes=[d["Bof"]])
                S.op("dve", lambda e: e.tensor_tensor(out=d["ob"][:], in0=d["of"][:], in1=d["sg"][:], op=ALU.mult), reads=[d["Bof"], d["Bsg"]], writes=[d["Bob"]])
                S.dma("sp", lambda e: e.dma_start(out=OT2[s0 // half, h, :, (s0 % half):(s0 % half) + TT], in_=d["ob"][:]), d["Bob"], reads=[d["Bob"]])
        S.barrier()


def stage_sgu(K, XT, XO, gx, wu, wv, gvb, wsT, bs, wout, nt):
    S = K.S
    with ExitStack() as st:
        wu_s = K.sb(st, [128, KC, D], BF16); Bwu = S.buf()
        wv_s = K.sb(st, [128, KC, D], BF16); Bwv = S.buf()
        wo_s = K.sb(st, [128, KC, D], BF16); Bwo = S.buf()
        load_w(K, wu_s, Bwu, wu)
        load_w(K, wv_s, Bwv, wv)
        load_w(K, wo_s, Bwo, wout)
        g = K.sb(st, [128, KC], F32); Bg = S.buf()
        load_vec(K, g, Bg, gx)
        gv = K.sb(st, [128, D], F32); Bgv = S.buf()
        load_vec(K, gv, Bgv, gvb)
        wsf = K.sb(st, [128, 8, 128], F32); Bwsf = S.buf()
        load_vec(K, wsf, Bwsf, wsT)
        wsm = K.sb(st, [128, 8, 128], BF16); Bwsm = S.buf()
        for gi in range(8):
            S.op("dve", lambda e: e.tensor_tensor(out=wsm[:, gi, :], in0=wsf[:, gi, :], in1=K.cf[:, 0:128], op=ALU.mult),
                 reads=[Bwsf, K.Bcf], writes=[Bwsm])
        bsf = K.sb(st, [1, D], F32); Bbsf = S.buf()
        load_vec(K, bsf, Bbsf, bs)
        bsb = K.sb(st, [1, D], BF16); Bbsb = S.buf()
        S.op("dve", lambda e: e.tensor_copy(out=bsb[:], in_=bsf[:]), reads=[Bbsf], writes=[Bbsb])
        xs = [(K.sb(st, [128, KC, TT], F32), S.buf()) for _ in range(2)]
        sq = K.sb(st, [128, KC, TT], BF16); Bsq = S.buf()
        r = K.sb(st, [128, TT], F32); Br = S.buf()
        xn = K.sb(st, [128, KC, TT], BF16); Bxn = S.buf()
        uT = K.sb(st, [128, KC, TT], BF16); BuT = S.buf()
        zT = K.sb(st, [128, KC, TT], BF16); BzT = S.buf()
        vf = [(K.sb(st, [128, D], F32), S.buf()) for _ in range(2)]
        junk = K.sb(st, [128, D], BF16); Bjunk = S.buf()
        ssq = [(K.sb(st, [128, 2], F32), S.buf()) for _ in range(2)]
        vn = [(K.sb(st, [128, D], BF16), S.buf()) for _ in range(2)]
        ntile = nt // TT

        def load(i):
            x, Bx = xs[i % 2]
            S.dma("sp", lambda e: e.dma_start(out=x[:], in_=xt_tile(XT, i * TT, TT)), Bx, writes=[Bx])
        load(0)
        it = 0
        for i in range(ntile):
            x, Bx = xs[i % 2]
            if i + 1 < ntile:
                load(i + 1)
            rmsnorm_fm(K, x, Bx, g, Bg, sq, Bsq, r, Br, xn, Bxn, TT)

            def ev_u(j, ps, Bps):
                S.op("act", lambda e: e.activation(out=uT[:, j, :], in_=ps[:, :TT], func=AF.Gelu), reads=[Bps], writes=[BuT])
            linear_fm(K, wu_s, Bwu, xn, Bxn, TT, range(KC), ev_u)
            for sub in range(TT // 128):
                v_, Bv_ = vf[it % 2]; s2, Bs2 = ssq[it % 2]; n_, Bn_ = vn[it % 2]
                it += 1
                ts = slice(sub * 128, (sub + 1) * 128)
                for hf in range(2):
                    ps, Bps = K.ps()
                    for k in range(KC):
                        S.op("pe", lambda e, k=k: e.matmul(ps[:, :512], xn[:, k, ts], wv_s[:, k, hf * 512:(hf + 1) * 512],
                                                           start=(k == 0), stop=(k == KC - 1)), reads=[Bxn, Bwv], writes=[Bps], inc=(k == KC - 1))
                    S.op("act", lambda e: e.activation(out=v_[:, hf * 512:(hf + 1) * 512], in_=ps[:, :512], func=AF.Gelu), reads=[Bps], writes=[Bv_])
                S.op("act", lambda e: e.activation(out=junk[:], in_=v_[:], func=AF.Square, accum_out=s2[:, 0:1]), reads=[Bv_], writes=[Bjunk, Bs2])
                S.op("dve", lambda e: e.tensor_scalar(out=s2[:, 1:2], in0=s2[:, 0:1], scalar1=1.0 / D, scalar2=EPS, op0=ALU.mult, op1=ALU.add),
                     reads=[Bs2], writes=[Bs2])
                S.op("act", lambda e: e.activation(out=s2[:, 1:2], in_=s2[:, 1:2], func=AF.Sqrt), reads=[Bs2], writes=[Bs2])
                S.op("dve", lambda e: e.reciprocal(out=s2[:, 1:2], in_=s2[:, 1:2]), reads=[Bs2], writes=[Bs2])
                S.op("dve", lambda e: e.scalar_tensor_tensor(out=n_[:], in0=v_[:], scalar=s2[:, 1:2], in1=gv[:], op0=ALU.mult, op1=ALU.mult),
                     reads=[Bv_, Bs2, Bgv], writes=[Bn_])
                for g0 in (0, 4):
                    ps, Bps = K.ps()
                    for gg in range(4):
                        gi = g0 + gg
                        S.op("pe", lambda e: e.matmul(ps[:, gg * 128:(gg + 1) * 128], n_[:, gi * 128:(gi + 1) * 128], wsm[:, gi, :], start=True, stop=False),
                             reads=[Bn_, Bwsm], writes=[Bps], inc=False)
                        S.op("pe", lambda e: e.matmul(ps[:, gg * 128:(gg + 1) * 128], K.onesrow[0:1, :], bsb[0:1, gi * 128:(gi + 1) * 128], start=False, stop=True),
                             reads=[K.Bc, Bbsb], writes=[Bps], inc=(gg == 3))
                    S.op("dve", lambda e: e.tensor_tensor(out=zT[:, g0:g0 + 4, ts], in0=ps[:, :512].rearrange("p (g t) -> p g t", g=4),
                                                          in1=uT[:, g0:g0 + 4, ts], op=ALU.mult), reads=[Bps, BuT], writes=[BzT])

            def ev_o(j, ps, Bps):
                S.op("dve", lambda e: e.tensor_tensor(out=x[:, j, :], in0=ps[:, :TT], in1=x[:, j, :], op=ALU.add), reads=[Bps, Bx], writes=[Bx])
            linear_fm(K, wo_s, Bwo, zT, BzT, TT, range(KC), ev_o)
            S.dma("sp", lambda e: e.dma_start(out=xt_tile(XO, i * TT, TT), in_=x[:]), Bx, reads=[Bx])
        S.barrier()


def fm(x):
    T, F_ = x.shape
    return np.ascontiguousarray(x.T.reshape(F_ // 128, 128, T))


def unfm(a):
    c, p, T = a.shape
    return np.ascontiguousarray(a.reshape(c * p, T).T)


def vec8(g):
    return np.ascontiguousarray(np.asarray(g, np.float32).reshape(-1, 128).T)


def _bucket_table():
    kk_ = np.arange(128)[:, None]
    jj = np.arange(1024)[None, :]
    d = jj - 384 - kk_
    n = np.maximum(d, 0)
    ex = 16
    nf = np.maximum(n, ex).astype(np.float32)
    large = ex + (np.log(nf / ex) / math.log(128 / ex) * (32 - ex)).astype(np.int32)
    large = np.minimum(large, 31)
    bk = np.where(n < ex, n, large)
    cm = np.where(d < 0, -30000.0, 0.0).astype(np.float32)
    return bk, cm


class Prog:
    def __init__(self, ncores=8):
        self.nc = bass.Bass("TRN2", target_bir_lowering=False)
        self.ncores = ncores
        self.in_maps = [dict() for _ in range(ncores)]
        self.out_names = []

    def inp(self, name, arrs):
        if not isinstance(arrs, (list, tuple)):
            arrs = [arrs] * self.ncores
        a0 = arrs[0]
        dt = BF16 if a0.dtype == NPBF else F32
        ap = self.nc.dram_tensor(name, list(a0.shape), dt, kind="ExternalInput").ap()
        for c in range(self.ncores):
            self.in_maps[c][name] = np.ascontiguousarray(arrs[c])
        return ap

    def out(self, name, shape, dt):
        self.out_names.append(name)
        return self.nc.dram_tensor(name, list(shape), dt, kind="ExternalOutput").ap()

    def tmp(self, name, shape, dt):
        return self.nc.dram_tensor(name, list(shape), dt, kind="Internal").ap()

    def run(self):
        res = run_bass_kernel_spmd(self.nc, self.in_maps, core_ids=list(range(self.ncores)))
        return res.results


class Host:
    def __init__(self, inp):
        self.i = {k: np.asarray(v) for k, v in inp.items()}
        self.bk, self.cm = _bucket_table()

    def core(self, c):
        return c // 2, c % 2

    def xT(self):
        x = self.i["x"]
        return [fm(x[c // 2, (c % 2) * NT:(c % 2 + 1) * NT]) for c in range(8)]

    def memT(self):
        return [fm(self.i["mem"][c // 2]) for c in range(8)]

    def heads(self, c):
        r = c % 2
        return list(range(r * NH, (r + 1) * NH))

    def hcols(self, c):
        return np.concatenate([np.arange(h * 128, (h + 1) * 128) for h in self.heads(c)])

    def da(self, j):
        I = self.i
        w = I["da_w_in"][j]
        out = {}
        for nm, off in (("wq", 0), ("wk", D), ("wv", 2 * D)):
            out[nm] = [np.ascontiguousarray(w[:, off + self.hcols(c)]) for c in range(8)]
        out["GB"] = [np.ascontiguousarray(np.stack([I["rel_bias"][h][self.bk] for h in self.heads(c)]).astype(np.float32)) for c in range(8)]
        out["CM"] = self.cm
        miscs = []
        for c in range(8):
            m = np.zeros((128, 4 * 64 + NH + 1), np.float32)
            m[:, 0:64] = I["da_lq1"][j]; m[:, 64:128] = I["da_lk1"][j]; m[:, 128:192] = I["da_lq2"][j]; m[:, 192:256] = I["da_lk2"][j]
            for hi, h in enumerate(self.heads(c)):
                m[:, 256 + hi] = I["rel_bias"][h, 31]
            m[:, 256 + NH] = I["da_subln"][j]
            miscs.append(m)
        out["misc"] = miscs
        out["wout"] = I["da_w_out"][j]
        return out

    def hg(self, j):
        I = self.i
        w = I["hg_w_in"][j]
        out = {}
        for k, nm in enumerate(("wq", "wf", "wi", "wg")):
            out[nm] = [np.ascontiguousarray(w[:, k * D + self.hcols(c)]) for c in range(8)]
        lbr = I["hg_lower_bounds"]
        out["oml"] = [np.ascontiguousarray(lbr[:, self.hcols(c)].reshape(lbr.shape[0], NH, 128).transpose(2, 1, 0)) for c in range(8)]
        out["gon"] = np.ascontiguousarray(I["hg_onorm"][j].reshape(128, 1))
        out["wout"] = I["hg_w_out"][j]
        return out

    def sg(self, j):
        I = self.i
        w = I["sg_w_in"][j]
        return dict(wu=np.ascontiguousarray(w[:, :D]), wv=np.ascontiguousarray(w[:, D:]),
                    gvb=np.ascontiguousarray(np.broadcast_to(I["sg_vnorm"][j], (128, D))),
                    wsT=np.ascontiguousarray(I["sg_w_s"][j].transpose(2, 0, 1)),
                    bs=np.ascontiguousarray(I["sg_b_s"][j].reshape(1, D)), wout=I["sg_w_out"][j])


def lam_init_of(layer_idx):
    return 0.8 - 0.6 * math.exp(-0.3 * layer_idx)


def emit_tail(P, K, H, i, XT_in, OG, wout, pre, next_norm_g, final, tagp):
    I = H.i
    mk = P.tmp
    cur = XT_in
    if OG is not None:
        X1 = mk(tagp + "X1", [KC, 128, NT], F32)
        stage_mix_out(K, cur, X1, OG, P.inp(tagp + "wout", wout), NT)
        cur = X1
    X2 = mk(tagp + "X2", [KC, 128, NT], F32)
    stage_cross(K, cur, X2, pre["memT"], pre["gmem"], P.inp(tagp + "gcx", vec8(I["norm_cross"][i])),
                P.inp(tagp + "cwq", I["ca_w_q"][i]), P.inp(tagp + "cwkv", I["ca_w_kv"][i]), P.inp(tagp + "cwo", I["ca_w_o"][i]), NT)
    X3 = mk(tagp + "X3", [KC, 128, NT], F32)
    XNs = mk(tagp + "XNs", [KC, 128, NT], BF16)
    gf = P.inp(tagp + "gfx", vec8(I["norm_ffn"][i]))
    wgu = P.inp(tagp + "wgu", I["ffn_w_gu"][i])
    wdn = P.inp(tagp + "wdn", I["ffn_w_down"][i])
    stage_ffn(K, X2, X3, XNs, gf, wgu, wdn, 0, NT)
    return X3, XNs, gf, wgu, wdn


def kernel_multi(**inp):
    H = Host(inp)
    I = H.i
    consts = make_consts()
    xT = H.xT()
    memT = H.memT()
    gmem = vec8(I["norm_mem"])
    depth = I["norm_mix"].shape[0]

    def newprog():
        P = Prog()
        st = ExitStack()
        K = Ctx(P.nc, st, P.inp("consts", consts))
        return P, K, st

    def pair_gather(xn):
        return [np.stack([xn[2 * (c // 2)], xn[2 * (c // 2) + 1]]) for c in range(8)]

    def pair_a2a(ot2):
        return [np.stack([ot2[2 * (c // 2) + rp][c % 2] for rp in range(2)]) for c in range(8)]

    P, K, st = newprog()
    with st:
        XT = P.inp("XT", xT)
        XN = P.out("XN", [KC, 128, NT], BF16)
        stage_norm(K, XT, P.inp("g", vec8(I["norm_mix"][0])), XN, NT)
    res = P.run()
    xn = [r["XN"] for r in res]
    OG = None
    wout = None
    for i in range(depth):
        kind, j = i % 3, i // 3
        if kind == 0:
            d = H.da(j)
            P, K, st = newprog()
            with st:
                XNF = P.inp("XNF", pair_gather(xn))
                QT = P.tmp("QT", [NH, 128, SEQ], BF16); KT = P.tmp("KT", [NH, 128, SEQ], BF16)
                V = P.tmp("V", [NH, 128, SEQ // 128, 128], BF16)
                OT2 = P.out("OT2", [2, NH, 128, NT], BF16)
                stage_da_in(K, XNF, P.inp("wq", d["wq"]), P.inp("wk", d["wk"]), P.inp("wv", d["wv"]), QT, KT, V, SEQ, NT)
                stage_da_core(K, QT, KT, V, P.inp("GB", d["GB"]), P.inp("CM", d["CM"]), P.inp("misc", d["misc"]), OT2, SEQ, lam_init_of(i))
            res = P.run()
            OG = pair_a2a([r["OT2"] for r in res]); wout = d["wout"]
        elif kind == 1:
            d = H.hg(j)
            P, K, st = newprog()
            with st:
                XNF = P.inp("XNF", pair_gather(xn))
                SQ = P.tmp("SQ", [NH, 128, SEQ], BF16); KKd = P.tmp("KK", [NH, 128, SEQ], F32)
                V = P.tmp("V", [NH, 128, SEQ // 128, 128], BF16); SG = P.tmp("SG", [NH, 128, SEQ], BF16)
                OT2 = P.out("OT2", [2, NH, 128, NT], BF16)
                stage_hg_in(K, XNF, P.inp("wq", d["wq"]), P.inp("wf", d["wf"]), P.inp("wi", d["wi"]), P.inp("wg", d["wg"]),
                            P.inp("oml", d["oml"]), SQ, KKd, V, SG, SEQ, NT, layer_idx=i)
                stage_hg_core(K, SQ, KKd, V, SG, P.inp("gon", d["gon"]), OT2, SEQ)
            res = P.run()
            OG = pair_a2a([r["OT2"] for r in res]); wout = d["wout"]
        P, K, st = newprog()
        with st:
            XT = P.inp("XT", xT)
            pre = dict(memT=P.inp("memT", memT), gmem=P.inp("gmem", gmem))
            cur = XT
            if kind == 2:
                d = H.sg(j)
                Xs = P.tmp("Xs", [KC, 128, NT], F32)
                stage_sgu(K, cur, Xs, P.inp("gmx", vec8(I["norm_mix"][i])), P.inp("wu", d["wu"]), P.inp("wv", d["wv"]), P.inp("gvb", d["gvb"]),
                          P.inp("wsT", d["wsT"]), P.inp("bs", d["bs"]), P.inp("swout", d["wout"]), NT)
                cur = Xs
                ogap = None
            else:
                ogap = P.inp("OG", OG)
            X3, XNs, gf, wgu, wdn = emit_tail(P, K, H, i, cur, ogap, wout, pre, None, False, "t_")
            last = (i == depth - 1)
            XO = P.out("XO", [KC, 128, NT], F32) if not last else P.tmp("XO", [KC, 128, NT], F32)
            stage_ffn(K, X3, XO, XNs, gf, wgu, wdn, 1, NT)
            if last:
                OUT = P.out("OUT", [KC, 128, NT], F32)
                stage_norm(K, XO, P.inp("gfin", vec8(I["norm_final"])), OUT, NT, out_f32=True)
            elif (i + 1) % 3 != 2:
                XN = P.out("XN", [KC, 128, NT], BF16)
                stage_norm(K, XO, P.inp("gnx", vec8(I["norm_mix"][i + 1])), XN, NT)
        res = P.run()
        if last:
            outT = [r["OUT"] for r in res]
        else:
            xT = [r["XO"] for r in res]
            if (i + 1) % 3 != 2:
                xn = [r["XN"] for r in res]
    B = I["x"].shape[0]
    out = np.empty((B, SEQ, D), np.float32)
    for c in range(8):
        out[c // 2, (c % 2) * NT:(c % 2 + 1) * NT] = unfm(outT[c])
    return out


def kernel_fused(**inp):
    H = Host(inp)
    I = H.i
    depth = I["norm_mix"].shape[0]
    B = I["x"].shape[0]
    S_ = SEQ
    P = Prog()
    nc = P.nc
    bmap = [c % B for c in range(8)]
    with ExitStack() as st:
        K = Ctx(nc, st, P.inp("consts", make_consts()))
        XT = P.inp("XT", [fm(I["x"][b]) for b in bmap])
        memT = P.inp("memT", [fm(I["mem"][b]) for b in bmap])
        gmem = P.inp("gmem", vec8(I["norm_mem"]))
        XA = P.tmp("XA", [KC, 128, S_], F32)
        XB = P.tmp("XB", [KC, 128, S_], F32)
        XN = P.tmp("XN", [1, KC, 128, S_], BF16)
        XNs = P.tmp("XNs", [KC, 128, S_], BF16)
        OT = [P.tmp("OT%d" % g, [1, NH, 128, S_], BF16) for g in range(2)]
        QT = P.tmp("QT", [NH, 128, S_], BF16)
        KT = P.tmp("KT", [NH, 128, S_], BF16)
        V = P.tmp("V", [NH, 128, S_ // 128, 128], BF16)
        KKd = P.tmp("KK", [NH, 128, S_], F32)
        SGd = P.tmp("SG", [NH, 128, S_], BF16)
        OUT = P.out("OUT", [KC, 128, S_], F32)
        cur = XT
        bufs = [XA, XB]
        nb = [0]

        def nxt():
            t = bufs[nb[0] % 2]
            nb[0] += 1
            return t
        for i in range(depth):
            kind, j = i % 3, i // 3
            tg = "L%d_" % i
            if kind in (0, 1):
                nrm0 = (cur, P.inp(tg + "gmix", vec8(I["norm_mix"][i]))) if i == 0 else None
                d = H.da(j) if kind == 0 else H.hg(j)
                if kind == 0:
                    CM = P.inp(tg + "CM", d["CM"])
                else:
                    gon = P.inp(tg + "gon", d["gon"])
                for g in range(2):
                    tgg = tg + "g%d_" % g
                    if kind == 0:
                        stage_da_in(K, XN, P.inp(tgg + "wq", d["wq"][g]), P.inp(tgg + "wk", d["wk"][g]), P.inp(tgg + "wv", d["wv"][g]),
                                    QT, KT, V, S_, S_, norm=(nrm0 if g == 0 else None))
                        stage_da_core(K, QT, KT, V, P.inp(tgg + "GB", d["GB"][g]), CM, P.inp(tgg + "misc", d["misc"][g]), OT[g], S_,
                                      lam_init_of(i), half=S_)
                    else:
                        stage_hg_in(K, XN, P.inp(tgg + "wq", d["wq"][g]), P.inp(tgg + "wf", d["wf"][g]), P.inp(tgg + "wi", d["wi"][g]),
                                    P.inp(tgg + "wg", d["wg"][g]), P.inp(tgg + "oml", d["oml"][g]), QT, KKd, V, SGd, S_, S_, layer_idx=i)
                        stage_hg_core(K, QT, KKd, V, SGd, gon, OT[g], S_, half=S_)
                o = nxt()
                stage_mix_out(K, cur, o, [OT[0][0], OT[1][0]], P.inp(tg + "wout", d["wout"]), S_)
                cur = o
            else:
                d = H.sg(j)
                o = nxt()
                stage_sgu(K, cur, o, P.inp(tg + "gmix", vec8(I["norm_mix"][i])), P.inp(tg + "wu", d["wu"]), P.inp(tg + "wv", d["wv"]),
                          P.inp(tg + "gvb", d["gvb"]), P.inp(tg + "wsT", d["wsT"]), P.inp(tg + "bs", d["bs"]), P.inp(tg + "wout", d["wout"]), S_)
                cur = o
            o = nxt()
            stage_cross(K, cur, o, memT, gmem, P.inp(tg + "gcx", vec8(I["norm_cross"][i])), P.inp(tg + "cwq", I["ca_w_q"][i]),
                        P.inp(tg + "cwkv", I["ca_w_kv"][i]), P.inp(tg + "cwo", I["ca_w_o"][i]), S_)
            cur = o
            gf = P.inp(tg + "gfx", vec8(I["norm_ffn"][i]))
            wgu = P.inp(tg + "wgu", I["ffn_w_gu"][i])
            wdn = P.inp(tg + "wdn", I["ffn_w_down"][i])
            for hf in range(2):
                o = nxt()
                nx = None
                if hf == 1:
                    if i == depth - 1:
                        nx = (P.inp("gfin", vec8(I["norm_final"])), OUT, True)
                    elif (i + 1) % 3 != 2:
                        nx = (P.inp(tg + "gnext", vec8(I["norm_mix"][i + 1])), XN[0], False)
                stage_ffn(K, cur, o, XNs, gf, wgu, wdn, hf, S_, nxt=nx)
                cur = o
        K.S.barrier()
        print("fused program: %d instructions, %d dma sems" % (K.S.ninst, K.S.nsem))
    res = P.run()
    out = np.empty((B, SEQ, D), np.float32)
    for b in range(B):
        out[b] = unfm(res[b]["OUT"])
    return out


def kernel(**inputs):
    return kernel_multi(**inputs)
```
